# Optimizing a Trainium2 kernel written in Bass

```python
import math
import jax, jax.numpy as jnp
from jax import lax
import numpy as np

D_MODEL = 2048
BATCH = 4
SEQ = 4096
DEPTH = 1

N_HEADS_A = 16
HEAD_DIM_A = 128
Q_RANK = 512
KV_RANK = 256
N_HEADS_IDX = 16
HEAD_DIM_IDX = 64
TOP_K_MAX = 256
Q_BLOCK = 128
N_BUCKETS = 32
MAX_DISTANCE = 128
SSM_EXPAND = 2
D_INNER = SSM_EXPAND * D_MODEL
SSM_HEAD_DIM = 64
N_HEADS_B = D_INNER // SSM_HEAD_DIM
SSM_GROUPS = 8
D_STATE = 128
CONV_WIDTH = 4
CHUNK = 128
D_XBC = D_INNER + 2 * SSM_GROUPS * D_STATE
D_FF = ((8 * D_MODEL // 3 + 255) // 256) * 256
IN_SPLITS = (Q_RANK, KV_RANK, HEAD_DIM_IDX, N_HEADS_IDX, D_INNER, D_XBC, N_HEADS_B, D_MODEL, D_MODEL)
D_IN_PROJ = sum(IN_SPLITS)
EPS = 1e-6

kernel_name = "hybrid_dsa_ssd_gated_block"


def rms_norm(x, g):
    xf = x.astype(jnp.float32)
    y = xf * lax.rsqrt(jnp.mean(xf * xf, axis=-1, keepdims=True) + EPS)
    return (y * g.astype(jnp.float32)).astype(x.dtype)


def layer_norm(x, g, b):
    xf = x.astype(jnp.float32)
    mu = jnp.mean(xf, axis=-1, keepdims=True)
    xc = xf - mu
    var = jnp.mean(xc * xc, axis=-1, keepdims=True)
    y = xc * lax.rsqrt(var + EPS) * g.astype(jnp.float32) + b.astype(jnp.float32)
    return y.astype(x.dtype)


def t5_bucket(dist):
    n = jnp.maximum(dist, 0)
    max_exact = N_BUCKETS // 2
    nf = jnp.maximum(n, 1).astype(jnp.float32)
    large = max_exact + (jnp.log(nf / max_exact) / math.log(MAX_DISTANCE / max_exact)
                         * (N_BUCKETS - max_exact)).astype(jnp.int32)
    large = jnp.minimum(large, N_BUCKETS - 1)
    return jnp.where(n < max_exact, n, large)


def dsa_branch(c_q, c_kv, k_idx, w_idx, w_uq, w_iq, w_uk, w_uv, rel_bias):
    B, L, _ = c_q.shape
    top_k = min(TOP_K_MAX, L // 4)
    n_blk = L // Q_BLOCK
    q = (c_q @ w_uq).reshape(B, L, N_HEADS_A, HEAD_DIM_A)
    q_idx = (c_q @ w_iq).reshape(B, L, N_HEADS_IDX, HEAD_DIM_IDX)
    q_lat = jnp.einsum('blhd,rhd->blhr', q, w_uk.reshape(KV_RANK, N_HEADS_A, HEAD_DIM_A)) * (HEAD_DIM_A ** -0.5)
    w_idx = w_idx * (N_HEADS_IDX ** -0.5 * HEAD_DIM_IDX ** -0.5)
    s_pos = jnp.arange(L, dtype=jnp.int32)

    def block(i):
        t0 = i * Q_BLOCK
        ql = lax.dynamic_slice_in_dim(q_lat, t0, Q_BLOCK, axis=1)
        qi = lax.dynamic_slice_in_dim(q_idx, t0, Q_BLOCK, axis=1)
        wi = lax.dynamic_slice_in_dim(w_idx, t0, Q_BLOCK, axis=1)
        t_pos = t0 + jnp.arange(Q_BLOCK, dtype=jnp.int32)
        idx_score = jnp.einsum('bqh,bqhs->bqs', wi,
                               jax.nn.relu(jnp.einsum('bqhd,bsd->bqhs', qi, k_idx)))
        causal = s_pos[None, :] <= t_pos[:, None]
        idx_score = jnp.where(causal[None], idx_score.astype(jnp.float32), -jnp.inf)
        _, sel = lax.top_k(idx_score, top_k)
        kv_sel = jax.vmap(lambda ckv, ix: ckv[ix])(c_kv, sel)
        logits = jnp.einsum('bqhr,bqkr->bqhk', ql, kv_sel).astype(jnp.float32)
        bias = rel_bias[t5_bucket(t_pos[None, :, None] - sel)]
        logits = logits + jnp.transpose(bias, (0, 1, 3, 2)).astype(jnp.float32)
        valid = (sel <= t_pos[None, :, None])[:, :, None, :]
        logits = jnp.where(valid, logits, -jnp.inf)
        p = jax.nn.softmax(logits, axis=-1).astype(kv_sel.dtype)
        return jnp.einsum('bqhk,bqkr->bqhr', p, kv_sel)

    o_lat = lax.map(block, jnp.arange(n_blk, dtype=jnp.int32))
    o_lat = jnp.transpose(o_lat, (1, 0, 2, 3, 4)).reshape(B, L, N_HEADS_A, KV_RANK)
    o = jnp.einsum('blhr,rhd->blhd', o_lat, w_uv.reshape(KV_RANK, N_HEADS_A, HEAD_DIM_A))
    return o.reshape(B, L, N_HEADS_A * HEAD_DIM_A)


def ssd_chunked(x, dt, a, b_in, c_in):
    Bsz, L, H, P = x.shape
    nc = L // CHUNK
    HG = H // SSM_GROUPS

    def to_chunks(t):
        return jnp.moveaxis(t.reshape(Bsz, nc, CHUNK, *t.shape[2:]), 1, 0)

    xc = to_chunks(x.astype(jnp.float32).reshape(Bsz, L, SSM_GROUPS, HG, P))
    dtc = to_chunks(dt.reshape(Bsz, L, SSM_GROUPS, HG))
    bc = to_chunks(b_in.astype(jnp.float32))
    cc = to_chunks(c_in.astype(jnp.float32))
    a_g = a.reshape(SSM_GROUPS, HG)
    tri = jnp.tril(jnp.ones((CHUNK, CHUNK), dtype=bool))[None, :, :, None, None]

    def step(state, inp):
        xq, dq, bq, cq = inp
        a_cs = jnp.cumsum(dq * a_g, axis=1)
        seg = a_cs[:, :, None] - a_cs[:, None, :]
        decay = jnp.exp(jnp.where(tri, seg, -jnp.inf))
        cb = jnp.einsum('btgn,bsgn->btsg', cq, bq)
        w = cb[..., None] * decay * dq[:, None]
        y_intra = jnp.einsum('btsgh,bsghp->btghp', w, xq)
        y_inter = jnp.einsum('btgn,bghpn->btghp', cq, state) * jnp.exp(a_cs)[..., None]
        decay_end = jnp.exp(a_cs[:, -1:] - a_cs) * dq
        new_state = (state * jnp.exp(a_cs[:, -1])[..., None, None]
                     + jnp.einsum('bsgh,bsgn,bsghp->bghpn', decay_end, bq, xq))
        return new_state, y_intra + y_inter

    state0 = jnp.zeros((Bsz, SSM_GROUPS, HG, P, D_STATE), jnp.float32)
    _, yc = lax.scan(step, state0, (xc, dtc, bc, cc))
    return jnp.moveaxis(yc, 0, 1).reshape(Bsz, L, H, P)


def ssd_branch(z, xbc, dt_raw, conv_w, conv_b, dt_bias, a_log, d_skip, ssm_norm_g):
    B, L, _ = xbc.shape
    conv = lax.conv_general_dilated(xbc, conv_w[:, None, :], window_strides=(1,),
                                    padding=[(CONV_WIDTH - 1, 0)],
                                    dimension_numbers=('NWC', 'WIO', 'NWC'),
                                    feature_group_count=D_XBC)
    conv = jax.nn.silu(conv + conv_b)
    xs, b_in, c_in = jnp.split(conv, [D_INNER, D_INNER + SSM_GROUPS * D_STATE], axis=-1)
    x_h = xs.reshape(B, L, N_HEADS_B, SSM_HEAD_DIM)
    dt = jax.nn.softplus(dt_raw.astype(jnp.float32) + dt_bias.astype(jnp.float32))
    a = -jnp.exp(a_log.astype(jnp.float32))
    y = ssd_chunked(x_h, dt, a,
                    b_in.reshape(B, L, SSM_GROUPS, D_STATE),
                    c_in.reshape(B, L, SSM_GROUPS, D_STATE))
    y = (y + d_skip.astype(jnp.float32)[:, None] * x_h.astype(jnp.float32)).astype(xbc.dtype)
    y = y.reshape(B, L, D_INNER) * jax.nn.silu(z)
    y = rms_norm(y.reshape(B, L, SSM_GROUPS, D_INNER // SSM_GROUPS),
                 ssm_norm_g.reshape(SSM_GROUPS, D_INNER // SSM_GROUPS))
    return y.reshape(B, L, D_INNER)


def hybrid_mixer(h, w_in, cq_norm_g, ckv_norm_g, kidx_norm_g, kidx_norm_b, w_uq, w_iq, w_uk, w_uv,
                 rel_bias, conv_w, conv_b, dt_bias, a_log, d_skip, ssm_norm_g, w_proj_a, w_proj_b, w_out):
    proj = h @ w_in
    split_points = [int(v) for v in np.cumsum(IN_SPLITS)[:-1]]
    c_q, c_kv, k_idx, w_idx, z, xbc, dt_raw, g_a, g_b = jnp.split(proj, split_points, axis=-1)
    c_q = rms_norm(c_q, cq_norm_g)
    c_kv = rms_norm(c_kv, ckv_norm_g)
    k_idx = layer_norm(k_idx, kidx_norm_g, kidx_norm_b)
    y_a = dsa_branch(c_q, c_kv, k_idx, w_idx, w_uq, w_iq, w_uk, w_uv, rel_bias)
    y_b = ssd_branch(z, xbc, dt_raw, conv_w, conv_b, dt_bias, a_log, d_skip, ssm_norm_g)
    merged = jax.nn.sigmoid(g_a) * (y_a @ w_proj_a) + jax.nn.sigmoid(g_b) * (y_b @ w_proj_b)
    return merged @ w_out


def swiglu(h, w_gate, w_up, w_down):
    return (jax.nn.silu(h @ w_gate) * (h @ w_up)) @ w_down


def setup_inputs(seed: int = 0) -> dict:
    key = jax.random.key(seed)
    ks = jax.random.split(key, 32)
    f32 = jnp.float32

    def nrm(k, shape, scale):
        return jax.random.normal(k, shape, f32) * scale

    def gain(k, shape):
        return 1.0 + 0.05 * jax.random.normal(k, shape, f32)

    NL = DEPTH
    HA = N_HEADS_A * HEAD_DIM_A
    dt0 = jnp.exp(jax.random.uniform(ks[17], (NL, N_HEADS_B), f32, math.log(1e-3), math.log(1e-1)))
    return {
        "x": nrm(ks[0], (BATCH, SEQ, D_MODEL), 1.0),
        "c": nrm(ks[1], (BATCH, D_MODEL), 1.0),
        "w_ada": nrm(ks[2], (NL, D_MODEL, 6 * D_MODEL), 0.2 * D_MODEL ** -0.5),
        "b_ada": nrm(ks[3], (NL, 6 * D_MODEL), 0.02),
        "norm1_g": gain(ks[4], (NL, D_MODEL)),
        "w_in": nrm(ks[5], (NL, D_MODEL, D_IN_PROJ), D_MODEL ** -0.5),
        "cq_norm_g": gain(ks[6], (NL, Q_RANK)),
        "ckv_norm_g": gain(ks[7], (NL, KV_RANK)),
        "kidx_norm_g": gain(ks[8], (NL, HEAD_DIM_IDX)),
        "kidx_norm_b": nrm(ks[9], (NL, HEAD_DIM_IDX), 0.02),
        "w_uq": nrm(ks[10], (NL, Q_RANK, HA), Q_RANK ** -0.5),
        "w_iq": nrm(ks[11], (NL, Q_RANK, N_HEADS_IDX * HEAD_DIM_IDX), Q_RANK ** -0.5),
        "w_uk": nrm(ks[12], (NL, KV_RANK, HA), KV_RANK ** -0.5),
        "w_uv": nrm(ks[13], (NL, KV_RANK, HA), KV_RANK ** -0.5),
        "rel_bias": nrm(ks[14], (N_BUCKETS, N_HEADS_A), 0.5),
        "conv_w": nrm(ks[15], (NL, CONV_WIDTH, D_XBC), CONV_WIDTH ** -0.5),
        "conv_b": nrm(ks[16], (NL, D_XBC), 0.02),
        "dt_bias": dt0 + jnp.log(-jnp.expm1(-dt0)),
        "a_log": jnp.log(jax.random.uniform(ks[18], (NL, N_HEADS_B), f32, 1.0, 16.0)),
        "d_skip": gain(ks[19], (NL, N_HEADS_B)),
        "ssm_norm_g": gain(ks[20], (NL, D_INNER)),
        "w_proj_a": nrm(ks[21], (NL, HA, D_MODEL), HA ** -0.5),
        "w_proj_b": nrm(ks[22], (NL, D_INNER, D_MODEL), D_INNER ** -0.5),
        "w_out": nrm(ks[23], (NL, D_MODEL, D_MODEL), D_MODEL ** -0.5),
        "norm2_g": gain(ks[24], (NL, D_MODEL)),
        "w_gate": nrm(ks[25], (NL, D_MODEL, D_FF), D_MODEL ** -0.5),
        "w_up": nrm(ks[26], (NL, D_MODEL, D_FF), D_MODEL ** -0.5),
        "w_down": nrm(ks[27], (NL, D_FF, D_MODEL), D_FF ** -0.5),
        "final_g": gain(ks[28], (D_MODEL,)),
    }


def reference(x, c, w_ada, b_ada, norm1_g, w_in, cq_norm_g, ckv_norm_g, kidx_norm_g, kidx_norm_b,
              w_uq, w_iq, w_uk, w_uv, rel_bias, conv_w, conv_b, dt_bias, a_log, d_skip, ssm_norm_g,
              w_proj_a, w_proj_b, w_out, norm2_g, w_gate, w_up, w_down, final_g):
    c_act = jax.nn.silu(c)
    for l in range(DEPTH):
        mod = (c_act @ w_ada[l] + b_ada[l])[:, None, :]
        sh1, sc1, g1, sh2, sc2, g2 = jnp.split(mod, 6, axis=-1)
        h = rms_norm(x, norm1_g[l]) * (1.0 + sc1) + sh1
        x = x + g1 * hybrid_mixer(h, w_in[l], cq_norm_g[l], ckv_norm_g[l], kidx_norm_g[l], kidx_norm_b[l],
                                  w_uq[l], w_iq[l], w_uk[l], w_uv[l], rel_bias, conv_w[l], conv_b[l],
                                  dt_bias[l], a_log[l], d_skip[l], ssm_norm_g[l],
                                  w_proj_a[l], w_proj_b[l], w_out[l])
        h = rms_norm(x, norm2_g[l]) * (1.0 + sc2) + sh2
        x = x + g2 * swiglu(h, w_gate[l], w_up[l], w_down[l])
    return rms_norm(x, final_g)
```

```python
import numpy as np
from contextlib import ExitStack
import concourse.bass as bass
import concourse.mybir as mybir
from concourse.bass_utils import run_bass_kernel_spmd

F32 = mybir.dt.float32
BF16 = mybir.dt.bfloat16
AF = mybir.ActivationFunctionType
ALU = mybir.AluOpType
AX = mybir.AxisListType

STREAMS = ("pe", "act", "dve", "pool", "sp")
N_DMA_SEMS = 10
EPS = 1e-6

OFF_CQ, OFF_CKV, OFF_KIDX, OFF_WIDX, OFF_Z, OFF_X, OFF_B, OFF_C, OFF_DT, OFF_GA, OFF_GB = (
    0, 512, 768, 832, 848, 4944, 9040, 10064, 11088, 11152, 13200)
NEG = -30000.0
NBIS = 17


class Buf:
    __slots__ = ("w", "r", "excl")

    def __init__(self, excl=False):
        self.w = None
        self.r = {}
        self.excl = excl


class Prog:
    def __init__(self, nc, stack):
        self.nc = nc
        self.ops = {s: [] for s in STREAMS}
        self.cnt = {s: 0 for s in STREAMS}
        self.waited = {s: {} for s in STREAMS}
        self.sems = {}
        for s in STREAMS:
            self.sems[("eng", s)] = stack.enter_context(nc.semaphore("s_" + s))
        self.dma_i = {}
        self.dma_target = {}
        for s in ("sp", "pool", "act"):
            self.dma_i[s] = 0
            for k in range(N_DMA_SEMS):
                key = ("dma", s, k)
                self.sems[key] = stack.enter_context(nc.semaphore("d_%s%d" % (s, k)))
                self.dma_target[key] = 0

    def _collect(self, stream, reads, writes):
        deps = []
        pe = stream == "pe"
        for b in reads:
            if b.w is not None and not (pe and b.w[2] == "pe"):
                deps.append(b.w)
            if b.excl:
                for tok in b.r.values():
                    if tok[2] != stream:
                        deps.append(tok)
        for b in writes:
            if b.w is not None and not (pe and b.w[2] == "pe"):
                deps.append(b.w)
            for tok in b.r.values():
                if not (pe and tok[2] == "pe"):
                    deps.append(tok)
        return deps

    def _waits(self, stream, deps):
        best = {}
        for (key, val, _p) in deps:
            if val > best.get(key, 0):
                best[key] = val
        out = []
        wd = self.waited[stream]
        for key, val in best.items():
            if wd.get(key, 0) >= val:
                continue
            wd[key] = val
            out.append((key, val))
        return out

    def _update(self, tok, reads, writes):
        for b in reads:
            b.r[tok[2]] = tok
        for b in writes:
            b.w = tok
            b.r = {}

    def op(self, stream, fn, reads=(), writes=()):
        deps = self._collect(stream, reads, writes)
        waits = self._waits(stream, deps)
        self.cnt[stream] += 1
        key = ("eng", stream)
        tok = (key, self.cnt[stream], stream)
        self.ops[stream].append((waits, fn, key, 1))
        self._update(tok, reads, writes)
        return tok

    def dma(self, stream, fn, reads=(), writes=()):
        i = self.dma_i[stream]
        self.dma_i[stream] = i + 1
        key = ("dma", stream, i % N_DMA_SEMS)
        deps = self._collect("dma:" + stream, reads, writes)
        prev = self.dma_target[key]
        if prev > 0:
            deps.append((key, prev, "x"))
        waits = self._waits(stream, deps)
        self.dma_target[key] = prev + 16
        tok = (key, prev + 16, "dma:%s%d" % (stream, i % N_DMA_SEMS))
        self.ops[stream].append((waits, fn, key, 16))
        self._update(tok, reads, writes)
        return tok

    def wait_all(self, stream, bufs):
        deps = [b.w for b in bufs if b.w is not None]
        waits = self._waits(stream, deps)
        if waits:
            self.ops[stream].append((waits, None, None, 0))

    def barrier(self):
        deps = [(("eng", s), self.cnt[s], s) for s in STREAMS if self.cnt[s] > 0]
        deps += [(k, v, "x") for k, v in self.dma_target.items() if v > 0]
        for s in STREAMS:
            waits = self._waits(s, [d for d in deps if d[0] != ("eng", s)])
            if waits:
                self.ops[s].append((waits, None, None, 0))

    def emit(self):
        nc = self.nc
        sems = self.sems

        def run(stream, eng):
            for (waits, fn, key, inc) in self.ops[stream]:
                for (wkey, val) in waits:
                    eng.wait_ge(sems[wkey], val)
                if fn is not None:
                    fn(eng).then_inc(sems[key], inc)

        with nc.Block() as block:
            @block.tensor
            def _(e):
                run("pe", e)

            @block.scalar
            def _(e):
                run("act", e)

            @block.vector
            def _(e):
                run("dve", e)

            @block.gpsimd
            def _(e):
                run("pool", e)

            @block.sync
            def _(e):
                run("sp", e)


class Scope:
    def __init__(self, bld):
        self.bld = bld

    def __enter__(self):
        self.mark = self.bld.arena_ptr
        return self

    def __exit__(self, *a):
        self.bld.arena_ptr = self.mark
        return False

    def alloc(self, shape, dt):
        bld = self.bld
        esz = 4 if dt == F32 else 2
        n = 1
        for d in shape[1:]:
            n *= d
        nbytes = (n * esz + 63) // 64 * 64
        off = bld.arena_ptr
        bld.arena_ptr = off + nbytes
        bld.arena_peak = max(bld.arena_peak, bld.arena_ptr)
        assert bld.arena_ptr <= bld.ARENA, "SBUF arena overflow %d" % bld.arena_ptr
        ap = bld.arena[0:shape[0], off:off + n * esz].bitcast(dt)
        if len(shape) == 3:
            ap = ap.rearrange("p (a b) -> p a b", a=shape[1])
        elif len(shape) == 4:
            ap = ap.rearrange("p (a b c) -> p a b c", a=shape[1], b=shape[2])
        return ap


class Rot:
    def __init__(self, items):
        self.items = items
        self.i = 0

    def next(self):
        it = self.items[self.i % len(self.items)]
        self.i += 1
        return it


class Builder:
    def __init__(self, NT, debug=(), stop=99):
        self.stop = stop
        self.NT = NT
        self.T = NT * 128
        self.NTG = NT // 4
        self.debug = set(debug)
        self.dbg_out = {}
        self.nc = bass.Bass("TRN2", target_bir_lowering=False)

    def mm(self, out, lhsT, rhs, start=True, stop=True, r=(), w=()):
        self.P.op("pe", lambda e: e.matmul(out, lhsT=lhsT, rhs=rhs, start=start, stop=stop), r, w)

    def tr(self, out, in_, ident, r=(), w=()):
        self.P.op("pe", lambda e: e.transpose(out, in_, ident), r, w)

    def act(self, out, in_, func, bias=None, scale=None, accum=None, r=(), w=()):
        kw = {}
        if bias is not None:
            kw["bias"] = bias
        if scale is not None:
            kw["scale"] = scale
        if accum is not None:
            kw["accum_out"] = accum
        self.P.op("act", lambda e: e.activation(out=out, in_=in_, func=func, **kw), r, w)

    def ts(self, eng, out, in0, s1, s2, op0, op1=None, accum=None, r=(), w=()):
        kw = {}
        if op1 is not None:
            kw["op1"] = op1
        if accum is not None:
            kw["accum_out"] = accum
        self.P.op(eng, lambda e: e.tensor_scalar(out=out, in0=in0, scalar1=s1, scalar2=s2, op0=op0, **kw), r, w)

    def tt(self, eng, out, in0, in1, op, r=(), w=()):
        self.P.op(eng, lambda e: e.tensor_tensor(out=out, in0=in0, in1=in1, op=op), r, w)

    def stt(self, out, in0, scalar, in1, op0, op1, r=(), w=()):
        self.P.op("dve", lambda e: e.scalar_tensor_tensor(out=out, in0=in0, scalar=scalar, in1=in1,
                                                          op0=op0, op1=op1), r, w)

    def cp(self, eng, out, in_, r=(), w=()):
        if eng == "act":
            self.P.op("act", lambda e: e.activation(out=out, in_=in_, func=AF.Copy), r, w)
        else:
            self.P.op(eng, lambda e: e.tensor_copy(out, in_), r, w)

    def memset(self, eng, ap, val, w=()):
        self.P.op(eng, lambda e: e.memset(ap, val), (), w)

    def dma(self, q, out, in_, r=(), w=()):
        self.P.dma(q, lambda e: e.dma_start(out=out, in_=in_), r, w)

    def recip(self, out, in_, r=(), w=()):
        self.P.op("dve", lambda e: e.reciprocal(out=out, in_=in_), r, w)

    def sb(self, st, name, shape, dt):
        return st.alloc(shape, dt)

    def scope(self):
        return Scope(self)

    def din(self, name, shape, dt=F32):
        return self.nc.dram_tensor(name, list(shape), dt, kind="ExternalInput").ap()

    def dscr(self, name, shape, dt):
        kind = "ExternalOutput" if name in self.debug else "Internal"
        t = self.nc.dram_tensor(name, list(shape), dt, kind=kind)
        if name in self.debug:
            self.dbg_out[name] = (list(shape), dt)
        return t

    def rstd(self, out, ss, n, tmp, bufs_r, bufs_w):
        self.ts("dve", tmp, ss, 1.0 / n, EPS, ALU.mult, ALU.add, r=bufs_r, w=bufs_w)
        self.act(tmp, tmp, AF.Sqrt, r=bufs_w, w=bufs_w)
        self.recip(out, tmp, r=bufs_w, w=bufs_w)

    def build(self):
        nc = self.nc
        NT, T, NTG = self.NT, self.T, self.NTG
        din = self.din
        self.xo = din("xo", [T, 2048])
        self.xp = din("xp", [T, 2048])
        self.cb_d = din("cb", [16, 128])
        self.flg_d = din("flg", [128, 2])
        self.w_ada = din("w_ada", [2048, 12288])
        self.b_ada = din("b_ada", [96, 128])
        self.n1_d = din("norm1_g", [16, 128])
        self.w_in = din("w_in", [2048, 15248])
        self.cqg_d = din("cq_g", [4, 128])
        self.ckvg_d = din("ckv_g", [1, 256])
        self.kig_d = din("kidx_g", [1, 64])
        self.kib_d = din("kidx_b", [1, 64])
        self.w_uq = din("w_uq", [512, 2048])
        self.w_iq = din("w_iq", [512, 1024])
        self.w_uk = din("w_uk", [256, 2048])
        self.w_uv = din("w_uv", [256, 2048])
        self.relb_d = din("rel_bias", [32, 16])
        self.convw_d = din("conv_w", [192, 128])
        self.convb_d = din("conv_b", [48, 128])
        self.dtb_d = din("dt_bias", [1, 64])
        self.alog_d = din("a_log", [1, 64])
        self.dsk_d = din("d_skip", [1, 64])
        self.ssmg_d = din("ssm_g", [1, 4096])
        self.w_pa = din("w_proj_a", [2048, 2048])
        self.w_pb = din("w_proj_b", [4096, 2048])
        self.w_o = din("w_out", [2048, 2048])
        self.n2_d = din("norm2_g", [16, 128])
        self.w_g = din("w_gate", [2048, 5632])
        self.w_u = din("w_up", [2048, 5632])
        self.w_d = din("w_down", [5632, 2048])
        self.fg_d = din("final_g", [1, 2048])
        self.cst_d = din("cst", [128, 768])
        self.ohd_d = din("ohd", [32, 384])
        self.y = nc.dram_tensor("y", [T, 2048], F32, kind="ExternalOutput").ap()

        self.modv = self.dscr("modv", [96, 128], F32)
        self.tv = self.dscr("tv", [16, 384], F32)
        self.ckvT_d = self.dscr("ckvT_d", [256, 2 * T], BF16)
        self.ckvtok_d = self.dscr("ckvtok_d", [2 * T, 256], BF16)
        self.kidxT_d = self.dscr("kidxT_d", [128, 2 * T], BF16)
        self.cqT_d = self.dscr("cqT_d", [512, T], BF16)
        self.widx_d = self.dscr("widx_d", [T, 16], F32)
        self.x_d = self.dscr("x_d", [2 * T, 4096], BF16)
        self.z_d = self.dscr("z_d", [T, 4096], BF16)
        self.BT_d = self.dscr("BT_d", [1024, T], BF16)
        self.CT_d = self.dscr("CT_d", [1024, T], BF16)
        self.Btok_d = self.dscr("Btok_d", [2 * T, 1024], BF16)
        self.gT_d = self.dscr("gT_d", [4096, T], F32)
        self.yaT_d = self.dscr("yaT_d", [2048, T], BF16)
        self.ybT_d = self.dscr("ybT_d", [4096, T], BF16)
        self.scr = {n: Buf() for n in ("modv", "tv", "ckvT", "ckvtok", "kidxT", "cqT", "widx", "x0", "x1", "z",
                                       "BT", "CT", "Btok0", "Btok1", "gT", "yaT", "ybT")}
        self.outb = Buf()

        with ExitStack() as gst0:
            self.P = Prog(nc, gst0)
            self.ps = [gst0.enter_context(nc.psum_tensor("ps%d" % i, [128, 512], F32)) for i in range(8)]
            self.psb = [Buf(excl=True) for _ in range(8)]
            self.ARENA = 207 * 1024
            self.arena = gst0.enter_context(nc.sbuf_tensor("arena", [128, self.ARENA], mybir.dt.uint8))
            self.arena_ptr = 0
            self.arena_peak = 0
            self.gst = Scope(self)
            self.gst.__enter__()
            self.phase0()
            with self.scope() as s1:
                if self.stop >= 1:
                    self.phase_inproj(s1)
                if self.stop >= 2:
                    self.phase_ssd()
            if self.stop >= 3:
                self.phase_attn()
            if self.stop >= 4:
                self.phase_merge_ffn()
            self.P.wait_all("sp", [self.outb])
            self.P.barrier()
            self.P.emit()
        return nc

    def psr(self, idxs):
        return Rot([(self.ps[i], self.psb[i]) for i in idxs])

    def load_cols(self, st, rows_ap, R, out_ap, out_buf, name, psrot):
        tmp = self.sb(st, "lc_" + name, [R, 128], F32)
        tb = Buf()
        self.dma("sp", tmp[:], rows_ap, w=[tb])
        ps, pb = psrot.next()
        self.tr(ps[:, 0:R], tmp[:], self.IDF[0:R, 0:R], r=[tb, self.cstb], w=[pb])
        self.cp("dve", out_ap, ps[:, 0:R], r=[pb], w=[out_buf])

    def phase0(self):
        nc, gst = self.nc, self.gst
        sb = self.sb
        self.cstf = sb(gst, "cstf", [128, 768], F32)
        self.cstb = Buf()
        self.cstbf = sb(gst, "cstbf", [128, 768], BF16)
        self.cstbb = self.cstb
        self.dma("sp", self.cstf[:], self.cst_d, w=[self.cstb])
        self.cp("dve", self.cstbf[:], self.cstf[:], r=[self.cstb], w=[self.cstb])
        c = self.cstf
        self.IDF, self.U, self.L1, self.J, self.CM, self.ONES = (c[:, 0:128], c[:, 128:256], c[:, 256:384],
                                                                 c[:, 384:512], c[:, 512:640], c[:, 640:768])
        cbf = self.cstbf
        self.IDB, self.UB, self.ONESB = cbf[:, 0:128], cbf[:, 128:256], cbf[:, 640:768]
        self.vec = sb(gst, "vec", [128, 512], F32)
        self.vecb = Buf()
        v = self.vec
        self.modT = v[:, 0:96]
        self.n1T, self.n2T = v[:, 96:112], v[:, 112:128]
        self.s1, self.s2 = v[:, 128:144], v[:, 144:160]
        self.cqgT = v[:, 160:164]
        self.convbT = v[:, 164:212]
        self.convwT = v[:, 212:404]
        self.cT = v[:, 404:420]
        self.badaT = v[:, 420:516] if False else None
        self.flg = sb(gst, "flgt", [128, 2], F32)
        self.flgb = Buf()
        self.dma("sp", self.flg[:], self.flg_d, w=[self.flgb])
        self.bc = sb(gst, "bct", [128, 256 + 64 * 5], F32)
        self.bcb = Buf()
        b = self.bc
        self.ckvg_bc, self.kig_bc, self.kib_bc = b[:, 0:256], b[:, 256:320], b[:, 320:384]
        self.dtb_bc, self.a_bc, self.dsk_bc = b[:, 384:448], b[:, 448:512], b[:, 512:576]
        for ap, src in ((self.ckvg_bc, self.ckvg_d), (self.kig_bc, self.kig_d), (self.kib_bc, self.kib_d),
                        (self.dtb_bc, self.dtb_d), (self.a_bc, self.alog_d), (self.dsk_bc, self.dsk_d)):
            self.dma("sp", ap, src.partition_broadcast(128), w=[self.bcb])
        self.act(self.a_bc, self.a_bc, AF.Exp, r=[self.bcb], w=[self.bcb])
        self.ts("dve", self.a_bc, self.a_bc, -1.0, None, ALU.mult, r=[self.bcb], w=[self.bcb])
        self.c_actT = sb(gst, "c_actT", [128, 16], BF16)
        self.halo = sb(gst, "halo", [128, 48, 3], F32)
        self.halob = Buf()

        with self.scope() as st:
            rot = self.psr([0, 1, 2, 3])
            badaT = sb(st, "badaT", [128, 96], F32)
            bb = Buf()
            self.load_cols(st, self.b_ada, 96, badaT[:], bb, "bada", rot)
            self.load_cols(st, self.n1_d, 16, self.n1T, self.vecb, "n1", rot)
            self.load_cols(st, self.n2_d, 16, self.n2T, self.vecb, "n2", rot)
            self.load_cols(st, self.cqg_d, 4, self.cqgT, self.vecb, "cqg", rot)
            self.load_cols(st, self.convb_d, 48, self.convbT, self.vecb, "cvb", rot)
            self.load_cols(st, self.convw_d[0:96, :], 96, self.convwT[:, 0:96], self.vecb, "cvw0", rot)
            self.load_cols(st, self.convw_d[96:192, :], 96, self.convwT[:, 96:192], self.vecb, "cvw1", rot)
            self.load_cols(st, self.cb_d, 16, self.cT, self.vecb, "cb", rot)
            self.act(self.c_actT[:], self.cT, AF.Silu, r=[self.vecb], w=[self.vecb])
            wa = [sb(st, "wa%d" % i, [128, 16, 1536], BF16) for i in range(2)]
            wab = [Buf(), Buf()]
            wsrc = self.w_ada.rearrange("(kc p) n -> p kc n", p=128)
            psm, psmb = self.ps[4], self.psb[4]
            self.dma("pool", wa[0][:], wsrc[:, :, 0:1536], w=[wab[0]])
            for fgp in range(8):
                if fgp + 1 < 8:
                    self.dma("pool", wa[(fgp + 1) % 2][:], wsrc[:, :, (fgp + 1) * 1536:(fgp + 2) * 1536],
                             w=[wab[(fgp + 1) % 2]])
                wt, wb = wa[fgp % 2], wab[fgp % 2]
                for fc in range(12):
                    col = fgp * 12 + fc
                    for kc in range(16):
                        self.mm(psm[:, col:col + 1], wt[:, kc, fc * 128:(fc + 1) * 128], self.c_actT[:, kc:kc + 1],
                                start=(kc == 0), stop=(kc == 15), r=[wb, self.vecb], w=[psmb])
            self.tt("dve", self.modT, psm[:, 0:96], badaT[:], ALU.add, r=[psmb, bb], w=[self.vecb])
            self.stt(self.s1, self.modT[:, 16:32], 1.0, self.n1T, ALU.add, ALU.mult, r=[self.vecb], w=[self.vecb])
            self.stt(self.s2, self.modT[:, 64:80], 1.0, self.n2T, ALU.add, ALU.mult, r=[self.vecb], w=[self.vecb])
            self.sh1, self.sh2 = self.modT[:, 0:16], self.modT[:, 48:64]
            ps, pb = rot.next()
            self.tr(ps[0:96, 0:128], self.modT, self.IDF, r=[self.vecb, self.cstb], w=[pb])
            modr = sb(st, "modr", [96, 128], F32)
            mrb = Buf()
            self.cp("dve", modr[:], ps[0:96, 0:128], r=[pb], w=[mrb])
            self.dma("sp", self.modv.ap(), modr[:], r=[mrb], w=[self.scr["modv"]])
            self.P.barrier()

    def norm_transpose(self, st, x_dram, hT, hTb, s_cols, sh_cols, tag, src_tile=None):
        NT = self.NT
        xb_t = [self.sb(st, "xt%s%d" % (tag, i), [128, 2048], F32) for i in range(2)]
        xbb = [Buf(), Buf()]
        xn_t = [self.sb(st, "xn%s%d" % (tag, i), [128, 2048], BF16) for i in range(2)]
        xnb = [Buf(), Buf()]
        ss = self.sb(st, "ss" + tag, [128, 3 * NT], F32)
        ssb = [Buf() for _ in range(NT)]
        rot = self.psr([0, 1, 2, 3])
        import os as _os
        lvl = int(_os.environ.get("KNT", "9"))
        for tc in range(NT):
            xt, xtb = xb_t[tc % 2], xbb[tc % 2]
            xn, xnbb = xn_t[tc % 2], xnb[tc % 2]
            self.dma("sp", xt[:], x_dram[tc * 128:(tc + 1) * 128, :], w=[xtb])
            if lvl < 1:
                continue
            self.act(xn[:], xt[:], AF.Square, accum=ss[:, 3 * tc:3 * tc + 1], r=[xtb], w=[xnbb, ssb[tc]])
            self.rstd(ss[:, 3 * tc + 1:3 * tc + 2], ss[:, 3 * tc:3 * tc + 1], 2048.0, ss[:, 3 * tc + 2:3 * tc + 3],
                      [ssb[tc]], [ssb[tc]])
            if lvl < 2:
                continue
            self.act(xn[:], xt[:], AF.Copy, scale=ss[:, 3 * tc + 1:3 * tc + 2], r=[xtb, ssb[tc]], w=[xnbb])
            if lvl < 3:
                continue
            for half in range(2):
                ps, pb = rot.next()
                psv = ps[:].bitcast(BF16)
                for j in range(8):
                    fc = half * 8 + j
                    self.tr(psv[:, j * 128:(j + 1) * 128], xn[:, fc * 128:(fc + 1) * 128], self.IDB,
                            r=[xnbb, self.cstb], w=[pb])
                if lvl < 4:
                    continue
                for j in range(8):
                    fc = half * 8 + j
                    dst = hT[:, fc, tc * 128:(tc + 1) * 128]
                    if half == 0:
                        self.act(dst, psv[:, j * 128:(j + 1) * 128], AF.Identity, bias=sh_cols[:, fc:fc + 1],
                                 scale=s_cols[:, fc:fc + 1], r=[pb, self.vecb], w=[hTb[tc][fc]])
                    else:
                        self.ts("dve", dst, psv[:, j * 128:(j + 1) * 128], s_cols[:, fc:fc + 1], sh_cols[:, fc:fc + 1],
                                ALU.mult, ALU.add, r=[pb, self.vecb], w=[hTb[tc][fc]])

    def load_w(self, dst, w_dram, c0, W, buf, k0=0, KC=None):
        src = w_dram.rearrange("(kc p) n -> p kc n", p=128)
        if KC is None:
            self.dma("pool", dst, src[:, :, c0:c0 + W], w=[buf])
        else:
            self.dma("pool", dst, src[:, k0:k0 + KC, c0:c0 + W], w=[buf])

    def hT_reads(self, hTb, kc, tcs):
        return [hTb[tc][kc] for tc in tcs]

    def phase_inproj(self, gst):
        NT, T, NTG = self.NT, self.T, self.NTG
        sb = self.sb
        self.dtp = {}
        for name in ("de_p", "dec_p", "dt_o", "adt_o", "ea_o", "de_o", "dec_o"):
            self.dtp[name] = sb(gst, name, [128, NT, 64], F32)
        self.dtpb = {name: [Buf() for _ in range(NT)] for name in self.dtp}
        with self.scope() as st:
            hT = sb(st, "hT", [128, 16, T], BF16)
            hTb = [[Buf() for _ in range(16)] for _ in range(NT)]
            wt = [sb(st, "wblk%d" % i, [128, 16, 512], BF16) for i in range(2)]
            wtb = [Buf(), Buf()]
            for prefix in (True, False):
              with self.scope() as stn:
                self.norm_transpose(stn, self.xp if prefix else self.xo, hT, hTb, self.s1, self.sh1,
                                    "p" if prefix else "o")
                self.P.barrier()
              with self.scope() as ste:
                self.ip_tiles(ste)
                blocks = []
                if not prefix:
                    blocks.append(("tm0", [(OFF_CQ, 512, 0)]))
                blocks.append(("tm1", [(OFF_CKV, 336, 0), (OFF_DT, 64, 336)]))
                if not prefix:
                    for g in range(8):
                        blocks.append(("z", [(OFF_Z + g * 512, 512, 0)], g))
                for g in range(8):
                    blocks.append(("x", [(OFF_X + g * 512, 512, 0)], g))
                for g2 in range(2):
                    blocks.append(("B", [(OFF_B + g2 * 512, 512, 0)], g2))
                for g2 in range(2):
                    blocks.append(("C", [(OFF_C + g2 * 512, 512, 0)], g2))
                if not prefix:
                    for g in range(8):
                        blocks.append(("gate", [(OFF_GA + g * 512, 512, 0)], g))

                import os as _os
                _lim = _os.environ.get("KLIMIT")
                if _lim is not None:
                    blocks = [b_ for b_ in blocks if b_[0] in _lim.split(",")]
                if not blocks:
                    self.P.barrier()
                    continue

                def issue(i):
                    for (c0, W, d0) in blocks[i][1]:
                        self.load_w(wt[i % 2][:, :, d0:d0 + W], self.w_in, c0, W, wtb[i % 2])
                issue(0)
                for i, blk in enumerate(blocks):
                    if i + 1 < len(blocks):
                        issue(i + 1)
                    w_t, w_b = wt[i % 2], wtb[i % 2]
                    kind = blk[0]
                    if kind == "tm0":
                        self.ep_tm0(hT, hTb, w_t, w_b)
                    elif kind == "tm1":
                        self.ep_tm1(hT, hTb, w_t, w_b, prefix)
                    elif kind == "z":
                        self.ep_z(hT, hTb, w_t, w_b, blk[2])
                    elif kind == "x":
                        self.ep_conv(hT, hTb, w_t, w_b, prefix, "x", blk[2])
                    elif kind == "B":
                        self.ep_conv(hT, hTb, w_t, w_b, prefix, "B", blk[2])
                    elif kind == "C":
                        self.ep_conv(hT, hTb, w_t, w_b, prefix, "C", blk[2])
                    elif kind == "gate":
                        self.ep_gate(hT, hTb, w_t, w_b, blk[2])
                self.P.barrier()

    def ip_tiles(self, st):
        sb = self.sb
        T, NT = self.T, self.NT
        self.pre = sb(st, "pre", [128, T + 3], F32)
        self.preb = Buf()
        self.acc = sb(st, "cacc", [128, T], F32)
        self.accb = Buf()
        self.cvs = [sb(st, "cv%d" % i, [128, T], BF16) for i in range(4)]
        self.cvbs = [Buf() for _ in range(4)]
        self.xstage = sb(st, "xstage", [128, NT, 512], BF16)
        self.xstb = Buf()
        self.st512 = [sb(st, "st512_%d" % i, [128, 512], F32) for i in range(3)]
        self.st512b = [Buf() for _ in range(3)]
        self.st512r = Rot(list(zip(self.st512, self.st512b)))
        self.zst = [sb(st, "zst%d" % i, [128, 512], BF16) for i in range(3)]
        self.zstb = [Buf() for _ in range(3)]
        self.zstr = Rot(list(zip(self.zst, self.zstb)))
        self.sm = [sb(st, "smf%d" % i, [128, 512], F32) for i in range(2)]
        self.smb = [Buf(), Buf()]
        self.smh = [sb(st, "smh%d" % i, [128, 1024], BF16) for i in range(2)]
        self.smhb = [Buf(), Buf()]
        self.memset("dve", self.pre[:, 0:3], 0.0, w=[self.preb])

    def ep_tm0(self, hT, hTb, wt, wb):
        NT = self.NT
        rot = self.psr([0, 1, 2, 3])
        cqv = self.cqT_d.ap().rearrange("(fc p) t -> p fc t", p=128)
        for tc in range(NT):
            ps, pb = rot.next()
            for kc in range(16):
                self.mm(ps[:, 0:512], hT[:, kc, tc * 128:(tc + 1) * 128], wt[:, kc, 0:512], start=(kc == 0),
                        stop=(kc == 15), r=[hTb[tc][kc], wb], w=[pb])
            sm, smb = self.sm[tc % 2], self.smb[tc % 2]
            sh, shb = self.smh[tc % 2], self.smhb[tc % 2]
            self.act(sh[:, 0:512], ps[:, 0:512], AF.Square, accum=sm[:, 0:1], r=[pb], w=[shb, smb])
            self.rstd(sm[:, 1:2], sm[:, 0:1], 512.0, sm[:, 2:3], [smb], [smb])
            self.act(sh[:, 0:512], ps[:, 0:512], AF.Copy, scale=sm[:, 1:2], r=[pb, smb], w=[shb])
            ps2, pb2 = rot.next()
            p2v = ps2[:].bitcast(BF16)
            for fc in range(4):
                self.tr(p2v[:, fc * 128:(fc + 1) * 128], sh[:, fc * 128:(fc + 1) * 128], self.IDB,
                        r=[shb, self.cstb], w=[pb2])
            for fc in range(4):
                self.ts("dve", sh[:, 512 + fc * 128:512 + (fc + 1) * 128], p2v[:, fc * 128:(fc + 1) * 128],
                        self.cqgT[:, fc:fc + 1], None, ALU.mult, r=[pb2, self.vecb], w=[shb])
            self.dma("sp", cqv[:, :, tc * 128:(tc + 1) * 128],
                     sh[:, 512:1024].rearrange("p (fc t) -> p fc t", fc=4), r=[shb], w=[Buf()])

    def ep_tm1(self, hT, hTb, wt, wb, prefix):
        NT, T = self.NT, self.T
        rot = self.psr([0, 1, 2, 3])
        key0 = 0 if prefix else T
        ckvTv = self.ckvT_d.ap().rearrange("(rc p) s -> p rc s", p=128)
        D = self.dtp
        DB = self.dtpb
        for tc in range(NT):
            ps, pb = rot.next()
            for kc in range(16):
                self.mm(ps[:, 0:400], hT[:, kc, tc * 128:(tc + 1) * 128], wt[:, kc, 0:400], start=(kc == 0),
                        stop=(kc == 15), r=[hTb[tc][kc], wb], w=[pb])
            sm, smb = self.sm[tc % 2], self.smb[tc % 2]
            sh, shb = self.smh[tc % 2], self.smhb[tc % 2]
            kpos = key0 + tc * 128
            self.act(sh[:, 0:256], ps[:, 0:256], AF.Square, accum=sm[:, 0:1], r=[pb], w=[shb, smb])
            self.rstd(sm[:, 1:2], sm[:, 0:1], 256.0, sm[:, 2:3], [smb], [smb])
            self.stt(sh[:, 0:256], ps[:, 0:256], sm[:, 1:2], self.ckvg_bc, ALU.mult, ALU.mult,
                     r=[pb, smb, self.bcb], w=[shb])
            self.dma("sp", self.ckvtok_d.ap()[kpos:kpos + 128, :], sh[:, 0:256], r=[shb], w=[Buf()])
            ps2, pb2 = rot.next()
            p2v = ps2[:].bitcast(BF16)
            for rc in range(2):
                self.tr(p2v[:, rc * 128:(rc + 1) * 128], sh[:, rc * 128:(rc + 1) * 128], self.IDB,
                        r=[shb, self.cstb], w=[pb2])
            self.act(sm[:, 64:128], ps[:, 256:320], AF.Identity, accum=sm[:, 3:4], r=[pb], w=[smb])
            self.act(sm[:, 64:128], ps[:, 256:320], AF.Square, accum=sm[:, 4:5], r=[pb], w=[smb])
            self.ts("dve", sm[:, 5:6], sm[:, 3:4], 1.0 / 64, None, ALU.mult, r=[smb], w=[smb])
            self.tt("dve", sm[:, 6:7], sm[:, 5:6], sm[:, 5:6], ALU.mult, r=[smb], w=[smb])
            self.stt(sm[:, 7:8], sm[:, 4:5], 1.0 / 64, sm[:, 6:7], ALU.mult, ALU.subtract, r=[smb], w=[smb])
            self.ts("dve", sm[:, 8:9], sm[:, 7:8], EPS, None, ALU.add, r=[smb], w=[smb])
            self.act(sm[:, 8:9], sm[:, 8:9], AF.Sqrt, r=[smb], w=[smb])
            self.recip(sm[:, 9:10], sm[:, 8:9], r=[smb], w=[smb])
            self.ts("dve", sm[:, 64:128], ps[:, 256:320], sm[:, 5:6], sm[:, 9:10], ALU.subtract, ALU.mult,
                    r=[pb, smb], w=[smb])
            self.tt("dve", sm[:, 64:128], sm[:, 64:128], self.kig_bc, ALU.mult, r=[smb, self.bcb], w=[smb])
            self.tt("dve", sh[:, 256:320], sm[:, 64:128], self.kib_bc, ALU.add, r=[smb, self.bcb], w=[shb])
            self.cp("dve", sh[:, 320:384], sh[:, 256:320], r=[shb], w=[shb])
            self.tr(p2v[:, 256:384], sh[:, 256:384], self.IDB, r=[shb, self.cstb], w=[pb2])
            self.cp("act", sh[:, 512:896], p2v[:, 0:384], r=[pb2], w=[shb])
            self.dma("sp", ckvTv[:, :, kpos:kpos + 128], sh[:, 512:768].rearrange("p (rc s) -> p rc s", rc=2),
                     r=[shb], w=[Buf()])
            self.dma("sp", self.kidxT_d.ap()[:, kpos:kpos + 128], sh[:, 768:896], r=[shb], w=[Buf()])
            if not prefix:
                self.ts("dve", sm[:, 16:32], ps[:, 320:336], 1.0 / 32.0, None, ALU.mult, r=[pb], w=[smb])
                self.dma("sp", self.widx_d.ap()[tc * 128:(tc + 1) * 128, :], sm[:, 16:32], r=[smb],
                         w=[Buf()])
            dt_t = sm[:, 128:192]
            self.tt("dve", dt_t, ps[:, 336:400], self.dtb_bc, ALU.add, r=[pb, self.bcb], w=[smb])
            self.act(dt_t, dt_t, AF.Exp, r=[smb], w=[smb])
            self.act(dt_t, dt_t, AF.Ln, bias=1.0, r=[smb], w=[smb])
            adt = sm[:, 192:256]
            self.tt("dve", adt, dt_t, self.a_bc, ALU.mult, r=[smb, self.bcb], w=[smb])
            ps3, pb3 = rot.next()
            self.mm(ps3[:, 0:64], self.U, adt, r=[self.cstb, smb], w=[pb3])
            self.mm(ps3[:, 64:128], self.ONES, adt, r=[self.cstb, smb], w=[pb3])
            acs = sm[:, 256:320]
            self.cp("act", acs, ps3[:, 0:64], r=[pb3], w=[smb])
            pn = "p" if prefix else "o"
            self.act(D["dec_" + pn][:, tc, :], ps3[:, 64:128], AF.Exp, r=[pb3], w=[DB["dec_" + pn][tc]])
            self.tt("dve", sm[:, 320:384], ps3[:, 64:128], acs, ALU.subtract, r=[pb3, smb], w=[smb])
            self.act(sm[:, 320:384], sm[:, 320:384], AF.Exp, r=[smb], w=[smb])
            self.tt("dve", D["de_" + pn][:, tc, :], sm[:, 320:384], dt_t, ALU.mult, r=[smb], w=[DB["de_" + pn][tc]])
            if not prefix:
                self.cp("dve", D["dt_o"][:, tc, :], dt_t, r=[smb], w=[DB["dt_o"][tc]])
                self.cp("dve", D["adt_o"][:, tc, :], adt, r=[smb], w=[DB["adt_o"][tc]])
                self.act(D["ea_o"][:, tc, :], acs, AF.Exp, r=[smb], w=[DB["ea_o"][tc]])

    def ep_z(self, hT, hTb, wt, wb, g):
        rot = self.psr([0, 1, 2, 3])
        for tc in range(self.NT):
            ps, pb = rot.next()
            for kc in range(16):
                self.mm(ps[:, 0:512], hT[:, kc, tc * 128:(tc + 1) * 128], wt[:, kc, 0:512], start=(kc == 0),
                        stop=(kc == 15), r=[hTb[tc][kc], wb], w=[pb])
            zt, ztb = self.zstr.next()
            self.act(zt[:], ps[:, 0:512], AF.Silu, r=[pb], w=[ztb])
            self.dma("sp", self.z_d.ap()[tc * 128:(tc + 1) * 128, g * 512:(g + 1) * 512], zt[:], r=[ztb],
                     w=[Buf()])

    def ep_gate(self, hT, hTb, wt, wb, g):
        rot = self.psr([0, 1, 2, 3])
        for mc in range(4):
            for tg in range(self.NTG):
                ps, pb = rot.next()
                tcs = range(tg * 4, tg * 4 + 4)
                for kc in range(16):
                    self.mm(ps[:, 0:512], wt[:, kc, mc * 128:(mc + 1) * 128], hT[:, kc, tg * 512:(tg + 1) * 512],
                            start=(kc == 0), stop=(kc == 15), r=[wb] + self.hT_reads(hTb, kc, tcs), w=[pb])
                gt, gtb = self.st512r.next()
                self.act(gt[:], ps[:, 0:512], AF.Sigmoid, r=[pb], w=[gtb])
                row = (g * 4 + mc) * 128
                self.dma("sp", self.gT_d.ap()[row:row + 128, tg * 512:(tg + 1) * 512], gt[:], r=[gtb],
                         w=[Buf()])

    def ep_conv(self, hT, hTb, wt, wb, prefix, kind, g):
        NT, T, NTG = self.NT, self.T, self.NTG
        rot = self.psr([0, 1, 2, 3, 4, 5, 6, 7])
        row0 = 0 if prefix else T
        for cc in range(4):
            cv, cvb = self.cvs[cc], self.cvbs[cc]
            if kind == "x":
                chunk = g * 4 + cc
            elif kind == "B":
                chunk = 32 + g * 4 + cc
            else:
                chunk = 40 + g * 4 + cc
            hidx = chunk
            tgs = range(NTG)
            if kind == "C" and prefix:
                tgs = [NTG - 1]
            if not prefix:
                self.cp("dve", self.pre[:, 0:3], self.halo[:, hidx, :], r=[self.halob], w=[self.preb])
            for tg in tgs:
                ps, pb = rot.next()
                tcs = range(tg * 4, tg * 4 + 4)
                for kc in range(16):
                    self.mm(ps[:, 0:512], wt[:, kc, cc * 128:(cc + 1) * 128], hT[:, kc, tg * 512:(tg + 1) * 512],
                            start=(kc == 0), stop=(kc == 15), r=[wb] + self.hT_reads(hTb, kc, tcs), w=[pb])
                self.cp("act", self.pre[:, 3 + tg * 512:3 + (tg + 1) * 512], ps[:, 0:512], r=[pb], w=[self.preb])
            if prefix:
                self.ts("dve", self.halo[:, hidx, :], self.pre[:, T:T + 3], self.flg[:, 0:1], None, ALU.mult,
                        r=[self.preb, self.flgb], w=[self.halob])
                if kind == "C":
                    continue
            cw = self.convwT
            self.ts("dve", self.acc[:], self.pre[:, 0:T], cw[:, chunk:chunk + 1], None, ALU.mult,
                    r=[self.preb, self.vecb], w=[self.accb])
            for k in range(1, 4):
                self.stt(self.acc[:], self.pre[:, k:k + T], cw[:, k * 48 + chunk:k * 48 + chunk + 1], self.acc[:],
                         ALU.mult, ALU.add, r=[self.preb, self.vecb, self.accb], w=[self.accb])
            self.act(cv[:], self.acc[:], AF.Silu, bias=self.convbT[:, chunk:chunk + 1], r=[self.accb, self.vecb],
                     w=[cvb])
            gc = g * 4 + cc
            if kind == "C":
                self.dma("sp", self.CT_d.ap()[gc * 128:(gc + 1) * 128, :], cv[:], r=[cvb], w=[Buf()])
            elif kind == "B" and not prefix:
                self.dma("sp", self.BT_d.ap()[gc * 128:(gc + 1) * 128, :], cv[:], r=[cvb], w=[Buf()])
        if kind == "C":
            return
        for cc in range(4):
            cv, cvb = self.cvs[cc], self.cvbs[cc]
            for t4 in range(NT // 4):
                ps, pb = rot.next()
                pv = ps[:].bitcast(BF16)
                for q in range(4):
                    tc = t4 * 4 + q
                    self.tr(pv[:, q * 128:(q + 1) * 128], cv[:, tc * 128:(tc + 1) * 128], self.IDB,
                            r=[cvb, self.cstb], w=[pb])
                self.cp("dve" if t4 % 2 == 0 else "act", self.xstage[:, t4 * 4:(t4 + 1) * 4, cc * 128:(cc + 1) * 128],
                        pv[:, 0:512].rearrange("p (q c) -> p q c", q=4), r=[pb], w=[self.xstb])
        if kind == "x":
            dst = self.x_d.ap()[row0:row0 + T, g * 512:(g + 1) * 512].rearrange("(tc p) c -> p tc c", p=128)
            self.dma("sp", dst, self.xstage[:], r=[self.xstb], w=[Buf()])
        else:
            dst = self.Btok_d.ap()[row0:row0 + T, g * 512:(g + 1) * 512].rearrange("(tc p) c -> p tc c", p=128)
            self.dma("sp", dst, self.xstage[:], r=[self.xstb], w=[Buf()])

    def phase_ssd(self):
        NT, T = self.NT, self.T
        sb = self.sb
        D, DB = self.dtp, self.dtpb
        with self.scope() as st:
            self.stateT = sb(st, "stateT", [128, 4096], F32)
            self.stateb = [Buf() for _ in range(8)]
            xg = [sb(st, "xg%d" % i, [128, NT, 512], BF16) for i in range(2)]
            bt = [sb(st, "btok%d" % i, [128, NT, 128], BF16) for i in range(2)]
            zg = [sb(st, "zg%d" % i, [128, NT, 512], BF16) for i in range(2)]
            BgT = [sb(st, "BgT%d" % i, [128, T], BF16) for i in range(2)]
            CgT = [sb(st, "CgT%d" % i, [128, T], BF16) for i in range(2)]
            gsm = [sb(st, "gsm%d" % i, [128, 512], F32) for i in range(2)]
            gb = [Buf(), Buf()]
            ybst = sb(st, "ybst", [128, 4, T], BF16)
            ybb = Buf()
            def dbl(name, shape, dt):
                return [sb(st, name + str(i), shape, dt) for i in range(2)], [Buf(), Buf()]
            rseg, rsegb = dbl("rseg", [128, 8, 128], F32)
            Eb, Ebb = dbl("Eb", [128, 8, 128], BF16)
            cbm, cbmb = dbl("cbm", [128, 128], BF16)
            WT, WTb = dbl("WT", [128, 8, 128], BF16)
            xdt, xdtb = dbl("xdt", [128, 512], BF16)
            xw, xwb = dbl("xw", [128, 512], BF16)
            stb_t = sb(st, "stbf", [128, 512], BF16)
            stbb = Buf()
            y1s, y1bs = dbl("y1", [128, 512], F32)
            y2s, y2bs = dbl("y2", [128, 512], F32)
            y5s, y5bs = dbl("y5", [128, 512], BF16)
            ysms, ysmbs = dbl("ysm", [128, 8], F32)
            junks, junkbs = dbl("yjunk", [128, 512], BF16)
            rot = self.psr([0, 1, 2, 3, 4, 5, 6, 7])
            ybv = self.ybT_d.ap().rearrange("(cc p) t -> p cc t", p=128)

            for prefix in (True, False):
                row0 = 0 if prefix else T
                pn = "p" if prefix else "o"

                def load_group(g):
                    i = g % 2
                    self.dma("sp", xg[i][:], self.x_d.ap()[row0:row0 + T, g * 512:(g + 1) * 512]
                             .rearrange("(tc p) c -> p tc c", p=128), r=[self.scr["x0" if prefix else "x1"]], w=[gb[i]])
                    self.dma("sp", bt[i][:], self.Btok_d.ap()[row0:row0 + T, g * 128:(g + 1) * 128]
                             .rearrange("(tc p) c -> p tc c", p=128), r=[self.scr["Btok0" if prefix else "Btok1"]],
                             w=[gb[i]])
                    if not prefix:
                        self.dma("sp", zg[i][:], self.z_d.ap()[:, g * 512:(g + 1) * 512]
                                 .rearrange("(tc p) c -> p tc c", p=128), r=[self.scr["z"]], w=[gb[i]])
                        self.dma("sp", BgT[i][:], self.BT_d.ap()[g * 128:(g + 1) * 128, :], r=[self.scr["BT"]], w=[gb[i]])
                        self.dma("sp", CgT[i][:], self.CT_d.ap()[g * 128:(g + 1) * 128, :], r=[self.scr["CT"]], w=[gb[i]])
                        self.dma("sp", gsm[i][:], self.ssmg_d[:, g * 512:(g + 1) * 512].partition_broadcast(128),
                                 w=[gb[i]])
                load_group(0)
                for g in range(8):
                    if g + 1 < 8:
                        load_group(g + 1)
                    i = g % 2
                    G = gb[i]
                    stg = self.stateT[:, g * 512:(g + 1) * 512]
                    sbuf = self.stateb[g]
                    hs = slice(g * 8, (g + 1) * 8)
                    if prefix:
                        self.memset("dve", stg, 0.0, w=[sbuf])
                    else:
                        self.ts("dve", stg, stg, self.flg[:, 0:1], None, ALU.mult, r=[sbuf, self.flgb], w=[sbuf])
                        self.cp("act", stb_t[:], stg, r=[sbuf], w=[stbb])

                    def stage1(c):
                        k = c % 2
                        xc = xg[i][:, c, :]
                        self.tt("pool", xw[k][:].rearrange("p (h q) -> p h q", h=8),
                                xc.rearrange("p (h q) -> p h q", h=8),
                                D["de_" + pn][:, c, hs].unsqueeze(2).broadcast_to([128, 8, 64]), ALU.mult,
                                r=[G, DB["de_" + pn][c]], w=[xwb[k]])
                        if prefix:
                            return
                        adt = D["adt_o"][:, c, hs]
                        self.tt("pool", rseg[k][:], self.U.unsqueeze(1).broadcast_to([128, 8, 128]),
                                adt.unsqueeze(2).broadcast_to([128, 8, 128]), ALU.mult,
                                r=[self.cstb, DB["adt_o"][c]], w=[rsegb[k]])
                        for hh in range(2):
                            psS, psSb = rot.next()
                            self.mm(psS[:, 0:512], self.L1,
                                    rseg[k][:, hh * 4:(hh + 1) * 4, :].rearrange("p h t -> p (h t)"),
                                    r=[self.cstb, rsegb[k]], w=[psSb])
                            self.act(Eb[k][:, hh * 4:(hh + 1) * 4, :].rearrange("p h t -> p (h t)"), psS[:, 0:512],
                                     AF.Exp, r=[psSb], w=[Ebb[k]])
                        psC, psCb = rot.next()
                        self.mm(psC[:, 0:128], BgT[i][:, c * 128:(c + 1) * 128], CgT[i][:, c * 128:(c + 1) * 128],
                                r=[G], w=[psCb])
                        self.tt("dve", cbm[k][:], psC[:, 0:128], self.U, ALU.mult, r=[psCb, self.cstb], w=[cbmb[k]])
                        self.tt("pool", WT[k][:], Eb[k][:], cbm[k][:].unsqueeze(1).broadcast_to([128, 8, 128]), ALU.mult,
                                r=[Ebb[k], cbmb[k]], w=[WTb[k]])
                        self.tt("dve", xdt[k][:].rearrange("p (h q) -> p h q", h=8),
                                xc.rearrange("p (h q) -> p h q", h=8),
                                D["dt_o"][:, c, hs].unsqueeze(2).broadcast_to([128, 8, 64]), ALU.mult,
                                r=[G, DB["dt_o"][c]], w=[xdtb[k]])

                    def stage2(c):
                        k = c % 2
                        xc = xg[i][:, c, :]
                        y1, y1b, y2, y2b, y5, y5b = y1s[k], y1bs[k], y2s[k], y2bs[k], y5s[k], y5bs[k]
                        ysm, ysmb, junk, junkb = ysms[k], ysmbs[k], junks[k], junkbs[k]
                        if not prefix:
                            psY, psYb = rot.next()
                            for h in range(8):
                                self.mm(psY[:, h * 64:(h + 1) * 64], WT[k][:, h, :], xdt[k][:, h * 64:(h + 1) * 64],
                                        r=[WTb[k], xdtb[k]], w=[psYb])
                            psI, psIb = rot.next()
                            self.mm(psI[:, 0:512], CgT[i][:, c * 128:(c + 1) * 128], stb_t[:], r=[G, stbb], w=[psIb])
                        psN, psNb = rot.next()
                        self.mm(psN[:, 0:512], bt[i][:, c, :], xw[k][:], r=[G, xwb[k]], w=[psNb])
                        self.tt("dve", stg.rearrange("p (h q) -> p h q", h=8), stg.rearrange("p (h q) -> p h q", h=8),
                                D["dec_" + pn][:, c, hs].unsqueeze(2).broadcast_to([128, 8, 64]), ALU.mult,
                                r=[sbuf, DB["dec_" + pn][c]], w=[sbuf])
                        self.tt("dve", stg, stg, psN[:, 0:512], ALU.add, r=[sbuf, psNb], w=[sbuf])
                        if prefix:
                            return
                        self.tt("dve", y1[:].rearrange("p (h q) -> p h q", h=8),
                                psI[:, 0:512].rearrange("p (h q) -> p h q", h=8),
                                D["ea_o"][:, c, hs].unsqueeze(2).broadcast_to([128, 8, 64]), ALU.mult,
                                r=[psIb, DB["ea_o"][c]], w=[y1b])
                        if c + 1 < NT:
                            self.cp("act", stb_t[:], stg, r=[sbuf], w=[stbb])
                        self.tt("dve", y1[:], psY[:, 0:512], y1[:], ALU.add, r=[psYb, y1b], w=[y1b])
                        self.tt("pool", y2[:].rearrange("p (h q) -> p h q", h=8),
                                xc.rearrange("p (h q) -> p h q", h=8),
                                self.dsk_bc[:, hs].unsqueeze(2).broadcast_to([128, 8, 64]), ALU.mult,
                                r=[G, self.bcb], w=[y2b])
                        self.tt("dve", y1[:], y1[:], y2[:], ALU.add, r=[y1b, y2b], w=[y1b])
                        self.tt("dve", y1[:], y1[:], zg[i][:, c, :], ALU.mult, r=[y1b, G], w=[y1b])
                        self.act(junk[:], y1[:], AF.Square, accum=ysm[:, 0:1], r=[y1b], w=[junkb, ysmb])
                        self.rstd(ysm[:, 1:2], ysm[:, 0:1], 512.0, ysm[:, 2:3], [ysmb], [ysmb])
                        self.stt(y5[:], y1[:], ysm[:, 1:2], gsm[i][:], ALU.mult, ALU.mult, r=[y1b, ysmb, G], w=[y5b])
                        psT, psTb = rot.next()
                        ptv = psT[:].bitcast(BF16)
                        for cc in range(4):
                            self.tr(ptv[:, cc * 128:(cc + 1) * 128], y5[:, cc * 128:(cc + 1) * 128], self.IDB,
                                    r=[y5b, self.cstb], w=[psTb])
                        self.cp("act", ybst[:, :, c * 128:(c + 1) * 128],
                                ptv[:, 0:512].rearrange("p (cc t) -> p cc t", cc=4), r=[psTb], w=[ybb])

                    stage1(0)
                    for c in range(NT):
                        if c + 1 < NT:
                            stage1(c + 1)
                        stage2(c)
                    if not prefix:
                        self.dma("sp", ybv[:, g * 4:(g + 1) * 4, :], ybst[:], r=[ybb], w=[Buf()])
            self.P.barrier()

    def phase_attn(self):
        NT, T, NTG = self.NT, self.T, self.NTG
        sb = self.sb
        KB = 2 * NT
        with self.scope() as st:
            ckvT = sb(st, "ckvT", [128, 2, 2 * T], BF16)
            ckvtok = sb(st, "ckvtok", [128, KB, 256], BF16)
            kidxT = sb(st, "kidxT", [128, 2 * T], BF16)
            cqT_t = sb(st, "cqT", [128, 4, 512], BF16)
            cqb = Buf()
            widx = sb(st, "widx", [128, NT, 16], F32)
            ldb = Buf()
            self.dma("sp", ckvT[:], self.ckvT_d.ap().rearrange("(rc p) s -> p rc s", p=128), r=[self.scr["ckvT"]], w=[ldb])
            self.dma("sp", ckvtok[:], self.ckvtok_d.ap().rearrange("(kb p) r -> p kb r", p=128), r=[self.scr["ckvtok"]],
                     w=[ldb])
            self.dma("sp", kidxT[:], self.kidxT_d.ap(), r=[self.scr["kidxT"]], w=[ldb])
            self.dma("sp", widx[:], self.widx_d.ap().rearrange("(tc p) h -> p tc h", p=128), r=[self.scr["widx"]], w=[ldb])
            wuq = sb(st, "wuq", [128, 4, 2048], BF16)
            wiq = sb(st, "wiq", [128, 4, 1024], BF16)
            wuv = sb(st, "wuv", [128, 2, 2048], BF16)
            wukT = sb(st, "wukT", [128, 16, 256], BF16)
            biasT = sb(st, "biasT", [128, 16, 2, 128], BF16)
            wb_ = Buf()
            self.load_w(wuq[:], self.w_uq, 0, 2048, wb_)
            self.load_w(wiq[:], self.w_iq, 0, 1024, wb_)
            self.load_w(wuv[:], self.w_uv, 0, 2048, wb_)
            rot = self.psr([0, 1, 2, 3, 4])
            wukTb = Buf()
            biasb = Buf()
            with self.scope() as stmp:
                wuk = sb(stmp, "wuk", [128, 2, 2048], BF16)
                wkb = Buf()
                self.load_w(wuk[:], self.w_uk, 0, 2048, wkb)
                for h in range(16):
                    ps, pb = rot.next()
                    pv = ps[:].bitcast(BF16)
                    for rc in range(2):
                        self.tr(pv[:, rc * 128:(rc + 1) * 128], wuk[:, rc, h * 128:(h + 1) * 128], self.IDB,
                                r=[wkb, self.cstb], w=[pb])
                    self.cp("dve" if h % 2 else "act", wukT[:, h, :], pv[:, 0:256], r=[pb], w=[wukTb])
                relb = sb(stmp, "relb", [32, 16], F32)
                ohd = sb(stmp, "ohd", [32, 384], F32)
                tvs = sb(stmp, "tvs", [16, 384], F32)
                H = sb(stmp, "Hank", [128, 16, 2, 128], F32)
                bb = Buf()
                self.dma("sp", relb[:], self.relb_d, w=[bb])
                self.dma("sp", ohd[:], self.ohd_d, w=[bb])
                ps, pb = rot.next()
                self.mm(ps[0:16, 0:384], relb[:], ohd[:], r=[bb], w=[pb])
                tvb = Buf()
                self.cp("dve", tvs[:], ps[0:16, 0:384], r=[pb], w=[tvb])
                self.dma("sp", self.tv.ap(), tvs[:], r=[tvb], w=[self.scr["tv"]])
                hb = Buf()
                self.dma("sp", H[:], bass.AP(self.tv, 0, [[1, 128], [384, 16], [128, 2], [1, 128]]), r=[self.scr["tv"]],
                         w=[hb])
                Hf = H[:].rearrange("p h k t -> p (h k t)")
                Bf = biasT[:].rearrange("p h k t -> p (h k t)")
                for q in range(8):
                    ps, pb = rot.next()
                    self.mm(ps[:, 0:512], self.J, Hf[:, q * 512:(q + 1) * 512], r=[self.cstb, hb], w=[pb])
                    self.cp("dve" if q % 2 else "act", Bf[:, q * 512:(q + 1) * 512], ps[:, 0:512], r=[pb], w=[biasb])
                self.P.barrier()

            qiT = sb(st, "qiT", [128, 8, 512], BF16)
            qiTb = Buf()
            diagw = sb(st, "diagw", [128, 16, 128], BF16)
            diagb = Buf()
            Rh = [sb(st, "Rh%d" % i, [128, 512], BF16) for i in range(3)]
            Rhb = [Buf() for _ in range(3)]
            Rrot = Rot(list(zip(Rh, Rhb)))
            scores = [sb(st, "score%d" % i, [128, 2 * T], F32) for i in range(2)]
            scbs = [Buf(), Buf()]
            negm = sb(st, "negm", [128, 2 * T], BF16)
            negb = Buf()
            negT = sb(st, "negT", [128, KB, 512], BF16)
            negTb = Buf()
            bis = sb(st, "bis", [128, 8], F32)
            bisb = Buf()
            qT = sb(st, "qT", [128, 512], BF16)
            qTb = Buf()
            qlT = sb(st, "qlT", [128, 2, 512], BF16)
            qlTb = Buf()
            pT = [sb(st, "pT%d" % i, [128, 512], BF16) for i in range(4)]
            pTb = [Buf() for _ in range(4)]
            prot = Rot(list(zip(pT, pTb)))
            rec = sb(st, "rec", [128, 512], F32)
            recb = Buf()
            onT = sb(st, "onT", [128, 2, 512], BF16)
            onTb = Buf()
            yast = [sb(st, "yast%d" % i, [128, 512], BF16) for i in range(1)] * 2
            yastb = [Buf()] * 2
            psO0, psO0b = self.ps[5], self.psb[5]
            psO1, psO1b = self.ps[6], self.psb[6]
            psD, psDb = self.ps[7], self.psb[7]
            arot = self.psr([5, 6])
            pfx_blocks = NT
            wsteps = [16.0 / (2 ** k) for k in range(NBIS + 1)]

            for tg in range(NTG):
                nkb = pfx_blocks + (tg + 1) * 4
                self.dma("sp", cqT_t[:], self.cqT_d.ap()[:, tg * 512:(tg + 1) * 512].rearrange("(fc p) t -> p fc t", p=128),
                         r=[self.scr["cqT"]], w=[cqb])
                for hp in range(8):
                    ps, pb = rot.next()
                    for kc in range(4):
                        self.mm(ps[:, 0:512], wiq[:, kc, hp * 128:(hp + 1) * 128], cqT_t[:, kc, :],
                                start=(kc == 0), stop=(kc == 3), r=[wb_, cqb], w=[pb])
                    self.cp("act" if hp % 2 else "dve", qiT[:, hp, :], ps[:, 0:512], r=[pb], w=[qiTb])
                def b_stage(tq):
                    tcq = tg * 4 + tq
                    score, scb = scores[tcq % 2], scbs[tcq % 2]
                    S = (pfx_blocks + tcq + 1) * 128
                    for h in range(16):
                        self.ts("pool", diagw[:, h, :], self.IDB, widx[:, tcq, h:h + 1], None, ALU.mult,
                                r=[self.cstb, ldb], w=[diagb])
                    nb5 = (S + 511) // 512
                    for k5 in range(nb5):
                        wv = min(512, S - k5 * 512)
                        psA, psAb = arot.next()

                        def idx_s(h):
                            hp, base = h // 2, (h % 2) * 64
                            psI, psIb = rot.next()
                            self.mm(psI[:, 0:wv], qiT[base:base + 64, hp, tq * 128:(tq + 1) * 128],
                                    kidxT[base:base + 64, k5 * 512:k5 * 512 + wv], r=[qiTb, ldb], w=[psIb])
                            rh, rhb = Rrot.next()
                            self.act(rh[:, 0:wv], psI[:, 0:wv], AF.Relu, r=[psIb], w=[rhb])
                            return rh, rhb
                        pend = [idx_s(0), idx_s(1)]
                        for h in range(16):
                            if h + 2 < 16:
                                pend.append(idx_s(h + 2))
                            rh, rhb = pend[h]
                            self.mm(psA[:, 0:wv], diagw[:, h, :], rh[:, 0:wv], start=(h == 0), stop=(h == 15),
                                    r=[diagb, rhb], w=[psAb])
                        dst = score[:, k5 * 512:k5 * 512 + wv]
                        if k5 * 512 < T:
                            self.act(dst, psA[:, 0:wv], AF.Identity, bias=self.flg[:, 1:2], r=[psAb, self.flgb], w=[scb])
                        else:
                            self.cp("act", dst, psA[:, 0:wv], r=[psAb], w=[scb])
                    self.tt("dve", score[:, S - 128:S], score[:, S - 128:S], self.CM, ALU.add, r=[scb, self.cstb], w=[scb])

                def c_stage(tq):
                    tcq = tg * 4 + tq
                    score, scb = scores[tcq % 2], scbs[tcq % 2]
                    S = (pfx_blocks + tcq + 1) * 128
                    self.P.op("dve", lambda e, S=S: e.tensor_reduce(out=bis[:, 0:1], in_=score[:, 0:S], axis=AX.X,
                                                                   op=ALU.max), [scb], [bisb])
                    self.ts("dve", bis[:, 1:2], bis[:, 0:1], -16.0, None, ALU.add, r=[bisb], w=[bisb])
                    for k in range(NBIS):
                        self.ts("dve", negm[:, 0:S], score[:, 0:S], bis[:, 1:2], 0.0, ALU.is_ge, ALU.add,
                                accum=bis[:, 2:3], r=[scb, bisb], w=[negb, bisb])
                        if k + 1 < NBIS:
                            wn = wsteps[k + 1]
                            self.ts("dve", bis[:, 3:4], bis[:, 2:3], 255.5, 2.0 * wn, ALU.is_ge, ALU.mult, r=[bisb], w=[bisb])
                            self.stt(bis[:, 1:2], bis[:, 3:4], -wn, bis[:, 1:2], ALU.add, ALU.add, r=[bisb], w=[bisb])
                        else:
                            wl = wsteps[k]
                            self.ts("dve", bis[:, 3:4], bis[:, 2:3], 255.5, wl, ALU.is_ge, ALU.mult, r=[bisb], w=[bisb])
                            self.stt(bis[:, 4:5], bis[:, 3:4], -wl, bis[:, 1:2], ALU.add, ALU.add, r=[bisb], w=[bisb])
                    self.ts("dve", negm[:, 0:S], score[:, 0:S], bis[:, 4:5], NEG, ALU.is_lt, ALU.mult, r=[scb, bisb],
                            w=[negb])
                    nkq = S // 128
                    for k4 in range((nkq + 3) // 4):
                        n4 = min(4, nkq - k4 * 4)
                        ps, pb = rot.next()
                        pv = ps[:].bitcast(BF16)
                        for q in range(n4):
                            kb = k4 * 4 + q
                            self.tr(pv[:, q * 128:(q + 1) * 128], negm[:, kb * 128:(kb + 1) * 128], self.IDB,
                                    r=[negb, self.cstb], w=[pb])
                        self.cp("act" if k4 % 2 else "dve", negT[:, k4 * 4:k4 * 4 + n4, tq * 128:(tq + 1) * 128],
                                pv[:, 0:n4 * 128].rearrange("p (q t) -> p q t", q=n4), r=[pb], w=[negTb])
                    if nkq < nkb:
                        self.memset("pool", negT[:, nkq:nkb, tq * 128:(tq + 1) * 128], NEG, w=[negTb])
                b_stage(0)
                for tq in range(4):
                    if tq + 1 < 4:
                        b_stage(tq + 1)
                    c_stage(tq)
                for h in range(16):
                    ps, pb = rot.next()
                    for kc in range(4):
                        self.mm(ps[:, 0:512], wuq[:, kc, h * 128:(h + 1) * 128], cqT_t[:, kc, :],
                                start=(kc == 0), stop=(kc == 3), r=[wb_, cqb], w=[pb])
                    self.cp("dve", qT[:], ps[:, 0:512], r=[pb], w=[qTb])
                    for rc in range(2):
                        ps, pb = rot.next()
                        self.mm(ps[:, 0:512], wukT[:, h, rc * 128:(rc + 1) * 128], qT[:], r=[wukTb, qTb], w=[pb])
                        self.act(qlT[:, rc, :], ps[:, 0:512], AF.Copy, scale=128.0 ** -0.5, r=[pb], w=[qlTb])
                    def logits(kb):
                        psL, psLb = rot.next()
                        self.mm(psL[:, 0:512], ckvT[:, 0, kb * 128:(kb + 1) * 128], qlT[:, 0, :], start=True, stop=False,
                                r=[ldb, qlTb], w=[psLb])
                        self.mm(psL[:, 0:512], ckvT[:, 1, kb * 128:(kb + 1) * 128], qlT[:, 1, :], start=False, stop=False,
                                r=[ldb, qlTb], w=[psLb])
                        extra = []
                        for tq in range(4):
                            qb = pfx_blocks + tg * 4 + tq
                            if kb == qb:
                                extra.append((tq, 0))
                            elif kb == qb - 1:
                                extra.append((tq, 1))
                        self.mm(psL[:, 0:512], self.IDB, negT[:, kb, :], start=False, stop=(len(extra) == 0),
                                r=[self.cstb, negTb], w=[psLb])
                        for ei, (tq, kind) in enumerate(extra):
                            self.mm(psL[:, tq * 128:(tq + 1) * 128], self.IDB, biasT[:, h, kind, :], start=False,
                                    stop=(ei == len(extra) - 1), r=[self.cstb, biasb], w=[psLb])
                        pt, ptb = prot.next()
                        self.act(pt[:], psL[:, 0:512], AF.Exp, r=[psLb], w=[ptb])
                        return pt, ptb
                    pendl = [logits(0)]
                    if nkb > 1:
                        pendl.append(logits(1))
                    for kb in range(nkb):
                        if kb + 2 < nkb:
                            pendl.append(logits(kb + 2))
                        pt, ptb = pendl[kb]
                        first, last = (kb == 0), (kb == nkb - 1)
                        self.mm(psO0[:, 0:512], ckvtok[:, kb, 0:128], pt[:], start=first, stop=last, r=[ldb, ptb], w=[psO0b])
                        self.mm(psO1[:, 0:512], ckvtok[:, kb, 128:256], pt[:], start=first, stop=last, r=[ldb, ptb], w=[psO1b])
                        self.mm(psD[:, 0:512], self.ONESB, pt[:], start=first, stop=last, r=[self.cstb, ptb], w=[psDb])
                    self.recip(rec[:], psD[:, 0:512], r=[psDb], w=[recb])
                    self.tt("dve", onT[:, 0, :], psO0[:, 0:512], rec[:], ALU.mult, r=[psO0b, recb], w=[onTb])
                    self.tt("dve", onT[:, 1, :], psO1[:, 0:512], rec[:], ALU.mult, r=[psO1b, recb], w=[onTb])
                    ps, pb = rot.next()
                    for rc in range(2):
                        self.mm(ps[:, 0:512], wuv[:, rc, h * 128:(h + 1) * 128], onT[:, rc, :], start=(rc == 0),
                                stop=(rc == 1), r=[wb_, onTb], w=[pb])
                    ya, yab = yast[h % 2], yastb[h % 2]
                    self.cp("act", ya[:], ps[:, 0:512], r=[pb], w=[yab])
                    self.dma("sp", self.yaT_d.ap()[h * 128:(h + 1) * 128, tg * 512:(tg + 1) * 512], ya[:], r=[yab],
                             w=[Buf()])
            self.P.barrier()

    def phase_merge_ffn(self):
        NT, T, NTG = self.NT, self.T, self.NTG
        sb = self.sb
        mv = self.modv.ap()
        rot = self.psr([0, 1, 2, 3, 4, 5, 6, 7])
        dsrc = self.w_d.rearrange("(kc p) n -> p kc n", p=128)
        with self.scope() as st:
            x1 = sb(st, "x1", [128, 4, 2048], F32)
            x1b = [Buf() for _ in range(4)]
            h2T = sb(st, "h2T", [128, 16, 512], BF16)
            h2Tb = [[Buf() for _ in range(16)] for _ in range(4)]
            xn2 = sb(st, "xn2", [128, 2048], BF16)
            xn2b = Buf()
            mtmp = sb(st, "mtmp", [128, 512], F32)
            mtb = Buf()
            fsm = sb(st, "fsm", [128, 16], F32)
            fsmb = Buf()
            wA = [sb(st, "wA%d" % i, [128, 16, 512], BF16) for i in range(2)]
            wAb = [Buf(), Buf()]
            for tg in range(NTG):
                tsl = slice(tg * 512, (tg + 1) * 512)
                for q in range(4):
                    self.dma("sp", x1[:, q, :], self.xo[tg * 512 + q * 128:tg * 512 + (q + 1) * 128, :], w=[x1b[q]])
                with self.scope() as sm_:
                    g1bc = sb(sm_, "g1bc", [128, 2048], F32)
                    gbb = Buf()
                    self.dma("sp", g1bc[:], mv[32:48, :].rearrange("c p -> (c p)").partition_broadcast(128),
                             r=[self.scr["modv"]], w=[gbb])
                    yaT = sb(sm_, "yaT", [128, 16, 512], BF16)
                    ybT = sb(sm_, "ybT", [128, 32, 512], BF16)
                    yb_ = Buf()
                    mT = sb(sm_, "mT", [128, 16, 512], BF16)
                    mTb = [Buf() for _ in range(16)]
                    wB = [sb(sm_, "wB%d" % i, [128, 32, 128], BF16) for i in range(2)]
                    wBb = [Buf(), Buf()]
                    gt = [sb(sm_, "gt%d" % i, [128, 2, 512], F32) for i in range(2)]
                    gtb = [Buf(), Buf()]
                    self.dma("sp", yaT[:], self.yaT_d.ap()[:, tsl].rearrange("(kc p) t -> p kc t", p=128),
                             r=[self.scr["yaT"]], w=[yb_])
                    self.dma("sp", ybT[:], self.ybT_d.ap()[:, tsl].rearrange("(kc p) t -> p kc t", p=128),
                             r=[self.scr["ybT"]], w=[yb_])

                    def issue_m(i):
                        self.load_w(wA[i % 2][:, :, 0:128], self.w_pa, i * 128, 128, wAb[i % 2])
                        self.load_w(wB[i % 2][:], self.w_pb, i * 128, 128, wBb[i % 2])
                    issue_m(0)
                    for mc in range(16):
                        i = mc
                        if i + 1 < 16:
                            issue_m(i + 1)
                        g_t, g_b = gt[mc % 2], gtb[mc % 2]
                        self.dma("sp", g_t[:, 0, :], self.gT_d.ap()[mc * 128:(mc + 1) * 128, tsl], r=[self.scr["gT"]],
                                 w=[g_b])
                        self.dma("sp", g_t[:, 1, :], self.gT_d.ap()[2048 + mc * 128:2048 + (mc + 1) * 128, tsl],
                                 r=[self.scr["gT"]], w=[g_b])
                        psa, psab = rot.next()
                        for kc in range(16):
                            self.mm(psa[:, 0:512], wA[i % 2][:, kc, 0:128], yaT[:, kc, :],
                                    start=(kc == 0), stop=(kc == 15), r=[wAb[i % 2], yb_], w=[psab])
                        psb_, psbb = rot.next()
                        for kc in range(32):
                            self.mm(psb_[:, 0:512], wB[i % 2][:, kc, :], ybT[:, kc, :],
                                    start=(kc == 0), stop=(kc == 31), r=[wBb[i % 2], yb_], w=[psbb])
                        self.tt("dve", mtmp[:], psa[:, 0:512], g_t[:, 0, :], ALU.mult, r=[psab, g_b], w=[mtb])
                        self.tt("dve", g_t[:, 1, :], psb_[:, 0:512], g_t[:, 1, :], ALU.mult, r=[psbb, g_b], w=[g_b])
                        self.tt("pool", mT[:, mc, :], mtmp[:], g_t[:, 1, :], ALU.add, r=[mtb, g_b], w=[mTb[mc]])

                    def issue_o(i):
                        self.load_w(wA[i % 2][:], self.w_o, i * 512, 512, wAb[i % 2])
                    issue_o(0)
                    for i in range(4):
                        if i + 1 < 4:
                            issue_o(i + 1)
                        for q in range(4):
                            ps, pb = rot.next()
                            for kc in range(16):
                                self.mm(ps[:, 0:512], mT[:, kc, q * 128:(q + 1) * 128], wA[i % 2][:, kc, :],
                                        start=(kc == 0), stop=(kc == 15), r=[mTb[kc], wAb[i % 2]], w=[pb])
                            self.tt("dve", mtmp[:], ps[:, 0:512], g1bc[:, i * 512:(i + 1) * 512], ALU.mult, r=[pb, gbb],
                                    w=[mtb])
                            self.tt("pool", x1[:, q, i * 512:(i + 1) * 512], x1[:, q, i * 512:(i + 1) * 512], mtmp[:],
                                    ALU.add, r=[x1b[q], mtb], w=[x1b[q]])
                    self.P.barrier()
                for q in range(4):
                    self.act(xn2[:], x1[:, q, :], AF.Square, accum=fsm[:, 0:1], r=[x1b[q]], w=[xn2b, fsmb])
                    self.rstd(fsm[:, 1:2], fsm[:, 0:1], 2048.0, fsm[:, 2:3], [fsmb], [fsmb])
                    self.act(xn2[:], x1[:, q, :], AF.Copy, scale=fsm[:, 1:2], r=[x1b[q], fsmb], w=[xn2b])
                    for half in range(2):
                        ps, pb = rot.next()
                        pv = ps[:].bitcast(BF16)
                        for j in range(8):
                            fc = half * 8 + j
                            self.tr(pv[:, j * 128:(j + 1) * 128], xn2[:, fc * 128:(fc + 1) * 128], self.IDB,
                                    r=[xn2b, self.cstb], w=[pb])
                        for j in range(8):
                            fc = half * 8 + j
                            dst = h2T[:, fc, q * 128:(q + 1) * 128]
                            if half == 0:
                                self.act(dst, pv[:, j * 128:(j + 1) * 128], AF.Identity, bias=self.sh2[:, fc:fc + 1],
                                         scale=self.s2[:, fc:fc + 1], r=[pb, self.vecb], w=[h2Tb[q][fc]])
                            else:
                                self.ts("dve", dst, pv[:, j * 128:(j + 1) * 128], self.s2[:, fc:fc + 1],
                                        self.sh2[:, fc:fc + 1], ALU.mult, ALU.add, r=[pb, self.vecb], w=[h2Tb[q][fc]])
                with self.scope() as sf_:
                    g2bc = sb(sf_, "g2bc", [128, 2048], F32)
                    gbb = Buf()
                    self.dma("sp", g2bc[:], mv[80:96, :].rearrange("c p -> (c p)").partition_broadcast(128),
                             r=[self.scr["modv"]], w=[gbb])
                    aT = sb(sf_, "aT", [128, 44, 512], BF16)
                    aTb = [Buf() for _ in range(44)]
                    sg = sb(sf_, "sg", [128, 512], F32)
                    sgb = Buf()
                    wD = [sb(sf_, "wD%d" % i, [128, 44, 256], BF16) for i in range(2)]
                    wDb = [Buf(), Buf()]

                    def issue_f(i):
                        self.load_w(wA[i % 2][:, :, 0:256], self.w_g, i * 256, 256, wAb[i % 2])
                        self.load_w(wA[i % 2][:, :, 256:512], self.w_u, i * 256, 256, wAb[i % 2])

                    def issue_d(i):
                        self.dma("pool", wD[i % 2][:], dsrc[:, :, i * 256:(i + 1) * 256], w=[wDb[i % 2]])
                    issue_f(0)
                    for i in range(22):
                        if i + 1 < 22:
                            issue_f(i + 1)
                        elif True:
                            issue_d(0)
                        for sub in range(2):
                            fcx = i * 2 + sub
                            hr = [h2Tb[q][kc] for q in range(4) for kc in range(16)]
                            psg, psgb = rot.next()
                            for kc in range(16):
                                self.mm(psg[:, 0:512], wA[i % 2][:, kc, sub * 128:(sub + 1) * 128], h2T[:, kc, :],
                                        start=(kc == 0), stop=(kc == 15), r=[wAb[i % 2]] + (hr if kc == 0 else []),
                                        w=[psgb])
                            psu, psub = rot.next()
                            for kc in range(16):
                                self.mm(psu[:, 0:512], wA[i % 2][:, kc, 256 + sub * 128:256 + (sub + 1) * 128],
                                        h2T[:, kc, :], start=(kc == 0), stop=(kc == 15), r=[wAb[i % 2]], w=[psub])
                            self.act(sg[:], psg[:, 0:512], AF.Silu, r=[psgb], w=[sgb])
                            self.tt("dve", aT[:, fcx, :], psu[:, 0:512], sg[:], ALU.mult, r=[psub, sgb], w=[aTb[fcx]])
                    for i in range(8):
                        if i + 1 < 8:
                            issue_d(i + 1)
                        for q in range(4):
                            ps, pb = rot.next()
                            for kc in range(44):
                                self.mm(ps[:, 0:256], aT[:, kc, q * 128:(q + 1) * 128], wD[i % 2][:, kc, :],
                                        start=(kc == 0), stop=(kc == 43), r=[aTb[kc], wDb[i % 2]], w=[pb])
                            self.tt("dve", mtmp[:, 0:256], ps[:, 0:256], g2bc[:, i * 256:(i + 1) * 256], ALU.mult,
                                    r=[pb, gbb], w=[mtb])
                            self.tt("pool", x1[:, q, i * 256:(i + 1) * 256], x1[:, q, i * 256:(i + 1) * 256],
                                    mtmp[:, 0:256], ALU.add, r=[x1b[q], mtb], w=[x1b[q]])
                    self.P.barrier()
                with self.scope() as so_:
                    fgbc = sb(so_, "fgbc", [128, 2048], F32)
                    gbb = Buf()
                    self.dma("sp", fgbc[:], self.fg_d.partition_broadcast(128), w=[gbb])
                    osb = [sb(so_, "osb%d" % i, [128, 2048], F32) for i in range(2)]
                    osbb = [Buf(), Buf()]
                    for q in range(4):
                        ob, obb = osb[q % 2], osbb[q % 2]
                        self.act(xn2[:], x1[:, q, :], AF.Square, accum=fsm[:, 4:5], r=[x1b[q]], w=[xn2b, fsmb])
                        self.rstd(fsm[:, 5:6], fsm[:, 4:5], 2048.0, fsm[:, 6:7], [fsmb], [fsmb])
                        self.stt(ob[:], x1[:, q, :], fsm[:, 5:6], fgbc[:], ALU.mult, ALU.mult, r=[x1b[q], fsmb, gbb],
                                 w=[obb])
                        self.dma("sp", self.y[tg * 512 + q * 128:tg * 512 + (q + 1) * 128, :], ob[:], r=[obb],
                                 w=[Buf()])
                    self.P.barrier()


def _t5_bucket(d):
    n = np.maximum(d, 0)
    nf = np.maximum(n, 1).astype(np.float32)
    large = 16 + (np.log(nf / np.float32(16)) / np.float32(np.log(128 / 16)) * np.float32(16)).astype(np.int32)
    large = np.minimum(large, 31)
    return np.where(n < 16, n, large)


def _consts():
    cst = np.zeros((128, 768), np.float32)
    i = np.arange(128)
    cst[:, 0:128] = np.eye(128)
    cst[:, 128:256] = (i[:, None] <= i[None, :])
    cst[:, 256:384] = (i[:, None] > i[None, :])
    cst[:, 384:512] = np.eye(128)[::-1]
    cst[:, 512:640] = np.where(i[None, :] <= i[:, None], 0.0, -1e30)
    cst[:, 640:768] = 1.0
    ohd = np.zeros((32, 384), np.float32)
    for j in range(384):
        d = j - 127
        if 0 <= d <= 255:
            ohd[int(_t5_bucket(np.array(d))), j] += 1.0
            ohd[31, j] -= 1.0
    return cst, ohd


_CACHE = {}


def make_in_maps(inputs, NT, n_seq):
    T = NT * 128
    f = lambda a: np.ascontiguousarray(np.asarray(a, dtype=np.float32))
    x = f(inputs["x"])
    c = f(inputs["c"])
    cst, ohd = _consts()
    shared = {
        "w_ada": f(inputs["w_ada"][0]), "b_ada": f(inputs["b_ada"][0]).reshape(96, 128),
        "norm1_g": f(inputs["norm1_g"][0]).reshape(16, 128), "w_in": f(inputs["w_in"][0]),
        "cq_g": f(inputs["cq_norm_g"][0]).reshape(4, 128), "ckv_g": f(inputs["ckv_norm_g"][0]).reshape(1, 256),
        "kidx_g": f(inputs["kidx_norm_g"][0]).reshape(1, 64), "kidx_b": f(inputs["kidx_norm_b"][0]).reshape(1, 64),
        "w_uq": f(inputs["w_uq"][0]), "w_iq": f(inputs["w_iq"][0]), "w_uk": f(inputs["w_uk"][0]),
        "w_uv": f(inputs["w_uv"][0]), "rel_bias": f(inputs["rel_bias"]),
        "conv_w": f(inputs["conv_w"][0]).reshape(192, 128), "conv_b": f(inputs["conv_b"][0]).reshape(48, 128),
        "dt_bias": f(inputs["dt_bias"][0]).reshape(1, 64), "a_log": f(inputs["a_log"][0]).reshape(1, 64),
        "d_skip": f(inputs["d_skip"][0]).reshape(1, 64), "ssm_g": f(inputs["ssm_norm_g"][0]).reshape(1, 4096),
        "w_proj_a": f(inputs["w_proj_a"][0]), "w_proj_b": f(inputs["w_proj_b"][0]), "w_out": f(inputs["w_out"][0]),
        "norm2_g": f(inputs["norm2_g"][0]).reshape(16, 128), "w_gate": f(inputs["w_gate"][0]),
        "w_up": f(inputs["w_up"][0]), "w_down": f(inputs["w_down"][0]), "final_g": f(inputs["final_g"]).reshape(1, 2048),
        "cst": cst, "ohd": ohd,
    }
    maps = []
    for core in range(2 * n_seq):
        b, j = core // 2, core % 2
        flg = np.zeros((128, 2), np.float32)
        flg[:, 0] = float(j)
        flg[:, 1] = 0.0 if j == 1 else -1e30
        m = dict(shared)
        m["xo"] = np.ascontiguousarray(x[b, j * T:(j + 1) * T])
        m["xp"] = np.ascontiguousarray(x[b, 0:T])
        m["cb"] = np.ascontiguousarray(c[b].reshape(16, 128))
        m["flg"] = flg
        maps.append(m)
    return maps


def run(inputs, NT, debug=(), stop=99, n_seq=4):
    key = (NT, tuple(debug), stop)
    if key not in _CACHE:
        bld = Builder(NT, debug, stop)
        nc = bld.build()
        _CACHE[key] = (bld, nc)
    bld, nc = _CACHE[key]
    maps = make_in_maps(inputs, NT, n_seq)
    res = run_bass_kernel_spmd(nc, maps, core_ids=list(range(2 * n_seq)))
    T = NT * 128
    out = np.zeros((4, 2 * T, 2048), np.float32)
    for core in range(2 * n_seq):
        b, j = core // 2, core % 2
        out[b, j * T:(j + 1) * T] = res.results[core]["y"]
    return out, res


def kernel(**inputs):
    out, _ = run(inputs, 16)
    return out
```

```python
import numpy as np
from contextlib import ExitStack
import concourse.bass as bass
import concourse.mybir as mybir
from concourse.bass_utils import run_bass_kernel_spmd

F32 = mybir.dt.float32
BF16 = mybir.dt.bfloat16
AF = mybir.ActivationFunctionType
ALU = mybir.AluOpType
AX = mybir.AxisListType

STREAMS = ("pe", "act", "dve", "pool", "sp")
N_DMA_SEMS = 10
EPS = 1e-6

OFF_CQ, OFF_CKV, OFF_KIDX, OFF_WIDX, OFF_Z, OFF_X, OFF_B, OFF_C, OFF_DT, OFF_GA, OFF_GB = (
    0, 512, 768, 832, 848, 4944, 9040, 10064, 11088, 11152, 13200)
NEG = -30000.0
NBIS = 17


class Buf:
    __slots__ = ("w", "r", "excl")

    def __init__(self, excl=False):
        self.w = None
        self.r = {}
        self.excl = excl


class Prog:
    def __init__(self, nc, stack):
        self.nc = nc
        self.ops = {s: [] for s in STREAMS}
        self.cnt = {s: 0 for s in STREAMS}
        self.waited = {s: {} for s in STREAMS}
        self.sems = {}
        for s in STREAMS:
            self.sems[("eng", s)] = stack.enter_context(nc.semaphore("s_" + s))
        self.dma_i = {}
        self.dma_target = {}
        for s in ("sp", "pool", "act"):
            self.dma_i[s] = 0
            for k in range(N_DMA_SEMS):
                key = ("dma", s, k)
                self.sems[key] = stack.enter_context(nc.semaphore("d_%s%d" % (s, k)))
                self.dma_target[key] = 0

    def _collect(self, stream, reads, writes):
        deps = []
        pe = stream == "pe"
        for b in reads:
            if b.w is not None and not (pe and b.w[2] == "pe"):
                deps.append(b.w)
            if b.excl:
                for tok in b.r.values():
                    if tok[2] != stream:
                        deps.append(tok)
        for b in writes:
            if b.w is not None and not (pe and b.w[2] == "pe"):
                deps.append(b.w)
            for tok in b.r.values():
                if not (pe and tok[2] == "pe"):
                    deps.append(tok)
        return deps

    def _waits(self, stream, deps):
        best = {}
        for (key, val, _p) in deps:
            if val > best.get(key, 0):
                best[key] = val
        out = []
        wd = self.waited[stream]
        for key, val in best.items():
            if wd.get(key, 0) >= val:
                continue
            wd[key] = val
            out.append((key, val))
        return out

    def _update(self, tok, reads, writes):
        for b in reads:
            b.r[tok[2]] = tok
        for b in writes:
            b.w = tok
            b.r = {}

    def op(self, stream, fn, reads=(), writes=()):
        deps = self._collect(stream, reads, writes)
        waits = self._waits(stream, deps)
        self.cnt[stream] += 1
        key = ("eng", stream)
        tok = (key, self.cnt[stream], stream)
        self.ops[stream].append((waits, fn, key, 1))
        self._update(tok, reads, writes)
        return tok

    def dma(self, stream, fn, reads=(), writes=()):
        i = self.dma_i[stream]
        self.dma_i[stream] = i + 1
        key = ("dma", stream, i % N_DMA_SEMS)
        deps = self._collect("dma:" + stream, reads, writes)
        prev = self.dma_target[key]
        if prev > 0:
            deps.append((key, prev, "x"))
        waits = self._waits(stream, deps)
        self.dma_target[key] = prev + 16
        tok = (key, prev + 16, "dma:%s%d" % (stream, i % N_DMA_SEMS))
        self.ops[stream].append((waits, fn, key, 16))
        self._update(tok, reads, writes)
        return tok

    def wait_all(self, stream, bufs):
        deps = [b.w for b in bufs if b.w is not None]
        waits = self._waits(stream, deps)
        if waits:
            self.ops[stream].append((waits, None, None, 0))

    def barrier(self):
        deps = [(("eng", s), self.cnt[s], s) for s in STREAMS if self.cnt[s] > 0]
        deps += [(k, v, "x") for k, v in self.dma_target.items() if v > 0]
        for s in STREAMS:
            waits = self._waits(s, [d for d in deps if d[0] != ("eng", s)])
            if waits:
                self.ops[s].append((waits, None, None, 0))

    def emit(self):
        nc = self.nc
        sems = self.sems

        def run(stream, eng):
            for (waits, fn, key, inc) in self.ops[stream]:
                for (wkey, val) in waits:
                    eng.wait_ge(sems[wkey], val)
                if fn is not None:
                    fn(eng).then_inc(sems[key], inc)

        with nc.Block() as block:
            @block.tensor
            def _(e):
                run("pe", e)

            @block.scalar
            def _(e):
                run("act", e)

            @block.vector
            def _(e):
                run("dve", e)

            @block.gpsimd
            def _(e):
                run("pool", e)

            @block.sync
            def _(e):
                run("sp", e)


class Scope:
    def __init__(self, bld):
        self.bld = bld

    def __enter__(self):
        self.mark = self.bld.arena_ptr
        return self

    def __exit__(self, *a):
        self.bld.arena_ptr = self.mark
        return False

    def alloc(self, shape, dt):
        bld = self.bld
        esz = 4 if dt == F32 else 2
        n = 1
        for d in shape[1:]:
            n *= d
        nbytes = (n * esz + 63) // 64 * 64
        off = bld.arena_ptr
        bld.arena_ptr = off + nbytes
        bld.arena_peak = max(bld.arena_peak, bld.arena_ptr)
        assert bld.arena_ptr <= bld.ARENA, "SBUF arena overflow %d" % bld.arena_ptr
        ap = bld.arena[0:shape[0], off:off + n * esz].bitcast(dt)
        if len(shape) == 3:
            ap = ap.rearrange("p (a b) -> p a b", a=shape[1])
        elif len(shape) == 4:
            ap = ap.rearrange("p (a b c) -> p a b c", a=shape[1], b=shape[2])
        return ap


class Rot:
    def __init__(self, items):
        self.items = items
        self.i = 0

    def next(self):
        it = self.items[self.i % len(self.items)]
        self.i += 1
        return it


class Builder:
    def __init__(self, NT, debug=(), stop=99):
        self.stop = stop
        self.NT = NT
        self.T = NT * 128
        self.NTG = NT // 4
        self.debug = set(debug)
        self.dbg_out = {}
        self.nc = bass.Bass("TRN2", target_bir_lowering=False)

    def mm(self, out, lhsT, rhs, start=True, stop=True, r=(), w=()):
        self.P.op("pe", lambda e: e.matmul(out, lhsT=lhsT, rhs=rhs, start=start, stop=stop), r, w)

    def tr(self, out, in_, ident, r=(), w=()):
        self.P.op("pe", lambda e: e.transpose(out, in_, ident), r, w)

    def act(self, out, in_, func, bias=None, scale=None, accum=None, r=(), w=()):
        kw = {}
        if bias is not None:
            kw["bias"] = bias
        if scale is not None:
            kw["scale"] = scale
        if accum is not None:
            kw["accum_out"] = accum
        self.P.op("act", lambda e: e.activation(out=out, in_=in_, func=func, **kw), r, w)

    def ts(self, eng, out, in0, s1, s2, op0, op1=None, accum=None, r=(), w=()):
        kw = {}
        if op1 is not None:
            kw["op1"] = op1
        if accum is not None:
            kw["accum_out"] = accum
        self.P.op(eng, lambda e: e.tensor_scalar(out=out, in0=in0, scalar1=s1, scalar2=s2, op0=op0, **kw), r, w)

    def tt(self, eng, out, in0, in1, op, r=(), w=()):
        self.P.op(eng, lambda e: e.tensor_tensor(out=out, in0=in0, in1=in1, op=op), r, w)

    def stt(self, out, in0, scalar, in1, op0, op1, r=(), w=()):
        self.P.op("dve", lambda e: e.scalar_tensor_tensor(out=out, in0=in0, scalar=scalar, in1=in1,
                                                          op0=op0, op1=op1), r, w)

    def cp(self, eng, out, in_, r=(), w=()):
        if eng == "act":
            self.P.op("act", lambda e: e.activation(out=out, in_=in_, func=AF.Copy), r, w)
        else:
            self.P.op(eng, lambda e: e.tensor_copy(out, in_), r, w)

    def memset(self, eng, ap, val, w=()):
        self.P.op(eng, lambda e: e.memset(ap, val), (), w)

    def dma(self, q, out, in_, r=(), w=()):
        self.P.dma(q, lambda e: e.dma_start(out=out, in_=in_), r, w)

    def recip(self, out, in_, r=(), w=()):
        self.P.op("dve", lambda e: e.reciprocal(out=out, in_=in_), r, w)

    def sb(self, st, name, shape, dt):
        return st.alloc(shape, dt)

    def scope(self):
        return Scope(self)

    def din(self, name, shape, dt=F32):
        return self.nc.dram_tensor(name, list(shape), dt, kind="ExternalInput").ap()

    def dscr(self, name, shape, dt):
        kind = "ExternalOutput" if name in self.debug else "Internal"
        t = self.nc.dram_tensor(name, list(shape), dt, kind=kind)
        if name in self.debug:
            self.dbg_out[name] = (list(shape), dt)
        return t

    def rstd(self, out, ss, n, tmp, bufs_r, bufs_w):
        self.ts("dve", tmp, ss, 1.0 / n, EPS, ALU.mult, ALU.add, r=bufs_r, w=bufs_w)
        self.act(tmp, tmp, AF.Sqrt, r=bufs_w, w=bufs_w)
        self.recip(out, tmp, r=bufs_w, w=bufs_w)

    def build(self):
        nc = self.nc
        NT, T, NTG = self.NT, self.T, self.NTG
        din = self.din
        self.xo = din("xo", [T, 2048])
        self.xp = din("xp", [T, 2048])
        self.cb_d = din("cb", [16, 128])
        self.flg_d = din("flg", [128, 2])
        self.w_ada = din("w_ada", [2048, 12288])
        self.b_ada = din("b_ada", [96, 128])
        self.n1_d = din("norm1_g", [16, 128])
        self.w_in = din("w_in", [2048, 15248])
        self.cqg_d = din("cq_g", [4, 128])
        self.ckvg_d = din("ckv_g", [1, 256])
        self.kig_d = din("kidx_g", [1, 64])
        self.kib_d = din("kidx_b", [1, 64])
        self.w_uq = din("w_uq", [512, 2048])
        self.w_iq = din("w_iq", [512, 1024])
        self.w_uk = din("w_uk", [256, 2048])
        self.w_uv = din("w_uv", [256, 2048])
        self.relb_d = din("rel_bias", [32, 16])
        self.convw_d = din("conv_w", [192, 128])
        self.convb_d = din("conv_b", [48, 128])
        self.dtb_d = din("dt_bias", [1, 64])
        self.alog_d = din("a_log", [1, 64])
        self.dsk_d = din("d_skip", [1, 64])
        self.ssmg_d = din("ssm_g", [1, 4096])
        self.w_pa = din("w_proj_a", [2048, 2048])
        self.w_pb = din("w_proj_b", [4096, 2048])
        self.w_o = din("w_out", [2048, 2048])
        self.n2_d = din("norm2_g", [16, 128])
        self.w_g = din("w_gate", [2048, 5632])
        self.w_u = din("w_up", [2048, 5632])
        self.w_d = din("w_down", [5632, 2048])
        self.fg_d = din("final_g", [1, 2048])
        self.cst_d = din("cst", [128, 768])
        self.ohd_d = din("ohd", [32, 384])
        self.y = nc.dram_tensor("y", [T, 2048], F32, kind="ExternalOutput").ap()

        self.modv = self.dscr("modv", [96, 128], F32)
        self.tv = self.dscr("tv", [16, 384], F32)
        self.ckvT_d = self.dscr("ckvT_d", [256, 2 * T], BF16)
        self.ckvtok_d = self.dscr("ckvtok_d", [2 * T, 256], BF16)
        self.kidxT_d = self.dscr("kidxT_d", [128, 2 * T], BF16)
        self.cqT_d = self.dscr("cqT_d", [512, T], BF16)
        self.widx_d = self.dscr("widx_d", [T, 16], F32)
        self.x_d = self.dscr("x_d", [2 * T, 4096], BF16)
        self.z_d = self.dscr("z_d", [T, 4096], BF16)
        self.BT_d = self.dscr("BT_d", [1024, T], BF16)
        self.CT_d = self.dscr("CT_d", [1024, T], BF16)
        self.Btok_d = self.dscr("Btok_d", [2 * T, 1024], BF16)
        self.gT_d = self.dscr("gT_d", [4096, T], F32)
        self.yaT_d = self.dscr("yaT_d", [2048, T], BF16)
        self.ybT_d = self.dscr("ybT_d", [4096, T], BF16)
        self.scr = {n: Buf() for n in ("modv", "tv", "ckvT", "ckvtok", "kidxT", "cqT", "widx", "x0", "x1", "z",
                                       "BT", "CT", "Btok0", "Btok1", "gT", "yaT", "ybT")}
        self.outb = Buf()

        with ExitStack() as gst0:
            self.P = Prog(nc, gst0)
            self.ps = [gst0.enter_context(nc.psum_tensor("ps%d" % i, [128, 512], F32)) for i in range(8)]
            self.psb = [Buf(excl=True) for _ in range(8)]
            self.ARENA = 207 * 1024
            self.arena = gst0.enter_context(nc.sbuf_tensor("arena", [128, self.ARENA], mybir.dt.uint8))
            self.arena_ptr = 0
            self.arena_peak = 0
            self.gst = Scope(self)
            self.gst.__enter__()
            self.phase0()
            with self.scope() as s1:
                if self.stop >= 1:
                    self.phase_inproj(s1)
                if self.stop >= 2:
                    self.phase_ssd()
            if self.stop >= 3:
                self.phase_attn()
            if self.stop >= 4:
                self.phase_merge_ffn()
            self.P.wait_all("sp", [self.outb])
            self.P.barrier()
            self.P.emit()
        return nc

    def psr(self, idxs):
        return Rot([(self.ps[i], self.psb[i]) for i in idxs])

    def load_cols(self, st, rows_ap, R, out_ap, out_buf, name, psrot):
        tmp = self.sb(st, "lc_" + name, [R, 128], F32)
        tb = Buf()
        self.dma("sp", tmp[:], rows_ap, w=[tb])
        ps, pb = psrot.next()
        self.tr(ps[:, 0:R], tmp[:], self.IDF[0:R, 0:R], r=[tb, self.cstb], w=[pb])
        self.cp("dve", out_ap, ps[:, 0:R], r=[pb], w=[out_buf])

    def phase0(self):
        nc, gst = self.nc, self.gst
        sb = self.sb
        self.cstf = sb(gst, "cstf", [128, 768], F32)
        self.cstb = Buf()
        self.cstbf = sb(gst, "cstbf", [128, 768], BF16)
        self.cstbb = self.cstb
        self.dma("sp", self.cstf[:], self.cst_d, w=[self.cstb])
        self.cp("dve", self.cstbf[:], self.cstf[:], r=[self.cstb], w=[self.cstb])
        c = self.cstf
        self.IDF, self.U, self.L1, self.J, self.CM, self.ONES = (c[:, 0:128], c[:, 128:256], c[:, 256:384],
                                                                 c[:, 384:512], c[:, 512:640], c[:, 640:768])
        cbf = self.cstbf
        self.IDB, self.UB, self.ONESB = cbf[:, 0:128], cbf[:, 128:256], cbf[:, 640:768]
        self.vec = sb(gst, "vec", [128, 512], F32)
        self.vecb = Buf()
        v = self.vec
        self.modT = v[:, 0:96]
        self.n1T, self.n2T = v[:, 96:112], v[:, 112:128]
        self.s1, self.s2 = v[:, 128:144], v[:, 144:160]
        self.cqgT = v[:, 160:164]
        self.convbT = v[:, 164:212]
        self.convwT = v[:, 212:404]
        self.cT = v[:, 404:420]
        self.badaT = v[:, 420:516] if False else None
        self.flg = sb(gst, "flgt", [128, 2], F32)
        self.flgb = Buf()
        self.dma("sp", self.flg[:], self.flg_d, w=[self.flgb])
        self.bc = sb(gst, "bct", [128, 256 + 64 * 5], F32)
        self.bcb = Buf()
        b = self.bc
        self.ckvg_bc, self.kig_bc, self.kib_bc = b[:, 0:256], b[:, 256:320], b[:, 320:384]
        self.dtb_bc, self.a_bc, self.dsk_bc = b[:, 384:448], b[:, 448:512], b[:, 512:576]
        for ap, src in ((self.ckvg_bc, self.ckvg_d), (self.kig_bc, self.kig_d), (self.kib_bc, self.kib_d),
                        (self.dtb_bc, self.dtb_d), (self.a_bc, self.alog_d), (self.dsk_bc, self.dsk_d)):
            self.dma("sp", ap, src.partition_broadcast(128), w=[self.bcb])
        self.act(self.a_bc, self.a_bc, AF.Exp, r=[self.bcb], w=[self.bcb])
        self.ts("dve", self.a_bc, self.a_bc, -1.0, None, ALU.mult, r=[self.bcb], w=[self.bcb])
        self.c_actT = sb(gst, "c_actT", [128, 16], BF16)
        self.halo = sb(gst, "halo", [128, 48, 3], F32)
        self.halob = Buf()

        with self.scope() as st:
            rot = self.psr([0, 1, 2, 3])
            badaT = sb(st, "badaT", [128, 96], F32)
            bb = Buf()
            self.load_cols(st, self.b_ada, 96, badaT[:], bb, "bada", rot)
            self.load_cols(st, self.n1_d, 16, self.n1T, self.vecb, "n1", rot)
            self.load_cols(st, self.n2_d, 16, self.n2T, self.vecb, "n2", rot)
            self.load_cols(st, self.cqg_d, 4, self.cqgT, self.vecb, "cqg", rot)
            self.load_cols(st, self.convb_d, 48, self.convbT, self.vecb, "cvb", rot)
            self.load_cols(st, self.convw_d[0:96, :], 96, self.convwT[:, 0:96], self.vecb, "cvw0", rot)
            self.load_cols(st, self.convw_d[96:192, :], 96, self.convwT[:, 96:192], self.vecb, "cvw1", rot)
            self.load_cols(st, self.cb_d, 16, self.cT, self.vecb, "cb", rot)
            self.act(self.c_actT[:], self.cT, AF.Silu, r=[self.vecb], w=[self.vecb])
            wa = [sb(st, "wa%d" % i, [128, 16, 1536], BF16) for i in range(2)]
            wab = [Buf(), Buf()]
            wsrc = self.w_ada.rearrange("(kc p) n -> p kc n", p=128)
            psm, psmb = self.ps[4], self.psb[4]
            self.dma("pool", wa[0][:], wsrc[:, :, 0:1536], w=[wab[0]])
            for fgp in range(8):
                if fgp + 1 < 8:
                    self.dma("pool", wa[(fgp + 1) % 2][:], wsrc[:, :, (fgp + 1) * 1536:(fgp + 2) * 1536],
                             w=[wab[(fgp + 1) % 2]])
                wt, wb = wa[fgp % 2], wab[fgp % 2]
                for fc in range(12):
                    col = fgp * 12 + fc
                    for kc in range(16):
                        self.mm(psm[:, col:col + 1], wt[:, kc, fc * 128:(fc + 1) * 128], self.c_actT[:, kc:kc + 1],
                                start=(kc == 0), stop=(kc == 15), r=[wb, self.vecb], w=[psmb])
            self.tt("dve", self.modT, psm[:, 0:96], badaT[:], ALU.add, r=[psmb, bb], w=[self.vecb])
            self.stt(self.s1, self.modT[:, 16:32], 1.0, self.n1T, ALU.add, ALU.mult, r=[self.vecb], w=[self.vecb])
            self.stt(self.s2, self.modT[:, 64:80], 1.0, self.n2T, ALU.add, ALU.mult, r=[self.vecb], w=[self.vecb])
            self.sh1, self.sh2 = self.modT[:, 0:16], self.modT[:, 48:64]
            ps, pb = rot.next()
            self.tr(ps[0:96, 0:128], self.modT, self.IDF, r=[self.vecb, self.cstb], w=[pb])
            modr = sb(st, "modr", [96, 128], F32)
            mrb = Buf()
            self.cp("dve", modr[:], ps[0:96, 0:128], r=[pb], w=[mrb])
            self.dma("sp", self.modv.ap(), modr[:], r=[mrb], w=[self.scr["modv"]])
            self.P.barrier()

    def norm_transpose(self, st, x_dram, hT, hTb, s_cols, sh_cols, tag, src_tile=None):
        NT = self.NT
        xb_t = [self.sb(st, "xt%s%d" % (tag, i), [128, 2048], F32) for i in range(2)]
        xbb = [Buf(), Buf()]
        xn_t = [self.sb(st, "xn%s%d" % (tag, i), [128, 2048], BF16) for i in range(2)]
        xnb = [Buf(), Buf()]
        ss = self.sb(st, "ss" + tag, [128, 3 * NT], F32)
        ssb = [Buf() for _ in range(NT)]
        rot = self.psr([0, 1, 2, 3])
        import os as _os
        lvl = int(_os.environ.get("KNT", "9"))
        for tc in range(NT):
            xt, xtb = xb_t[tc % 2], xbb[tc % 2]
            xn, xnbb = xn_t[tc % 2], xnb[tc % 2]
            self.dma("sp", xt[:], x_dram[tc * 128:(tc + 1) * 128, :], w=[xtb])
            if lvl < 1:
                continue
            self.act(xn[:], xt[:], AF.Square, accum=ss[:, 3 * tc:3 * tc + 1], r=[xtb], w=[xnbb, ssb[tc]])
            self.rstd(ss[:, 3 * tc + 1:3 * tc + 2], ss[:, 3 * tc:3 * tc + 1], 2048.0, ss[:, 3 * tc + 2:3 * tc + 3],
                      [ssb[tc]], [ssb[tc]])
            if lvl < 2:
                continue
            self.act(xn[:], xt[:], AF.Copy, scale=ss[:, 3 * tc + 1:3 * tc + 2], r=[xtb, ssb[tc]], w=[xnbb])
            if lvl < 3:
                continue
            for half in range(2):
                ps, pb = rot.next()
                psv = ps[:].bitcast(BF16)
                for j in range(8):
                    fc = half * 8 + j
                    self.tr(psv[:, j * 128:(j + 1) * 128], xn[:, fc * 128:(fc + 1) * 128], self.IDB,
                            r=[xnbb, self.cstb], w=[pb])
                if lvl < 4:
                    continue
                for j in range(8):
                    fc = half * 8 + j
                    dst = hT[:, fc, tc * 128:(tc + 1) * 128]
                    if half == 0:
                        self.act(dst, psv[:, j * 128:(j + 1) * 128], AF.Identity, bias=sh_cols[:, fc:fc + 1],
                                 scale=s_cols[:, fc:fc + 1], r=[pb, self.vecb], w=[hTb[tc][fc]])
                    else:
                        self.ts("dve", dst, psv[:, j * 128:(j + 1) * 128], s_cols[:, fc:fc + 1], sh_cols[:, fc:fc + 1],
                                ALU.mult, ALU.add, r=[pb, self.vecb], w=[hTb[tc][fc]])

    def load_w(self, dst, w_dram, c0, W, buf, k0=0, KC=None):
        src = w_dram.rearrange("(kc p) n -> p kc n", p=128)
        if KC is None:
            self.dma("pool", dst, src[:, :, c0:c0 + W], w=[buf])
        else:
            self.dma("pool", dst, src[:, k0:k0 + KC, c0:c0 + W], w=[buf])

    def hT_reads(self, hTb, kc, tcs):
        return [hTb[tc][kc] for tc in tcs]

    def phase_inproj(self, gst):
        NT, T, NTG = self.NT, self.T, self.NTG
        sb = self.sb
        self.dtp = {}
        for name in ("de_p", "dec_p", "dt_o", "adt_o", "ea_o", "de_o", "dec_o"):
            self.dtp[name] = sb(gst, name, [128, NT, 64], F32)
        self.dtpb = {name: [Buf() for _ in range(NT)] for name in self.dtp}
        with self.scope() as st:
            hT = sb(st, "hT", [128, 16, T], BF16)
            hTb = [[Buf() for _ in range(16)] for _ in range(NT)]
            wt = [sb(st, "wblk%d" % i, [128, 16, 512], BF16) for i in range(2)]
            wtb = [Buf(), Buf()]
            for prefix in (True, False):
              with self.scope() as stn:
                self.norm_transpose(stn, self.xp if prefix else self.xo, hT, hTb, self.s1, self.sh1,
                                    "p" if prefix else "o")
                self.P.barrier()
              with self.scope() as ste:
                self.ip_tiles(ste)
                blocks = []
                if not prefix:
                    blocks.append(("tm0", [(OFF_CQ, 512, 0)]))
                blocks.append(("tm1", [(OFF_CKV, 336, 0), (OFF_DT, 64, 336)]))
                if not prefix:
                    for g in range(8):
                        blocks.append(("z", [(OFF_Z + g * 512, 512, 0)], g))
                for g in range(8):
                    blocks.append(("x", [(OFF_X + g * 512, 512, 0)], g))
                for g2 in range(2):
                    blocks.append(("B", [(OFF_B + g2 * 512, 512, 0)], g2))
                for g2 in range(2):
                    blocks.append(("C", [(OFF_C + g2 * 512, 512, 0)], g2))
                if not prefix:
                    for g in range(8):
                        blocks.append(("gate", [(OFF_GA + g * 512, 512, 0)], g))

                import os as _os
                _lim = _os.environ.get("KLIMIT")
                if _lim is not None:
                    blocks = [b_ for b_ in blocks if b_[0] in _lim.split(",")]
                if not blocks:
                    self.P.barrier()
                    continue

                def issue(i):
                    for (c0, W, d0) in blocks[i][1]:
                        self.load_w(wt[i % 2][:, :, d0:d0 + W], self.w_in, c0, W, wtb[i % 2])
                issue(0)
                for i, blk in enumerate(blocks):
                    if i + 1 < len(blocks):
                        issue(i + 1)
                    w_t, w_b = wt[i % 2], wtb[i % 2]
                    kind = blk[0]
                    if kind == "tm0":
                        self.ep_tm0(hT, hTb, w_t, w_b)
                    elif kind == "tm1":
                        self.ep_tm1(hT, hTb, w_t, w_b, prefix)
                    elif kind == "z":
                        self.ep_z(hT, hTb, w_t, w_b, blk[2])
                    elif kind == "x":
                        self.ep_conv(hT, hTb, w_t, w_b, prefix, "x", blk[2])
                    elif kind == "B":
                        self.ep_conv(hT, hTb, w_t, w_b, prefix, "B", blk[2])
                    elif kind == "C":
                        self.ep_conv(hT, hTb, w_t, w_b, prefix, "C", blk[2])
                    elif kind == "gate":
                        self.ep_gate(hT, hTb, w_t, w_b, blk[2])
                self.P.barrier()

    def ip_tiles(self, st):
        sb = self.sb
        T, NT = self.T, self.NT
        self.pre = sb(st, "pre", [128, T + 3], F32)
        self.preb = Buf()
        self.acc = sb(st, "cacc", [128, T], F32)
        self.accb = Buf()
        self.cvs = [sb(st, "cv%d" % i, [128, T], BF16) for i in range(4)]
        self.cvbs = [Buf() for _ in range(4)]
        self.xstage = sb(st, "xstage", [128, NT, 512], BF16)
        self.xstb = Buf()
        self.st512 = [sb(st, "st512_%d" % i, [128, 512], F32) for i in range(3)]
        self.st512b = [Buf() for _ in range(3)]
        self.st512r = Rot(list(zip(self.st512, self.st512b)))
        self.zst = [sb(st, "zst%d" % i, [128, 512], BF16) for i in range(3)]
        self.zstb = [Buf() for _ in range(3)]
        self.zstr = Rot(list(zip(self.zst, self.zstb)))
        self.sm = [sb(st, "smf%d" % i, [128, 512], F32) for i in range(2)]
        self.smb = [Buf(), Buf()]
        self.smh = [sb(st, "smh%d" % i, [128, 1024], BF16) for i in range(2)]
        self.smhb = [Buf(), Buf()]
        self.memset("dve", self.pre[:, 0:3], 0.0, w=[self.preb])

    def ep_tm0(self, hT, hTb, wt, wb):
        NT = self.NT
        rot = self.psr([0, 1, 2, 3])
        cqv = self.cqT_d.ap().rearrange("(fc p) t -> p fc t", p=128)
        for tc in range(NT):
            ps, pb = rot.next()
            for kc in range(16):
                self.mm(ps[:, 0:512], hT[:, kc, tc * 128:(tc + 1) * 128], wt[:, kc, 0:512], start=(kc == 0),
                        stop=(kc == 15), r=[hTb[tc][kc], wb], w=[pb])
            sm, smb = self.sm[tc % 2], self.smb[tc % 2]
            sh, shb = self.smh[tc % 2], self.smhb[tc % 2]
            self.act(sh[:, 0:512], ps[:, 0:512], AF.Square, accum=sm[:, 0:1], r=[pb], w=[shb, smb])
            self.rstd(sm[:, 1:2], sm[:, 0:1], 512.0, sm[:, 2:3], [smb], [smb])
            self.act(sh[:, 0:512], ps[:, 0:512], AF.Copy, scale=sm[:, 1:2], r=[pb, smb], w=[shb])
            ps2, pb2 = rot.next()
            p2v = ps2[:].bitcast(BF16)
            for fc in range(4):
                self.tr(p2v[:, fc * 128:(fc + 1) * 128], sh[:, fc * 128:(fc + 1) * 128], self.IDB,
                        r=[shb, self.cstb], w=[pb2])
            for fc in range(4):
                self.ts("dve", sh[:, 512 + fc * 128:512 + (fc + 1) * 128], p2v[:, fc * 128:(fc + 1) * 128],
                        self.cqgT[:, fc:fc + 1], None, ALU.mult, r=[pb2, self.vecb], w=[shb])
            self.dma("sp", cqv[:, :, tc * 128:(tc + 1) * 128],
                     sh[:, 512:1024].rearrange("p (fc t) -> p fc t", fc=4), r=[shb], w=[Buf()])

    def ep_tm1(self, hT, hTb, wt, wb, prefix):
        NT, T = self.NT, self.T
        rot = self.psr([0, 1, 2, 3])
        key0 = 0 if prefix else T
        ckvTv = self.ckvT_d.ap().rearrange("(rc p) s -> p rc s", p=128)
        D = self.dtp
        DB = self.dtpb
        for tc in range(NT):
            ps, pb = rot.next()
            for kc in range(16):
                self.mm(ps[:, 0:400], hT[:, kc, tc * 128:(tc + 1) * 128], wt[:, kc, 0:400], start=(kc == 0),
                        stop=(kc == 15), r=[hTb[tc][kc], wb], w=[pb])
            sm, smb = self.sm[tc % 2], self.smb[tc % 2]
            sh, shb = self.smh[tc % 2], self.smhb[tc % 2]
            kpos = key0 + tc * 128
            self.act(sh[:, 0:256], ps[:, 0:256], AF.Square, accum=sm[:, 0:1], r=[pb], w=[shb, smb])
            self.rstd(sm[:, 1:2], sm[:, 0:1], 256.0, sm[:, 2:3], [smb], [smb])
            self.stt(sh[:, 0:256], ps[:, 0:256], sm[:, 1:2], self.ckvg_bc, ALU.mult, ALU.mult,
                     r=[pb, smb, self.bcb], w=[shb])
            self.dma("sp", self.ckvtok_d.ap()[kpos:kpos + 128, :], sh[:, 0:256], r=[shb], w=[Buf()])
            ps2, pb2 = rot.next()
            p2v = ps2[:].bitcast(BF16)
            for rc in range(2):
                self.tr(p2v[:, rc * 128:(rc + 1) * 128], sh[:, rc * 128:(rc + 1) * 128], self.IDB,
                        r=[shb, self.cstb], w=[pb2])
            self.act(sm[:, 64:128], ps[:, 256:320], AF.Identity, accum=sm[:, 3:4], r=[pb], w=[smb])
            self.act(sm[:, 64:128], ps[:, 256:320], AF.Square, accum=sm[:, 4:5], r=[pb], w=[smb])
            self.ts("dve", sm[:, 5:6], sm[:, 3:4], 1.0 / 64, None, ALU.mult, r=[smb], w=[smb])
            self.tt("dve", sm[:, 6:7], sm[:, 5:6], sm[:, 5:6], ALU.mult, r=[smb], w=[smb])
            self.stt(sm[:, 7:8], sm[:, 4:5], 1.0 / 64, sm[:, 6:7], ALU.mult, ALU.subtract, r=[smb], w=[smb])
            self.ts("dve", sm[:, 8:9], sm[:, 7:8], EPS, None, ALU.add, r=[smb], w=[smb])
            self.act(sm[:, 8:9], sm[:, 8:9], AF.Sqrt, r=[smb], w=[smb])
            self.recip(sm[:, 9:10], sm[:, 8:9], r=[smb], w=[smb])
            self.ts("dve", sm[:, 64:128], ps[:, 256:320], sm[:, 5:6], sm[:, 9:10], ALU.subtract, ALU.mult,
                    r=[pb, smb], w=[smb])
            self.tt("dve", sm[:, 64:128], sm[:, 64:128], self.kig_bc, ALU.mult, r=[smb, self.bcb], w=[smb])
            self.tt("dve", sh[:, 256:320], sm[:, 64:128], self.kib_bc, ALU.add, r=[smb, self.bcb], w=[shb])
            self.cp("dve", sh[:, 320:384], sh[:, 256:320], r=[shb], w=[shb])
            self.tr(p2v[:, 256:384], sh[:, 256:384], self.IDB, r=[shb, self.cstb], w=[pb2])
            self.cp("act", sh[:, 512:896], p2v[:, 0:384], r=[pb2], w=[shb])
            self.dma("sp", ckvTv[:, :, kpos:kpos + 128], sh[:, 512:768].rearrange("p (rc s) -> p rc s", rc=2),
                     r=[shb], w=[Buf()])
            self.dma("sp", self.kidxT_d.ap()[:, kpos:kpos + 128], sh[:, 768:896], r=[shb], w=[Buf()])
            if not prefix:
                self.ts("dve", sm[:, 16:32], ps[:, 320:336], 1.0 / 32.0, None, ALU.mult, r=[pb], w=[smb])
                self.dma("sp", self.widx_d.ap()[tc * 128:(tc + 1) * 128, :], sm[:, 16:32], r=[smb],
                         w=[Buf()])
            dt_t = sm[:, 128:192]
            self.tt("dve", dt_t, ps[:, 336:400], self.dtb_bc, ALU.add, r=[pb, self.bcb], w=[smb])
            self.act(dt_t, dt_t, AF.Exp, r=[smb], w=[smb])
            self.act(dt_t, dt_t, AF.Ln, bias=1.0, r=[smb], w=[smb])
            adt = sm[:, 192:256]
            self.tt("dve", adt, dt_t, self.a_bc, ALU.mult, r=[smb, self.bcb], w=[smb])
            ps3, pb3 = rot.next()
            self.mm(ps3[:, 0:64], self.U, adt, r=[self.cstb, smb], w=[pb3])
            self.mm(ps3[:, 64:128], self.ONES, adt, r=[self.cstb, smb], w=[pb3])
            acs = sm[:, 256:320]
            self.cp("act", acs, ps3[:, 0:64], r=[pb3], w=[smb])
            pn = "p" if prefix else "o"
            self.act(D["dec_" + pn][:, tc, :], ps3[:, 64:128], AF.Exp, r=[pb3], w=[DB["dec_" + pn][tc]])
            self.tt("dve", sm[:, 320:384], ps3[:, 64:128], acs, ALU.subtract, r=[pb3, smb], w=[smb])
            self.act(sm[:, 320:384], sm[:, 320:384], AF.Exp, r=[smb], w=[smb])
            self.tt("dve", D["de_" + pn][:, tc, :], sm[:, 320:384], dt_t, ALU.mult, r=[smb], w=[DB["de_" + pn][tc]])
            if not prefix:
                self.cp("dve", D["dt_o"][:, tc, :], dt_t, r=[smb], w=[DB["dt_o"][tc]])
                self.cp("dve", D["adt_o"][:, tc, :], adt, r=[smb], w=[DB["adt_o"][tc]])
                self.act(D["ea_o"][:, tc, :], acs, AF.Exp, r=[smb], w=[DB["ea_o"][tc]])

    def ep_z(self, hT, hTb, wt, wb, g):
        rot = self.psr([0, 1, 2, 3])
        for tc in range(self.NT):
            ps, pb = rot.next()
            for kc in range(16):
                self.mm(ps[:, 0:512], hT[:, kc, tc * 128:(tc + 1) * 128], wt[:, kc, 0:512], start=(kc == 0),
                        stop=(kc == 15), r=[hTb[tc][kc], wb], w=[pb])
            zt, ztb = self.zstr.next()
            self.act(zt[:], ps[:, 0:512], AF.Silu, r=[pb], w=[ztb])
            self.dma("sp", self.z_d.ap()[tc * 128:(tc + 1) * 128, g * 512:(g + 1) * 512], zt[:], r=[ztb],
                     w=[Buf()])

    def ep_gate(self, hT, hTb, wt, wb, g):
        rot = self.psr([0, 1, 2, 3])
        for mc in range(4):
            for tg in range(self.NTG):
                ps, pb = rot.next()
                tcs = range(tg * 4, tg * 4 + 4)
                for kc in range(16):
                    self.mm(ps[:, 0:512], wt[:, kc, mc * 128:(mc + 1) * 128], hT[:, kc, tg * 512:(tg + 1) * 512],
                            start=(kc == 0), stop=(kc == 15), r=[wb] + self.hT_reads(hTb, kc, tcs), w=[pb])
                gt, gtb = self.st512r.next()
                self.act(gt[:], ps[:, 0:512], AF.Sigmoid, r=[pb], w=[gtb])
                row = (g * 4 + mc) * 128
                self.dma("sp", self.gT_d.ap()[row:row + 128, tg * 512:(tg + 1) * 512], gt[:], r=[gtb],
                         w=[Buf()])

    def ep_conv(self, hT, hTb, wt, wb, prefix, kind, g):
        NT, T, NTG = self.NT, self.T, self.NTG
        rot = self.psr([0, 1, 2, 3, 4, 5, 6, 7])
        row0 = 0 if prefix else T
        for cc in range(4):
            cv, cvb = self.cvs[cc], self.cvbs[cc]
            if kind == "x":
                chunk = g * 4 + cc
            elif kind == "B":
                chunk = 32 + g * 4 + cc
            else:
                chunk = 40 + g * 4 + cc
            hidx = chunk
            tgs = range(NTG)
            if kind == "C" and prefix:
                tgs = [NTG - 1]
            if not prefix:
                self.cp("dve", self.pre[:, 0:3], self.halo[:, hidx, :], r=[self.halob], w=[self.preb])
            for tg in tgs:
                ps, pb = rot.next()
                tcs = range(tg * 4, tg * 4 + 4)
                for kc in range(16):
                    self.mm(ps[:, 0:512], wt[:, kc, cc * 128:(cc + 1) * 128], hT[:, kc, tg * 512:(tg + 1) * 512],
                            start=(kc == 0), stop=(kc == 15), r=[wb] + self.hT_reads(hTb, kc, tcs), w=[pb])
                self.cp("act", self.pre[:, 3 + tg * 512:3 + (tg + 1) * 512], ps[:, 0:512], r=[pb], w=[self.preb])
            if prefix:
                self.ts("dve", self.halo[:, hidx, :], self.pre[:, T:T + 3], self.flg[:, 0:1], None, ALU.mult,
                        r=[self.preb, self.flgb], w=[self.halob])
                if kind == "C":
                    continue
            cw = self.convwT
            self.ts("dve", self.acc[:], self.pre[:, 0:T], cw[:, chunk:chunk + 1], None, ALU.mult,
                    r=[self.preb, self.vecb], w=[self.accb])
            for k in range(1, 4):
                self.stt(self.acc[:], self.pre[:, k:k + T], cw[:, k * 48 + chunk:k * 48 + chunk + 1], self.acc[:],
                         ALU.mult, ALU.add, r=[self.preb, self.vecb, self.accb], w=[self.accb])
            self.act(cv[:], self.acc[:], AF.Silu, bias=self.convbT[:, chunk:chunk + 1], r=[self.accb, self.vecb],
                     w=[cvb])
            gc = g * 4 + cc
            if kind == "C":
                self.dma("sp", self.CT_d.ap()[gc * 128:(gc + 1) * 128, :], cv[:], r=[cvb], w=[Buf()])
            elif kind == "B" and not prefix:
                self.dma("sp", self.BT_d.ap()[gc * 128:(gc + 1) * 128, :], cv[:], r=[cvb], w=[Buf()])
        if kind == "C":
            return
        for cc in range(4):
            cv, cvb = self.cvs[cc], self.cvbs[cc]
            for t4 in range(NT // 4):
                ps, pb = rot.next()
                pv = ps[:].bitcast(BF16)
                for q in range(4):
                    tc = t4 * 4 + q
                    self.tr(pv[:, q * 128:(q + 1) * 128], cv[:, tc * 128:(tc + 1) * 128], self.IDB,
                            r=[cvb, self.cstb], w=[pb])
                self.cp("dve" if t4 % 2 == 0 else "act", self.xstage[:, t4 * 4:(t4 + 1) * 4, cc * 128:(cc + 1) * 128],
                        pv[:, 0:512].rearrange("p (q c) -> p q c", q=4), r=[pb], w=[self.xstb])
        if kind == "x":
            dst = self.x_d.ap()[row0:row0 + T, g * 512:(g + 1) * 512].rearrange("(tc p) c -> p tc c", p=128)
            self.dma("sp", dst, self.xstage[:], r=[self.xstb], w=[Buf()])
        else:
            dst = self.Btok_d.ap()[row0:row0 + T, g * 512:(g + 1) * 512].rearrange("(tc p) c -> p tc c", p=128)
            self.dma("sp", dst, self.xstage[:], r=[self.xstb], w=[Buf()])

    def phase_ssd(self):
        NT, T = self.NT, self.T
        sb = self.sb
        D, DB = self.dtp, self.dtpb
        with self.scope() as st:
            self.stateT = sb(st, "stateT", [128, 4096], F32)
            self.stateb = [Buf() for _ in range(8)]
            xg = [sb(st, "xg%d" % i, [128, NT, 512], BF16) for i in range(2)]
            bt = [sb(st, "btok%d" % i, [128, NT, 128], BF16) for i in range(2)]
            zg = [sb(st, "zg%d" % i, [128, NT, 512], BF16) for i in range(2)]
            BgT = [sb(st, "BgT%d" % i, [128, T], BF16) for i in range(2)]
            CgT = [sb(st, "CgT%d" % i, [128, T], BF16) for i in range(2)]
            gsm = [sb(st, "gsm%d" % i, [128, 512], F32) for i in range(2)]
            gb = [Buf(), Buf()]
            ybst = sb(st, "ybst", [128, 4, T], BF16)
            ybb = Buf()
            def dbl(name, shape, dt):
                return [sb(st, name + str(i), shape, dt) for i in range(2)], [Buf(), Buf()]
            rseg, rsegb = dbl("rseg", [128, 8, 128], F32)
            Eb, Ebb = dbl("Eb", [128, 8, 128], BF16)
            cbm, cbmb = dbl("cbm", [128, 128], BF16)
            WT, WTb = dbl("WT", [128, 8, 128], BF16)
            xdt, xdtb = dbl("xdt", [128, 512], BF16)
            xw, xwb = dbl("xw", [128, 512], BF16)
            stb_t = sb(st, "stbf", [128, 512], BF16)
            stbb = Buf()
            y1s, y1bs = dbl("y1", [128, 512], F32)
            y2s, y2bs = dbl("y2", [128, 512], F32)
            y5s, y5bs = dbl("y5", [128, 512], BF16)
            ysms, ysmbs = dbl("ysm", [128, 8], F32)
            junks, junkbs = dbl("yjunk", [128, 512], BF16)
            rot = self.psr([0, 1, 2, 3, 4, 5, 6, 7])
            ybv = self.ybT_d.ap().rearrange("(cc p) t -> p cc t", p=128)

            for prefix in (True, False):
                row0 = 0 if prefix else T
                pn = "p" if prefix else "o"

                def load_group(g):
                    i = g % 2
                    self.dma("sp", xg[i][:], self.x_d.ap()[row0:row0 + T, g * 512:(g + 1) * 512]
                             .rearrange("(tc p) c -> p tc c", p=128), r=[self.scr["x0" if prefix else "x1"]], w=[gb[i]])
                    self.dma("sp", bt[i][:], self.Btok_d.ap()[row0:row0 + T, g * 128:(g + 1) * 128]
                             .rearrange("(tc p) c -> p tc c", p=128), r=[self.scr["Btok0" if prefix else "Btok1"]],
                             w=[gb[i]])
                    if not prefix:
                        self.dma("sp", zg[i][:], self.z_d.ap()[:, g * 512:(g + 1) * 512]
                                 .rearrange("(tc p) c -> p tc c", p=128), r=[self.scr["z"]], w=[gb[i]])
                        self.dma("sp", BgT[i][:], self.BT_d.ap()[g * 128:(g + 1) * 128, :], r=[self.scr["BT"]], w=[gb[i]])
                        self.dma("sp", CgT[i][:], self.CT_d.ap()[g * 128:(g + 1) * 128, :], r=[self.scr["CT"]], w=[gb[i]])
                        self.dma("sp", gsm[i][:], self.ssmg_d[:, g * 512:(g + 1) * 512].partition_broadcast(128),
                                 w=[gb[i]])
                load_group(0)
                for g in range(8):
                    if g + 1 < 8:
                        load_group(g + 1)
                    i = g % 2
                    G = gb[i]
                    stg = self.stateT[:, g * 512:(g + 1) * 512]
                    sbuf = self.stateb[g]
                    hs = slice(g * 8, (g + 1) * 8)
                    if prefix:
                        self.memset("dve", stg, 0.0, w=[sbuf])
                    else:
                        self.ts("dve", stg, stg, self.flg[:, 0:1], None, ALU.mult, r=[sbuf, self.flgb], w=[sbuf])
                        self.cp("act", stb_t[:], stg, r=[sbuf], w=[stbb])

                    def stage1(c):
                        k = c % 2
                        xc = xg[i][:, c, :]
                        self.tt("pool", xw[k][:].rearrange("p (h q) -> p h q", h=8),
                                xc.rearrange("p (h q) -> p h q", h=8),
                                D["de_" + pn][:, c, hs].unsqueeze(2).broadcast_to([128, 8, 64]), ALU.mult,
                                r=[G, DB["de_" + pn][c]], w=[xwb[k]])
                        if prefix:
                            return
                        adt = D["adt_o"][:, c, hs]
                        self.tt("pool", rseg[k][:], self.U.unsqueeze(1).broadcast_to([128, 8, 128]),
                                adt.unsqueeze(2).broadcast_to([128, 8, 128]), ALU.mult,
                                r=[self.cstb, DB["adt_o"][c]], w=[rsegb[k]])
                        for hh in range(2):
                            psS, psSb = rot.next()
                            self.mm(psS[:, 0:512], self.L1,
                                    rseg[k][:, hh * 4:(hh + 1) * 4, :].rearrange("p h t -> p (h t)"),
                                    r=[self.cstb, rsegb[k]], w=[psSb])
                            self.act(Eb[k][:, hh * 4:(hh + 1) * 4, :].rearrange("p h t -> p (h t)"), psS[:, 0:512],
                                     AF.Exp, r=[psSb], w=[Ebb[k]])
                        psC, psCb = rot.next()
                        self.mm(psC[:, 0:128], BgT[i][:, c * 128:(c + 1) * 128], CgT[i][:, c * 128:(c + 1) * 128],
                                r=[G], w=[psCb])
                        self.tt("dve", cbm[k][:], psC[:, 0:128], self.U, ALU.mult, r=[psCb, self.cstb], w=[cbmb[k]])
                        self.tt("pool", WT[k][:], Eb[k][:], cbm[k][:].unsqueeze(1).broadcast_to([128, 8, 128]), ALU.mult,
                                r=[Ebb[k], cbmb[k]], w=[WTb[k]])
                        self.tt("dve", xdt[k][:].rearrange("p (h q) -> p h q", h=8),
                                xc.rearrange("p (h q) -> p h q", h=8),
                                D["dt_o"][:, c, hs].unsqueeze(2).broadcast_to([128, 8, 64]), ALU.mult,
                                r=[G, DB["dt_o"][c]], w=[xdtb[k]])

                    def stage2(c):
                        k = c % 2
                        xc = xg[i][:, c, :]
                        y1, y1b, y2, y2b, y5, y5b = y1s[k], y1bs[k], y2s[k], y2bs[k], y5s[k], y5bs[k]
                        ysm, ysmb, junk, junkb = ysms[k], ysmbs[k], junks[k], junkbs[k]
                        if not prefix:
                            psY, psYb = rot.next()
                            for h in range(8):
                                self.mm(psY[:, h * 64:(h + 1) * 64], WT[k][:, h, :], xdt[k][:, h * 64:(h + 1) * 64],
                                        r=[WTb[k], xdtb[k]], w=[psYb])
                            psI, psIb = rot.next()
                            self.mm(psI[:, 0:512], CgT[i][:, c * 128:(c + 1) * 128], stb_t[:], r=[G, stbb], w=[psIb])
                        psN, psNb = rot.next()
                        self.mm(psN[:, 0:512], bt[i][:, c, :], xw[k][:], r=[G, xwb[k]], w=[psNb])
                        self.tt("dve", stg.rearrange("p (h q) -> p h q", h=8), stg.rearrange("p (h q) -> p h q", h=8),
                                D["dec_" + pn][:, c, hs].unsqueeze(2).broadcast_to([128, 8, 64]), ALU.mult,
                                r=[sbuf, DB["dec_" + pn][c]], w=[sbuf])
                        self.tt("dve", stg, stg, psN[:, 0:512], ALU.add, r=[sbuf, psNb], w=[sbuf])
                        if prefix:
                            return
                        self.tt("dve", y1[:].rearrange("p (h q) -> p h q", h=8),
                                psI[:, 0:512].rearrange("p (h q) -> p h q", h=8),
                                D["ea_o"][:, c, hs].unsqueeze(2).broadcast_to([128, 8, 64]), ALU.mult,
                                r=[psIb, DB["ea_o"][c]], w=[y1b])
                        if c + 1 < NT:
                            self.cp("act", stb_t[:], stg, r=[sbuf], w=[stbb])
                        self.tt("dve", y1[:], psY[:, 0:512], y1[:], ALU.add, r=[psYb, y1b], w=[y1b])
                        self.tt("pool", y2[:].rearrange("p (h q) -> p h q", h=8),
                                xc.rearrange("p (h q) -> p h q", h=8),
                                self.dsk_bc[:, hs].unsqueeze(2).broadcast_to([128, 8, 64]), ALU.mult,
                                r=[G, self.bcb], w=[y2b])
                        self.tt("dve", y1[:], y1[:], y2[:], ALU.add, r=[y1b, y2b], w=[y1b])
                        self.tt("dve", y1[:], y1[:], zg[i][:, c, :], ALU.mult, r=[y1b, G], w=[y1b])
                        self.act(junk[:], y1[:], AF.Square, accum=ysm[:, 0:1], r=[y1b], w=[junkb, ysmb])
                        self.rstd(ysm[:, 1:2], ysm[:, 0:1], 512.0, ysm[:, 2:3], [ysmb], [ysmb])
                        self.stt(y5[:], y1[:], ysm[:, 1:2], gsm[i][:], ALU.mult, ALU.mult, r=[y1b, ysmb, G], w=[y5b])
                        psT, psTb = rot.next()
                        ptv = psT[:].bitcast(BF16)
                        for cc in range(4):
                            self.tr(ptv[:, cc * 128:(cc + 1) * 128], y5[:, cc * 128:(cc + 1) * 128], self.IDB,
                                    r=[y5b, self.cstb], w=[psTb])
                        self.cp("act", ybst[:, :, c * 128:(c + 1) * 128],
                                ptv[:, 0:512].rearrange("p (cc t) -> p cc t", cc=4), r=[psTb], w=[ybb])

                    stage1(0)
                    for c in range(NT):
                        if c + 1 < NT:
                            stage1(c + 1)
                        stage2(c)
                    if not prefix:
                        self.dma("sp", ybv[:, g * 4:(g + 1) * 4, :], ybst[:], r=[ybb], w=[Buf()])
            self.P.barrier()

    def phase_attn(self):
        NT, T, NTG = self.NT, self.T, self.NTG
        sb = self.sb
        KB = 2 * NT
        with self.scope() as st:
            ckvT = sb(st, "ckvT", [128, 2, 2 * T], BF16)
            ckvtok = sb(st, "ckvtok", [128, KB, 256], BF16)
            kidxT = sb(st, "kidxT", [128, 2 * T], BF16)
            cqT_t = sb(st, "cqT", [128, 4, 512], BF16)
            cqb = Buf()
            widx = sb(st, "widx", [128, NT, 16], F32)
            ldb = Buf()
            self.dma("sp", ckvT[:], self.ckvT_d.ap().rearrange("(rc p) s -> p rc s", p=128), r=[self.scr["ckvT"]], w=[ldb])
            self.dma("sp", ckvtok[:], self.ckvtok_d.ap().rearrange("(kb p) r -> p kb r", p=128), r=[self.scr["ckvtok"]],
                     w=[ldb])
            self.dma("sp", kidxT[:], self.kidxT_d.ap(), r=[self.scr["kidxT"]], w=[ldb])
            self.dma("sp", widx[:], self.widx_d.ap().rearrange("(tc p) h -> p tc h", p=128), r=[self.scr["widx"]], w=[ldb])
            wuq = sb(st, "wuq", [128, 4, 2048], BF16)
            wiq = sb(st, "wiq", [128, 4, 1024], BF16)
            wuv = sb(st, "wuv", [128, 2, 2048], BF16)
            wukT = sb(st, "wukT", [128, 16, 256], BF16)
            biasT = sb(st, "biasT", [128, 16, 2, 128], BF16)
            wb_ = Buf()
            self.load_w(wuq[:], self.w_uq, 0, 2048, wb_)
            self.load_w(wiq[:], self.w_iq, 0, 1024, wb_)
            self.load_w(wuv[:], self.w_uv, 0, 2048, wb_)
            rot = self.psr([0, 1, 2, 3, 4])
            wukTb = Buf()
            biasb = Buf()
            with self.scope() as stmp:
                wuk = sb(stmp, "wuk", [128, 2, 2048], BF16)
                wkb = Buf()
                self.load_w(wuk[:], self.w_uk, 0, 2048, wkb)
                for h in range(16):
                    ps, pb = rot.next()
                    pv = ps[:].bitcast(BF16)
                    for rc in range(2):
                        self.tr(pv[:, rc * 128:(rc + 1) * 128], wuk[:, rc, h * 128:(h + 1) * 128], self.IDB,
                                r=[wkb, self.cstb], w=[pb])
                    self.cp("dve" if h % 2 else "act", wukT[:, h, :], pv[:, 0:256], r=[pb], w=[wukTb])
                relb = sb(stmp, "relb", [32, 16], F32)
                ohd = sb(stmp, "ohd", [32, 384], F32)
                tvs = sb(stmp, "tvs", [16, 384], F32)
                H = sb(stmp, "Hank", [128, 16, 2, 128], F32)
                bb = Buf()
                self.dma("sp", relb[:], self.relb_d, w=[bb])
                self.dma("sp", ohd[:], self.ohd_d, w=[bb])
                ps, pb = rot.next()
                self.mm(ps[0:16, 0:384], relb[:], ohd[:], r=[bb], w=[pb])
                tvb = Buf()
                self.cp("dve", tvs[:], ps[0:16, 0:384], r=[pb], w=[tvb])
                self.dma("sp", self.tv.ap(), tvs[:], r=[tvb], w=[self.scr["tv"]])
                hb = Buf()
                self.dma("sp", H[:], bass.AP(self.tv, 0, [[1, 128], [384, 16], [128, 2], [1, 128]]), r=[self.scr["tv"]],
                         w=[hb])
                Hf = H[:].rearrange("p h k t -> p (h k t)")
                Bf = biasT[:].rearrange("p h k t -> p (h k t)")
                for q in range(8):
                    ps, pb = rot.next()
                    self.mm(ps[:, 0:512], self.J, Hf[:, q * 512:(q + 1) * 512], r=[self.cstb, hb], w=[pb])
                    self.cp("dve" if q % 2 else "act", Bf[:, q * 512:(q + 1) * 512], ps[:, 0:512], r=[pb], w=[biasb])
                self.P.barrier()

            qiT = sb(st, "qiT", [128, 8, 512], BF16)
            qiTb = Buf()
            diagw = sb(st, "diagw", [128, 16, 128], BF16)
            diagb = Buf()
            Rh = [sb(st, "Rh%d" % i, [128, 512], BF16) for i in range(3)]
            Rhb = [Buf() for _ in range(3)]
            Rrot = Rot(list(zip(Rh, Rhb)))
            scores = [sb(st, "score%d" % i, [128, 2 * T], F32) for i in range(2)]
            scbs = [Buf(), Buf()]
            negm = sb(st, "negm", [128, 2 * T], BF16)
            negb = Buf()
            negT = sb(st, "negT", [128, KB, 512], BF16)
            negTb = Buf()
            bis2 = [sb(st, "bis%d" % i, [128, 8], F32) for i in range(2)]
            bisb2 = [Buf(), Buf()]
            junk1 = [sb(st, "junk1_%d" % i, [128, 2], BF16) for i in range(2)]
            junkb1 = [Buf(), Buf()]
            qT = sb(st, "qT", [128, 512], BF16)
            qTb = Buf()
            qlT = sb(st, "qlT", [128, 2, 512], BF16)
            qlTb = Buf()
            pT = [sb(st, "pT%d" % i, [128, 512], BF16) for i in range(4)]
            pTb = [Buf() for _ in range(4)]
            prot = Rot(list(zip(pT, pTb)))
            rec = sb(st, "rec", [128, 512], F32)
            recb = Buf()
            onT = sb(st, "onT", [128, 2, 512], BF16)
            onTb = Buf()
            yast = [sb(st, "yast%d" % i, [128, 512], BF16) for i in range(1)] * 2
            yastb = [Buf()] * 2
            psO0, psO0b = self.ps[5], self.psb[5]
            psO1, psO1b = self.ps[6], self.psb[6]
            psD, psDb = self.ps[7], self.psb[7]
            arot = self.psr([5, 6])
            pfx_blocks = NT
            wsteps = [16.0 / (2 ** k) for k in range(NBIS + 1)]

            for tg in range(NTG):
                nkb = pfx_blocks + (tg + 1) * 4
                self.dma("sp", cqT_t[:], self.cqT_d.ap()[:, tg * 512:(tg + 1) * 512].rearrange("(fc p) t -> p fc t", p=128),
                         r=[self.scr["cqT"]], w=[cqb])
                for hp in range(8):
                    ps, pb = rot.next()
                    for kc in range(4):
                        self.mm(ps[:, 0:512], wiq[:, kc, hp * 128:(hp + 1) * 128], cqT_t[:, kc, :],
                                start=(kc == 0), stop=(kc == 3), r=[wb_, cqb], w=[pb])
                    self.cp("act" if hp % 2 else "dve", qiT[:, hp, :], ps[:, 0:512], r=[pb], w=[qiTb])
                def b_stage(tq):
                    tcq = tg * 4 + tq
                    score, scb = scores[tcq % 2], scbs[tcq % 2]
                    S = (pfx_blocks + tcq + 1) * 128
                    for h in range(16):
                        self.ts("pool", diagw[:, h, :], self.IDB, widx[:, tcq, h:h + 1], None, ALU.mult,
                                r=[self.cstb, ldb], w=[diagb])
                    nb5 = (S + 511) // 512
                    for k5 in range(nb5):
                        wv = min(512, S - k5 * 512)
                        psA, psAb = arot.next()

                        def idx_s(h):
                            hp, base = h // 2, (h % 2) * 64
                            psI, psIb = rot.next()
                            self.mm(psI[:, 0:wv], qiT[base:base + 64, hp, tq * 128:(tq + 1) * 128],
                                    kidxT[base:base + 64, k5 * 512:k5 * 512 + wv], r=[qiTb, ldb], w=[psIb])
                            rh, rhb = Rrot.next()
                            self.act(rh[:, 0:wv], psI[:, 0:wv], AF.Relu, r=[psIb], w=[rhb])
                            return rh, rhb
                        pend = [idx_s(0), idx_s(1)]
                        for h in range(16):
                            if h + 2 < 16:
                                pend.append(idx_s(h + 2))
                            rh, rhb = pend[h]
                            self.mm(psA[:, 0:wv], diagw[:, h, :], rh[:, 0:wv], start=(h == 0), stop=(h == 15),
                                    r=[diagb, rhb], w=[psAb])
                        dst = score[:, k5 * 512:k5 * 512 + wv]
                        if k5 * 512 < T:
                            self.act(dst, psA[:, 0:wv], AF.Identity, bias=self.flg[:, 1:2], r=[psAb, self.flgb], w=[scb])
                        else:
                            self.cp("act", dst, psA[:, 0:wv], r=[psAb], w=[scb])

                def c_pair(tqs):
                    st_ = []
                    for n_, tq in enumerate(tqs):
                        tcq = tg * 4 + tq
                        st_.append((tq, scores[tcq % 2], scbs[tcq % 2], (pfx_blocks + tcq + 1) * 128, bis2[n_], bisb2[n_]))
                    for (tq, score, scb, S, bis, bisb) in st_:
                        self.tt("dve", score[:, S - 128:S], score[:, S - 128:S], self.CM, ALU.add, r=[scb, self.cstb], w=[scb])
                        self.P.op("dve", lambda e, S=S, score=score, bis=bis: e.tensor_reduce(
                            out=bis[:, 0:1], in_=score[:, 0:S], axis=AX.X, op=ALU.max), [scb], [bisb])
                        self.ts("dve", bis[:, 1:2], bis[:, 0:1], -16.0, None, ALU.add, r=[bisb], w=[bisb])
                    for k in range(NBIS):
                        for n_, (tq, score, scb, S, bis, bisb) in enumerate(st_):
                            self.ts("dve", junk1[n_][:, 0:1].broadcast_to([128, S]), score[:, 0:S], bis[:, 1:2], 0.0,
                                    ALU.is_ge, ALU.add, accum=bis[:, 2:3], r=[scb, bisb], w=[junkb1[n_], bisb])
                        for (tq, score, scb, S, bis, bisb) in st_:
                            if k + 1 < NBIS:
                                wn = wsteps[k + 1]
                                self.ts("dve", bis[:, 3:4], bis[:, 2:3], 255.5, 2.0 * wn, ALU.is_ge, ALU.mult, r=[bisb], w=[bisb])
                            else:
                                wl = wsteps[k]
                                self.ts("dve", bis[:, 3:4], bis[:, 2:3], 255.5, wl, ALU.is_ge, ALU.mult, r=[bisb], w=[bisb])
                        for (tq, score, scb, S, bis, bisb) in st_:
                            if k + 1 < NBIS:
                                wn = wsteps[k + 1]
                                self.stt(bis[:, 1:2], bis[:, 3:4], -wn, bis[:, 1:2], ALU.add, ALU.add, r=[bisb], w=[bisb])
                            else:
                                wl = wsteps[k]
                                self.stt(bis[:, 4:5], bis[:, 3:4], -wl, bis[:, 1:2], ALU.add, ALU.add, r=[bisb], w=[bisb])
                    for (tq, score, scb, S, bis, bisb) in st_:
                        self.ts("dve", negm[:, 0:S], score[:, 0:S], bis[:, 4:5], NEG, ALU.is_lt, ALU.mult, r=[scb, bisb],
                                w=[negb])
                        nkq = S // 128
                        for k4 in range((nkq + 3) // 4):
                            n4 = min(4, nkq - k4 * 4)
                            ps, pb = rot.next()
                            pv = ps[:].bitcast(BF16)
                            for q in range(n4):
                                kb = k4 * 4 + q
                                self.tr(pv[:, q * 128:(q + 1) * 128], negm[:, kb * 128:(kb + 1) * 128], self.IDB,
                                        r=[negb, self.cstb], w=[pb])
                            self.cp("act" if k4 % 2 else "dve", negT[:, k4 * 4:k4 * 4 + n4, tq * 128:(tq + 1) * 128],
                                    pv[:, 0:n4 * 128].rearrange("p (q t) -> p q t", q=n4), r=[pb], w=[negTb])
                        if nkq < nkb:
                            self.memset("pool", negT[:, nkq:nkb, tq * 128:(tq + 1) * 128], NEG, w=[negTb])

                b_stage(0)
                b_stage(1)
                c_pair((0, 1))
                b_stage(2)
                b_stage(3)
                c_pair((2, 3))
                for h in range(16):
                    ps, pb = rot.next()
                    for kc in range(4):
                        self.mm(ps[:, 0:512], wuq[:, kc, h * 128:(h + 1) * 128], cqT_t[:, kc, :],
                                start=(kc == 0), stop=(kc == 3), r=[wb_, cqb], w=[pb])
                    self.cp("dve", qT[:], ps[:, 0:512], r=[pb], w=[qTb])
                    for rc in range(2):
                        ps, pb = rot.next()
                        self.mm(ps[:, 0:512], wukT[:, h, rc * 128:(rc + 1) * 128], qT[:], r=[wukTb, qTb], w=[pb])
                        self.act(qlT[:, rc, :], ps[:, 0:512], AF.Copy, scale=128.0 ** -0.5, r=[pb], w=[qlTb])
                    def logits(kb):
                        psL, psLb = rot.next()
                        self.mm(psL[:, 0:512], ckvT[:, 0, kb * 128:(kb + 1) * 128], qlT[:, 0, :], start=True, stop=False,
                                r=[ldb, qlTb], w=[psLb])
                        self.mm(psL[:, 0:512], ckvT[:, 1, kb * 128:(kb + 1) * 128], qlT[:, 1, :], start=False, stop=False,
                                r=[ldb, qlTb], w=[psLb])
                        extra = []
                        for tq in range(4):
                            qb = pfx_blocks + tg * 4 + tq
                            if kb == qb:
                                extra.append((tq, 0))
                            elif kb == qb - 1:
                                extra.append((tq, 1))
                        self.mm(psL[:, 0:512], self.IDB, negT[:, kb, :], start=False, stop=(len(extra) == 0),
                                r=[self.cstb, negTb], w=[psLb])
                        for ei, (tq, kind) in enumerate(extra):
                            self.mm(psL[:, tq * 128:(tq + 1) * 128], self.IDB, biasT[:, h, kind, :], start=False,
                                    stop=(ei == len(extra) - 1), r=[self.cstb, biasb], w=[psLb])
                        pt, ptb = prot.next()
                        self.act(pt[:], psL[:, 0:512], AF.Exp, r=[psLb], w=[ptb])
                        return pt, ptb
                    pendl = [logits(0)]
                    if nkb > 1:
                        pendl.append(logits(1))
                    for kb in range(nkb):
                        if kb + 2 < nkb:
                            pendl.append(logits(kb + 2))
                        pt, ptb = pendl[kb]
                        first, last = (kb == 0), (kb == nkb - 1)
                        self.mm(psO0[:, 0:512], ckvtok[:, kb, 0:128], pt[:], start=first, stop=last, r=[ldb, ptb], w=[psO0b])
                        self.mm(psO1[:, 0:512], ckvtok[:, kb, 128:256], pt[:], start=first, stop=last, r=[ldb, ptb], w=[psO1b])
                        self.mm(psD[:, 0:512], self.ONESB, pt[:], start=first, stop=last, r=[self.cstb, ptb], w=[psDb])
                    self.recip(rec[:], psD[:, 0:512], r=[psDb], w=[recb])
                    self.tt("dve", onT[:, 0, :], psO0[:, 0:512], rec[:], ALU.mult, r=[psO0b, recb], w=[onTb])
                    self.tt("dve", onT[:, 1, :], psO1[:, 0:512], rec[:], ALU.mult, r=[psO1b, recb], w=[onTb])
                    ps, pb = rot.next()
                    for rc in range(2):
                        self.mm(ps[:, 0:512], wuv[:, rc, h * 128:(h + 1) * 128], onT[:, rc, :], start=(rc == 0),
                                stop=(rc == 1), r=[wb_, onTb], w=[pb])
                    ya, yab = yast[h % 2], yastb[h % 2]
                    self.cp("act", ya[:], ps[:, 0:512], r=[pb], w=[yab])
                    self.dma("sp", self.yaT_d.ap()[h * 128:(h + 1) * 128, tg * 512:(tg + 1) * 512], ya[:], r=[yab],
                             w=[Buf()])
            self.P.barrier()

    def phase_merge_ffn(self):
        NT, T, NTG = self.NT, self.T, self.NTG
        sb = self.sb
        mv = self.modv.ap()
        rot = self.psr([0, 1, 2, 3, 4, 5, 6, 7])
        dsrc = self.w_d.rearrange("(kc p) n -> p kc n", p=128)
        with self.scope() as st:
            x1 = sb(st, "x1", [128, 4, 2048], F32)
            x1b = [Buf() for _ in range(4)]
            h2T = sb(st, "h2T", [128, 16, 512], BF16)
            h2Tb = [[Buf() for _ in range(16)] for _ in range(4)]
            xn2 = sb(st, "xn2", [128, 2048], BF16)
            xn2b = Buf()
            mtmp = sb(st, "mtmp", [128, 512], F32)
            mtb = Buf()
            fsm = sb(st, "fsm", [128, 16], F32)
            fsmb = Buf()
            wA = [sb(st, "wA%d" % i, [128, 16, 512], BF16) for i in range(2)]
            wAb = [Buf(), Buf()]
            for tg in range(NTG):
                tsl = slice(tg * 512, (tg + 1) * 512)
                for q in range(4):
                    self.dma("sp", x1[:, q, :], self.xo[tg * 512 + q * 128:tg * 512 + (q + 1) * 128, :], w=[x1b[q]])
                with self.scope() as sm_:
                    g1bc = sb(sm_, "g1bc", [128, 2048], F32)
                    gbb = Buf()
                    self.dma("sp", g1bc[:], mv[32:48, :].rearrange("c p -> (c p)").partition_broadcast(128),
                             r=[self.scr["modv"]], w=[gbb])
                    yaT = sb(sm_, "yaT", [128, 16, 512], BF16)
                    ybT = sb(sm_, "ybT", [128, 32, 512], BF16)
                    yb_ = Buf()
                    mT = sb(sm_, "mT", [128, 16, 512], BF16)
                    mTb = [Buf() for _ in range(16)]
                    wB = [sb(sm_, "wB%d" % i, [128, 32, 128], BF16) for i in range(2)]
                    wBb = [Buf(), Buf()]
                    gt = [sb(sm_, "gt%d" % i, [128, 2, 512], F32) for i in range(2)]
                    gtb = [Buf(), Buf()]
                    self.dma("sp", yaT[:], self.yaT_d.ap()[:, tsl].rearrange("(kc p) t -> p kc t", p=128),
                             r=[self.scr["yaT"]], w=[yb_])
                    self.dma("sp", ybT[:], self.ybT_d.ap()[:, tsl].rearrange("(kc p) t -> p kc t", p=128),
                             r=[self.scr["ybT"]], w=[yb_])

                    def issue_m(i):
                        self.load_w(wA[i % 2][:, :, 0:128], self.w_pa, i * 128, 128, wAb[i % 2])
                        self.load_w(wB[i % 2][:], self.w_pb, i * 128, 128, wBb[i % 2])
                    issue_m(0)
                    for mc in range(16):
                        i = mc
                        if i + 1 < 16:
                            issue_m(i + 1)
                        g_t, g_b = gt[mc % 2], gtb[mc % 2]
                        self.dma("sp", g_t[:, 0, :], self.gT_d.ap()[mc * 128:(mc + 1) * 128, tsl], r=[self.scr["gT"]],
                                 w=[g_b])
                        self.dma("sp", g_t[:, 1, :], self.gT_d.ap()[2048 + mc * 128:2048 + (mc + 1) * 128, tsl],
                                 r=[self.scr["gT"]], w=[g_b])
                        psa, psab = rot.next()
                        for kc in range(16):
                            self.mm(psa[:, 0:512], wA[i % 2][:, kc, 0:128], yaT[:, kc, :],
                                    start=(kc == 0), stop=(kc == 15), r=[wAb[i % 2], yb_], w=[psab])
                        psb_, psbb = rot.next()
                        for kc in range(32):
                            self.mm(psb_[:, 0:512], wB[i % 2][:, kc, :], ybT[:, kc, :],
                                    start=(kc == 0), stop=(kc == 31), r=[wBb[i % 2], yb_], w=[psbb])
                        self.tt("dve", mtmp[:], psa[:, 0:512], g_t[:, 0, :], ALU.mult, r=[psab, g_b], w=[mtb])
                        self.tt("dve", g_t[:, 1, :], psb_[:, 0:512], g_t[:, 1, :], ALU.mult, r=[psbb, g_b], w=[g_b])
                        self.tt("pool", mT[:, mc, :], mtmp[:], g_t[:, 1, :], ALU.add, r=[mtb, g_b], w=[mTb[mc]])

                    def issue_o(i):
                        self.load_w(wA[i % 2][:], self.w_o, i * 512, 512, wAb[i % 2])
                    issue_o(0)
                    for i in range(4):
                        if i + 1 < 4:
                            issue_o(i + 1)
                        for q in range(4):
                            ps, pb = rot.next()
                            for kc in range(16):
                                self.mm(ps[:, 0:512], mT[:, kc, q * 128:(q + 1) * 128], wA[i % 2][:, kc, :],
                                        start=(kc == 0), stop=(kc == 15), r=[mTb[kc], wAb[i % 2]], w=[pb])
                            self.tt("dve", mtmp[:], ps[:, 0:512], g1bc[:, i * 512:(i + 1) * 512], ALU.mult, r=[pb, gbb],
                                    w=[mtb])
                            self.tt("pool", x1[:, q, i * 512:(i + 1) * 512], x1[:, q, i * 512:(i + 1) * 512], mtmp[:],
                                    ALU.add, r=[x1b[q], mtb], w=[x1b[q]])
                    self.P.barrier()
                for q in range(4):
                    self.act(xn2[:], x1[:, q, :], AF.Square, accum=fsm[:, 0:1], r=[x1b[q]], w=[xn2b, fsmb])
                    self.rstd(fsm[:, 1:2], fsm[:, 0:1], 2048.0, fsm[:, 2:3], [fsmb], [fsmb])
                    self.act(xn2[:], x1[:, q, :], AF.Copy, scale=fsm[:, 1:2], r=[x1b[q], fsmb], w=[xn2b])
                    for half in range(2):
                        ps, pb = rot.next()
                        pv = ps[:].bitcast(BF16)
                        for j in range(8):
                            fc = half * 8 + j
                            self.tr(pv[:, j * 128:(j + 1) * 128], xn2[:, fc * 128:(fc + 1) * 128], self.IDB,
                                    r=[xn2b, self.cstb], w=[pb])
                        for j in range(8):
                            fc = half * 8 + j
                            dst = h2T[:, fc, q * 128:(q + 1) * 128]
                            if half == 0:
                                self.act(dst, pv[:, j * 128:(j + 1) * 128], AF.Identity, bias=self.sh2[:, fc:fc + 1],
                                         scale=self.s2[:, fc:fc + 1], r=[pb, self.vecb], w=[h2Tb[q][fc]])
                            else:
                                self.ts("dve", dst, pv[:, j * 128:(j + 1) * 128], self.s2[:, fc:fc + 1],
                                        self.sh2[:, fc:fc + 1], ALU.mult, ALU.add, r=[pb, self.vecb], w=[h2Tb[q][fc]])
                with self.scope() as sf_:
                    g2bc = sb(sf_, "g2bc", [128, 2048], F32)
                    gbb = Buf()
                    self.dma("sp", g2bc[:], mv[80:96, :].rearrange("c p -> (c p)").partition_broadcast(128),
                             r=[self.scr["modv"]], w=[gbb])
                    aT = sb(sf_, "aT", [128, 44, 512], BF16)
                    aTb = [Buf() for _ in range(44)]
                    sg = sb(sf_, "sg", [128, 512], F32)
                    sgb = Buf()
                    wD = [sb(sf_, "wD%d" % i, [128, 44, 256], BF16) for i in range(2)]
                    wDb = [Buf(), Buf()]

                    def issue_f(i):
                        self.load_w(wA[i % 2][:, :, 0:256], self.w_g, i * 256, 256, wAb[i % 2])
                        self.load_w(wA[i % 2][:, :, 256:512], self.w_u, i * 256, 256, wAb[i % 2])

                    def issue_d(i):
                        self.dma("pool", wD[i % 2][:], dsrc[:, :, i * 256:(i + 1) * 256], w=[wDb[i % 2]])
                    issue_f(0)
                    for i in range(22):
                        if i + 1 < 22:
                            issue_f(i + 1)
                        elif True:
                            issue_d(0)
                        for sub in range(2):
                            fcx = i * 2 + sub
                            hr = [h2Tb[q][kc] for q in range(4) for kc in range(16)]
                            psg, psgb = rot.next()
                            for kc in range(16):
                                self.mm(psg[:, 0:512], wA[i % 2][:, kc, sub * 128:(sub + 1) * 128], h2T[:, kc, :],
                                        start=(kc == 0), stop=(kc == 15), r=[wAb[i % 2]] + (hr if kc == 0 else []),
                                        w=[psgb])
                            psu, psub = rot.next()
                            for kc in range(16):
                                self.mm(psu[:, 0:512], wA[i % 2][:, kc, 256 + sub * 128:256 + (sub + 1) * 128],
                                        h2T[:, kc, :], start=(kc == 0), stop=(kc == 15), r=[wAb[i % 2]], w=[psub])
                            self.act(sg[:], psg[:, 0:512], AF.Silu, r=[psgb], w=[sgb])
                            self.tt("dve", aT[:, fcx, :], psu[:, 0:512], sg[:], ALU.mult, r=[psub, sgb], w=[aTb[fcx]])
                    for i in range(8):
                        if i + 1 < 8:
                            issue_d(i + 1)
                        for q in range(4):
                            ps, pb = rot.next()
                            for kc in range(44):
                                self.mm(ps[:, 0:256], aT[:, kc, q * 128:(q + 1) * 128], wD[i % 2][:, kc, :],
                                        start=(kc == 0), stop=(kc == 43), r=[aTb[kc], wDb[i % 2]], w=[pb])
                            self.tt("dve", mtmp[:, 0:256], ps[:, 0:256], g2bc[:, i * 256:(i + 1) * 256], ALU.mult,
                                    r=[pb, gbb], w=[mtb])
                            self.tt("pool", x1[:, q, i * 256:(i + 1) * 256], x1[:, q, i * 256:(i + 1) * 256],
                                    mtmp[:, 0:256], ALU.add, r=[x1b[q], mtb], w=[x1b[q]])
                    self.P.barrier()
                with self.scope() as so_:
                    fgbc = sb(so_, "fgbc", [128, 2048], F32)
                    gbb = Buf()
                    self.dma("sp", fgbc[:], self.fg_d.partition_broadcast(128), w=[gbb])
                    osb = [sb(so_, "osb%d" % i, [128, 2048], F32) for i in range(2)]
                    osbb = [Buf(), Buf()]
                    for q in range(4):
                        ob, obb = osb[q % 2], osbb[q % 2]
                        self.act(xn2[:], x1[:, q, :], AF.Square, accum=fsm[:, 4:5], r=[x1b[q]], w=[xn2b, fsmb])
                        self.rstd(fsm[:, 5:6], fsm[:, 4:5], 2048.0, fsm[:, 6:7], [fsmb], [fsmb])
                        self.stt(ob[:], x1[:, q, :], fsm[:, 5:6], fgbc[:], ALU.mult, ALU.mult, r=[x1b[q], fsmb, gbb],
                                 w=[obb])
                        self.dma("sp", self.y[tg * 512 + q * 128:tg * 512 + (q + 1) * 128, :], ob[:], r=[obb],
                                 w=[Buf()])
                    self.P.barrier()


def _t5_bucket(d):
    n = np.maximum(d, 0)
    nf = np.maximum(n, 1).astype(np.float32)
    large = 16 + (np.log(nf / np.float32(16)) / np.float32(np.log(128 / 16)) * np.float32(16)).astype(np.int32)
    large = np.minimum(large, 31)
    return np.where(n < 16, n, large)


def _consts():
    cst = np.zeros((128, 768), np.float32)
    i = np.arange(128)
    cst[:, 0:128] = np.eye(128)
    cst[:, 128:256] = (i[:, None] <= i[None, :])
    cst[:, 256:384] = (i[:, None] > i[None, :])
    cst[:, 384:512] = np.eye(128)[::-1]
    cst[:, 512:640] = np.where(i[None, :] <= i[:, None], 0.0, -1e30)
    cst[:, 640:768] = 1.0
    ohd = np.zeros((32, 384), np.float32)
    for j in range(384):
        d = j - 127
        if 0 <= d <= 255:
            ohd[int(_t5_bucket(np.array(d))), j] += 1.0
            ohd[31, j] -= 1.0
    return cst, ohd


_CACHE = {}


def make_in_maps(inputs, NT, n_seq):
    T = NT * 128
    f = lambda a: np.ascontiguousarray(np.asarray(a, dtype=np.float32))
    x = f(inputs["x"])
    c = f(inputs["c"])
    cst, ohd = _consts()
    shared = {
        "w_ada": f(inputs["w_ada"][0]), "b_ada": f(inputs["b_ada"][0]).reshape(96, 128),
        "norm1_g": f(inputs["norm1_g"][0]).reshape(16, 128), "w_in": f(inputs["w_in"][0]),
        "cq_g": f(inputs["cq_norm_g"][0]).reshape(4, 128), "ckv_g": f(inputs["ckv_norm_g"][0]).reshape(1, 256),
        "kidx_g": f(inputs["kidx_norm_g"][0]).reshape(1, 64), "kidx_b": f(inputs["kidx_norm_b"][0]).reshape(1, 64),
        "w_uq": f(inputs["w_uq"][0]), "w_iq": f(inputs["w_iq"][0]), "w_uk": f(inputs["w_uk"][0]),
        "w_uv": f(inputs["w_uv"][0]), "rel_bias": f(inputs["rel_bias"]),
        "conv_w": f(inputs["conv_w"][0]).reshape(192, 128), "conv_b": f(inputs["conv_b"][0]).reshape(48, 128),
        "dt_bias": f(inputs["dt_bias"][0]).reshape(1, 64), "a_log": f(inputs["a_log"][0]).reshape(1, 64),
        "d_skip": f(inputs["d_skip"][0]).reshape(1, 64), "ssm_g": f(inputs["ssm_norm_g"][0]).reshape(1, 4096),
        "w_proj_a": f(inputs["w_proj_a"][0]), "w_proj_b": f(inputs["w_proj_b"][0]), "w_out": f(inputs["w_out"][0]),
        "norm2_g": f(inputs["norm2_g"][0]).reshape(16, 128), "w_gate": f(inputs["w_gate"][0]),
        "w_up": f(inputs["w_up"][0]), "w_down": f(inputs["w_down"][0]), "final_g": f(inputs["final_g"]).reshape(1, 2048),
        "cst": cst, "ohd": ohd,
    }
    maps = []
    for core in range(2 * n_seq):
        b, j = core // 2, core % 2
        flg = np.zeros((128, 2), np.float32)
        flg[:, 0] = float(j)
        flg[:, 1] = 0.0 if j == 1 else -1e30
        m = dict(shared)
        m["xo"] = np.ascontiguousarray(x[b, j * T:(j + 1) * T])
        m["xp"] = np.ascontiguousarray(x[b, 0:T])
        m["cb"] = np.ascontiguousarray(c[b].reshape(16, 128))
        m["flg"] = flg
        maps.append(m)
    return maps


def run(inputs, NT, debug=(), stop=99, n_seq=4):
    key = (NT, tuple(debug), stop)
    if key not in _CACHE:
        bld = Builder(NT, debug, stop)
        nc = bld.build()
        _CACHE[key] = (bld, nc)
    bld, nc = _CACHE[key]
    maps = make_in_maps(inputs, NT, n_seq)
    res = run_bass_kernel_spmd(nc, maps, core_ids=list(range(2 * n_seq)))
    T = NT * 128
    out = np.zeros((4, 2 * T, 2048), np.float32)
    for core in range(2 * n_seq):
        b, j = core // 2, core % 2
        out[b, j * T:(j + 1) * T] = res.results[core]["y"]
    return out, res


def kernel(**inputs):
    out, _ = run(inputs, 16)
    return out
```

```python
import numpy as np
from contextlib import ExitStack
import concourse.bass as bass
import concourse.mybir as mybir
from concourse.bass_utils import run_bass_kernel_spmd

F32 = mybir.dt.float32
BF16 = mybir.dt.bfloat16
AF = mybir.ActivationFunctionType
ALU = mybir.AluOpType
AX = mybir.AxisListType

STREAMS = ("pe", "act", "dve", "pool", "sp")
N_DMA_SEMS = 10
EPS = 1e-6

OFF_CQ, OFF_CKV, OFF_KIDX, OFF_WIDX, OFF_Z, OFF_X, OFF_B, OFF_C, OFF_DT, OFF_GA, OFF_GB = (
    0, 512, 768, 832, 848, 4944, 9040, 10064, 11088, 11152, 13200)
NEG = -30000.0
NBIS = 17


class Buf:
    __slots__ = ("w", "r", "excl")

    def __init__(self, excl=False):
        self.w = None
        self.r = {}
        self.excl = excl


class Prog:
    def __init__(self, nc, stack):
        self.nc = nc
        self.ops = {s: [] for s in STREAMS}
        self.cnt = {s: 0 for s in STREAMS}
        self.waited = {s: {} for s in STREAMS}
        self.sems = {}
        for s in STREAMS:
            self.sems[("eng", s)] = stack.enter_context(nc.semaphore("s_" + s))
        self.dma_i = {}
        self.dma_target = {}
        for s in ("sp", "pool", "act"):
            self.dma_i[s] = 0
            for k in range(N_DMA_SEMS):
                key = ("dma", s, k)
                self.sems[key] = stack.enter_context(nc.semaphore("d_%s%d" % (s, k)))
                self.dma_target[key] = 0

    def _collect(self, stream, reads, writes):
        deps = []
        pe = stream == "pe"
        for b in reads:
            if b.w is not None and not (pe and b.w[2] == "pe"):
                deps.append(b.w)
            if b.excl:
                for tok in b.r.values():
                    if tok[2] != stream:
                        deps.append(tok)
        for b in writes:
            if b.w is not None and not (pe and b.w[2] == "pe"):
                deps.append(b.w)
            for tok in b.r.values():
                if not (pe and tok[2] == "pe"):
                    deps.append(tok)
        return deps

    def _waits(self, stream, deps):
        best = {}
        for (key, val, _p) in deps:
            if val > best.get(key, 0):
                best[key] = val
        out = []
        wd = self.waited[stream]
        for key, val in best.items():
            if wd.get(key, 0) >= val:
                continue
            wd[key] = val
            out.append((key, val))
        return out

    def _update(self, tok, reads, writes):
        for b in reads:
            b.r[tok[2]] = tok
        for b in writes:
            b.w = tok
            b.r = {}

    def op(self, stream, fn, reads=(), writes=()):
        deps = self._collect(stream, reads, writes)
        waits = self._waits(stream, deps)
        self.cnt[stream] += 1
        key = ("eng", stream)
        tok = (key, self.cnt[stream], stream)
        self.ops[stream].append((waits, fn, key, 1))
        self._update(tok, reads, writes)
        return tok

    def dma(self, stream, fn, reads=(), writes=()):
        i = self.dma_i[stream]
        self.dma_i[stream] = i + 1
        key = ("dma", stream, i % N_DMA_SEMS)
        deps = self._collect("dma:" + stream, reads, writes)
        prev = self.dma_target[key]
        if prev > 0:
            deps.append((key, prev, "x"))
        waits = self._waits(stream, deps)
        self.dma_target[key] = prev + 16
        tok = (key, prev + 16, "dma:%s%d" % (stream, i % N_DMA_SEMS))
        self.ops[stream].append((waits, fn, key, 16))
        self._update(tok, reads, writes)
        return tok

    def wait_all(self, stream, bufs):
        deps = [b.w for b in bufs if b.w is not None]
        waits = self._waits(stream, deps)
        if waits:
            self.ops[stream].append((waits, None, None, 0))

    def barrier(self):
        deps = [(("eng", s), self.cnt[s], s) for s in STREAMS if self.cnt[s] > 0]
        deps += [(k, v, "x") for k, v in self.dma_target.items() if v > 0]
        for s in STREAMS:
            waits = self._waits(s, [d for d in deps if d[0] != ("eng", s)])
            if waits:
                self.ops[s].append((waits, None, None, 0))

    def emit(self):
        nc = self.nc
        sems = self.sems

        def run(stream, eng):
            for (waits, fn, key, inc) in self.ops[stream]:
                for (wkey, val) in waits:
                    eng.wait_ge(sems[wkey], val)
                if fn is not None:
                    fn(eng).then_inc(sems[key], inc)

        with nc.Block() as block:
            @block.tensor
            def _(e):
                run("pe", e)

            @block.scalar
            def _(e):
                run("act", e)

            @block.vector
            def _(e):
                run("dve", e)

            @block.gpsimd
            def _(e):
                run("pool", e)

            @block.sync
            def _(e):
                run("sp", e)


class Scope:
    def __init__(self, bld):
        self.bld = bld

    def __enter__(self):
        self.mark = self.bld.arena_ptr
        return self

    def __exit__(self, *a):
        self.bld.arena_ptr = self.mark
        return False

    def alloc(self, shape, dt):
        bld = self.bld
        esz = 4 if dt == F32 else 2
        n = 1
        for d in shape[1:]:
            n *= d
        nbytes = (n * esz + 63) // 64 * 64
        off = bld.arena_ptr
        bld.arena_ptr = off + nbytes
        bld.arena_peak = max(bld.arena_peak, bld.arena_ptr)
        assert bld.arena_ptr <= bld.ARENA, "SBUF arena overflow %d" % bld.arena_ptr
        ap = bld.arena[0:shape[0], off:off + n * esz].bitcast(dt)
        if len(shape) == 3:
            ap = ap.rearrange("p (a b) -> p a b", a=shape[1])
        elif len(shape) == 4:
            ap = ap.rearrange("p (a b c) -> p a b c", a=shape[1], b=shape[2])
        return ap


class Rot:
    def __init__(self, items):
        self.items = items
        self.i = 0

    def next(self):
        it = self.items[self.i % len(self.items)]
        self.i += 1
        return it


class Builder:
    def __init__(self, NT, debug=(), stop=99):
        self.stop = stop
        self.NT = NT
        self.T = NT * 128
        self.NTG = NT // 4
        self.debug = set(debug)
        self.dbg_out = {}
        self.nc = bass.Bass("TRN2", target_bir_lowering=False)

    def mm(self, out, lhsT, rhs, start=True, stop=True, r=(), w=()):
        self.P.op("pe", lambda e: e.matmul(out, lhsT=lhsT, rhs=rhs, start=start, stop=stop), r, w)

    def tr(self, out, in_, ident, r=(), w=()):
        self.P.op("pe", lambda e: e.transpose(out, in_, ident), r, w)

    def act(self, out, in_, func, bias=None, scale=None, accum=None, r=(), w=()):
        kw = {}
        if bias is not None:
            kw["bias"] = bias
        if scale is not None:
            kw["scale"] = scale
        if accum is not None:
            kw["accum_out"] = accum
        self.P.op("act", lambda e: e.activation(out=out, in_=in_, func=func, **kw), r, w)

    def ts(self, eng, out, in0, s1, s2, op0, op1=None, accum=None, r=(), w=()):
        kw = {}
        if op1 is not None:
            kw["op1"] = op1
        if accum is not None:
            kw["accum_out"] = accum
        self.P.op(eng, lambda e: e.tensor_scalar(out=out, in0=in0, scalar1=s1, scalar2=s2, op0=op0, **kw), r, w)

    def tt(self, eng, out, in0, in1, op, r=(), w=()):
        self.P.op(eng, lambda e: e.tensor_tensor(out=out, in0=in0, in1=in1, op=op), r, w)

    def stt(self, out, in0, scalar, in1, op0, op1, r=(), w=()):
        self.P.op("dve", lambda e: e.scalar_tensor_tensor(out=out, in0=in0, scalar=scalar, in1=in1,
                                                          op0=op0, op1=op1), r, w)

    def cp(self, eng, out, in_, r=(), w=()):
        if eng == "act":
            self.P.op("act", lambda e: e.activation(out=out, in_=in_, func=AF.Copy), r, w)
        else:
            self.P.op(eng, lambda e: e.tensor_copy(out, in_), r, w)

    def memset(self, eng, ap, val, w=()):
        self.P.op(eng, lambda e: e.memset(ap, val), (), w)

    def dma(self, q, out, in_, r=(), w=()):
        self.P.dma(q, lambda e: e.dma_start(out=out, in_=in_), r, w)

    def recip(self, out, in_, r=(), w=()):
        self.P.op("dve", lambda e: e.reciprocal(out=out, in_=in_), r, w)

    def sb(self, st, name, shape, dt):
        return st.alloc(shape, dt)

    def scope(self):
        return Scope(self)

    def din(self, name, shape, dt=F32):
        return self.nc.dram_tensor(name, list(shape), dt, kind="ExternalInput").ap()

    def dscr(self, name, shape, dt):
        kind = "ExternalOutput" if name in self.debug else "Internal"
        t = self.nc.dram_tensor(name, list(shape), dt, kind=kind)
        if name in self.debug:
            self.dbg_out[name] = (list(shape), dt)
        return t

    def rstd(self, out, ss, n, tmp, bufs_r, bufs_w):
        self.ts("dve", tmp, ss, 1.0 / n, EPS, ALU.mult, ALU.add, r=bufs_r, w=bufs_w)
        self.act(tmp, tmp, AF.Sqrt, r=bufs_w, w=bufs_w)
        self.recip(out, tmp, r=bufs_w, w=bufs_w)

    def build(self):
        nc = self.nc
        NT, T, NTG = self.NT, self.T, self.NTG
        din = self.din
        self.xo = din("xo", [T, 2048])
        self.xp = din("xp", [T, 2048])
        self.cb_d = din("cb", [16, 128])
        self.flg_d = din("flg", [128, 2])
        self.w_ada = din("w_ada", [2048, 12288])
        self.b_ada = din("b_ada", [96, 128])
        self.n1_d = din("norm1_g", [16, 128])
        self.w_in = din("w_in", [2048, 15248])
        self.cqg_d = din("cq_g", [4, 128])
        self.ckvg_d = din("ckv_g", [1, 256])
        self.kig_d = din("kidx_g", [1, 64])
        self.kib_d = din("kidx_b", [1, 64])
        self.w_uq = din("w_uq", [512, 2048])
        self.w_iq = din("w_iq", [512, 1024])
        self.w_uk = din("w_uk", [256, 2048])
        self.w_uv = din("w_uv", [256, 2048])
        self.relb_d = din("rel_bias", [32, 16])
        self.convw_d = din("conv_w", [192, 128])
        self.convb_d = din("conv_b", [48, 128])
        self.dtb_d = din("dt_bias", [1, 64])
        self.alog_d = din("a_log", [1, 64])
        self.dsk_d = din("d_skip", [1, 64])
        self.ssmg_d = din("ssm_g", [1, 4096])
        self.w_pa = din("w_proj_a", [2048, 2048])
        self.w_pb = din("w_proj_b", [4096, 2048])
        self.w_o = din("w_out", [2048, 2048])
        self.n2_d = din("norm2_g", [16, 128])
        self.w_g = din("w_gate", [2048, 5632])
        self.w_u = din("w_up", [2048, 5632])
        self.w_d = din("w_down", [5632, 2048])
        self.fg_d = din("final_g", [1, 2048])
        self.cst_d = din("cst", [128, 768])
        self.ohd_d = din("ohd", [32, 384])
        self.y = nc.dram_tensor("y", [T, 2048], F32, kind="ExternalOutput").ap()

        self.modv = self.dscr("modv", [96, 128], F32)
        self.tv = self.dscr("tv", [16, 384], F32)
        self.ckvT_d = self.dscr("ckvT_d", [256, 2 * T], BF16)
        self.ckvtok_d = self.dscr("ckvtok_d", [2 * T, 256], BF16)
        self.kidxT_d = self.dscr("kidxT_d", [128, 2 * T], BF16)
        self.cqT_d = self.dscr("cqT_d", [512, T], BF16)
        self.widx_d = self.dscr("widx_d", [T, 16], F32)
        self.x_d = self.dscr("x_d", [2 * T, 4096], BF16)
        self.z_d = self.dscr("z_d", [T, 4096], BF16)
        self.BT_d = self.dscr("BT_d", [1024, T], BF16)
        self.CT_d = self.dscr("CT_d", [1024, T], BF16)
        self.Btok_d = self.dscr("Btok_d", [2 * T, 1024], BF16)
        self.gT_d = self.dscr("gT_d", [4096, T], F32)
        self.yaT_d = self.dscr("yaT_d", [2048, T], BF16)
        self.ybT_d = self.dscr("ybT_d", [4096, T], BF16)
        self.scr = {n: Buf() for n in ("modv", "tv", "ckvT", "ckvtok", "kidxT", "cqT", "widx", "x0", "x1", "z",
                                       "BT", "CT", "Btok0", "Btok1", "gT", "yaT", "ybT")}
        self.outb = Buf()

        with ExitStack() as gst0:
            self.P = Prog(nc, gst0)
            self.ps = [gst0.enter_context(nc.psum_tensor("ps%d" % i, [128, 512], F32)) for i in range(8)]
            self.psb = [Buf(excl=True) for _ in range(8)]
            self.ARENA = 207 * 1024
            self.arena = gst0.enter_context(nc.sbuf_tensor("arena", [128, self.ARENA], mybir.dt.uint8))
            self.arena_ptr = 0
            self.arena_peak = 0
            self.gst = Scope(self)
            self.gst.__enter__()
            self.phase0()
            with self.scope() as s1:
                if self.stop >= 1:
                    self.phase_inproj(s1)
                if self.stop >= 2:
                    self.phase_ssd()
            if self.stop >= 3:
                self.phase_attn()
            if self.stop >= 4:
                self.phase_merge_ffn()
            self.P.wait_all("sp", [self.outb])
            self.P.barrier()
            self.P.emit()
        return nc

    def psr(self, idxs):
        return Rot([(self.ps[i], self.psb[i]) for i in idxs])

    def load_cols(self, st, rows_ap, R, out_ap, out_buf, name, psrot):
        tmp = self.sb(st, "lc_" + name, [R, 128], F32)
        tb = Buf()
        self.dma("sp", tmp[:], rows_ap, w=[tb])
        ps, pb = psrot.next()
        self.tr(ps[:, 0:R], tmp[:], self.IDF[0:R, 0:R], r=[tb, self.cstb], w=[pb])
        self.cp("dve", out_ap, ps[:, 0:R], r=[pb], w=[out_buf])

    def phase0(self):
        nc, gst = self.nc, self.gst
        sb = self.sb
        self.cstf = sb(gst, "cstf", [128, 768], F32)
        self.cstb = Buf()
        self.cstbf = sb(gst, "cstbf", [128, 768], BF16)
        self.cstbb = self.cstb
        self.dma("sp", self.cstf[:], self.cst_d, w=[self.cstb])
        self.cp("dve", self.cstbf[:], self.cstf[:], r=[self.cstb], w=[self.cstb])
        c = self.cstf
        self.IDF, self.U, self.L1, self.J, self.CM, self.ONES = (c[:, 0:128], c[:, 128:256], c[:, 256:384],
                                                                 c[:, 384:512], c[:, 512:640], c[:, 640:768])
        cbf = self.cstbf
        self.IDB, self.UB, self.ONESB = cbf[:, 0:128], cbf[:, 128:256], cbf[:, 640:768]
        self.vec = sb(gst, "vec", [128, 512], F32)
        self.vecb = Buf()
        v = self.vec
        self.modT = v[:, 0:96]
        self.n1T, self.n2T = v[:, 96:112], v[:, 112:128]
        self.s1, self.s2 = v[:, 128:144], v[:, 144:160]
        self.cqgT = v[:, 160:164]
        self.convbT = v[:, 164:212]
        self.convwT = v[:, 212:404]
        self.cT = v[:, 404:420]
        self.badaT = v[:, 420:516] if False else None
        self.flg = sb(gst, "flgt", [128, 2], F32)
        self.flgb = Buf()
        self.dma("sp", self.flg[:], self.flg_d, w=[self.flgb])
        self.bc = sb(gst, "bct", [128, 256 + 64 * 5], F32)
        self.bcb = Buf()
        b = self.bc
        self.ckvg_bc, self.kig_bc, self.kib_bc = b[:, 0:256], b[:, 256:320], b[:, 320:384]
        self.dtb_bc, self.a_bc, self.dsk_bc = b[:, 384:448], b[:, 448:512], b[:, 512:576]
        for ap, src in ((self.ckvg_bc, self.ckvg_d), (self.kig_bc, self.kig_d), (self.kib_bc, self.kib_d),
                        (self.dtb_bc, self.dtb_d), (self.a_bc, self.alog_d), (self.dsk_bc, self.dsk_d)):
            self.dma("sp", ap, src.partition_broadcast(128), w=[self.bcb])
        self.act(self.a_bc, self.a_bc, AF.Exp, r=[self.bcb], w=[self.bcb])
        self.ts("dve", self.a_bc, self.a_bc, -1.0, None, ALU.mult, r=[self.bcb], w=[self.bcb])
        self.c_actT = sb(gst, "c_actT", [128, 16], BF16)
        self.halo = sb(gst, "halo", [128, 48, 3], F32)
        self.halob = Buf()

        with self.scope() as st:
            rot = self.psr([0, 1, 2, 3])
            badaT = sb(st, "badaT", [128, 96], F32)
            bb = Buf()
            self.load_cols(st, self.b_ada, 96, badaT[:], bb, "bada", rot)
            self.load_cols(st, self.n1_d, 16, self.n1T, self.vecb, "n1", rot)
            self.load_cols(st, self.n2_d, 16, self.n2T, self.vecb, "n2", rot)
            self.load_cols(st, self.cqg_d, 4, self.cqgT, self.vecb, "cqg", rot)
            self.load_cols(st, self.convb_d, 48, self.convbT, self.vecb, "cvb", rot)
            self.load_cols(st, self.convw_d[0:96, :], 96, self.convwT[:, 0:96], self.vecb, "cvw0", rot)
            self.load_cols(st, self.convw_d[96:192, :], 96, self.convwT[:, 96:192], self.vecb, "cvw1", rot)
            self.load_cols(st, self.cb_d, 16, self.cT, self.vecb, "cb", rot)
            self.act(self.c_actT[:], self.cT, AF.Silu, r=[self.vecb], w=[self.vecb])
            wa = [sb(st, "wa%d" % i, [128, 16, 1536], BF16) for i in range(2)]
            wab = [Buf(), Buf()]
            wsrc = self.w_ada.rearrange("(kc p) n -> p kc n", p=128)
            psm, psmb = self.ps[4], self.psb[4]
            self.dma("pool", wa[0][:], wsrc[:, :, 0:1536], w=[wab[0]])
            for fgp in range(8):
                if fgp + 1 < 8:
                    self.dma("pool", wa[(fgp + 1) % 2][:], wsrc[:, :, (fgp + 1) * 1536:(fgp + 2) * 1536],
                             w=[wab[(fgp + 1) % 2]])
                wt, wb = wa[fgp % 2], wab[fgp % 2]
                for fc in range(12):
                    col = fgp * 12 + fc
                    for kc in range(16):
                        self.mm(psm[:, col:col + 1], wt[:, kc, fc * 128:(fc + 1) * 128], self.c_actT[:, kc:kc + 1],
                                start=(kc == 0), stop=(kc == 15), r=[wb, self.vecb], w=[psmb])
            self.tt("dve", self.modT, psm[:, 0:96], badaT[:], ALU.add, r=[psmb, bb], w=[self.vecb])
            self.stt(self.s1, self.modT[:, 16:32], 1.0, self.n1T, ALU.add, ALU.mult, r=[self.vecb], w=[self.vecb])
            self.stt(self.s2, self.modT[:, 64:80], 1.0, self.n2T, ALU.add, ALU.mult, r=[self.vecb], w=[self.vecb])
            self.sh1, self.sh2 = self.modT[:, 0:16], self.modT[:, 48:64]
            ps, pb = rot.next()
            self.tr(ps[0:96, 0:128], self.modT, self.IDF, r=[self.vecb, self.cstb], w=[pb])
            modr = sb(st, "modr", [96, 128], F32)
            mrb = Buf()
            self.cp("dve", modr[:], ps[0:96, 0:128], r=[pb], w=[mrb])
            self.dma("sp", self.modv.ap(), modr[:], r=[mrb], w=[self.scr["modv"]])
            self.P.barrier()

    def norm_transpose(self, st, x_dram, hT, hTb, s_cols, sh_cols, tag, src_tile=None):
        NT = self.NT
        xb_t = [self.sb(st, "xt%s%d" % (tag, i), [128, 2048], F32) for i in range(2)]
        xbb = [Buf(), Buf()]
        xn_t = [self.sb(st, "xn%s%d" % (tag, i), [128, 2048], BF16) for i in range(2)]
        xnb = [Buf(), Buf()]
        ss = self.sb(st, "ss" + tag, [128, 3 * NT], F32)
        ssb = [Buf() for _ in range(NT)]
        rot = self.psr([0, 1, 2, 3])
        import os as _os
        lvl = int(_os.environ.get("KNT", "9"))
        for tc in range(NT):
            xt, xtb = xb_t[tc % 2], xbb[tc % 2]
            xn, xnbb = xn_t[tc % 2], xnb[tc % 2]
            self.dma("sp", xt[:], x_dram[tc * 128:(tc + 1) * 128, :], w=[xtb])
            if lvl < 1:
                continue
            self.act(xn[:], xt[:], AF.Square, accum=ss[:, 3 * tc:3 * tc + 1], r=[xtb], w=[xnbb, ssb[tc]])
            self.rstd(ss[:, 3 * tc + 1:3 * tc + 2], ss[:, 3 * tc:3 * tc + 1], 2048.0, ss[:, 3 * tc + 2:3 * tc + 3],
                      [ssb[tc]], [ssb[tc]])
            if lvl < 2:
                continue
            self.act(xn[:], xt[:], AF.Copy, scale=ss[:, 3 * tc + 1:3 * tc + 2], r=[xtb, ssb[tc]], w=[xnbb])
            if lvl < 3:
                continue
            for half in range(2):
                ps, pb = rot.next()
                psv = ps[:].bitcast(BF16)
                for j in range(8):
                    fc = half * 8 + j
                    self.tr(psv[:, j * 128:(j + 1) * 128], xn[:, fc * 128:(fc + 1) * 128], self.IDB,
                            r=[xnbb, self.cstb], w=[pb])
                if lvl < 4:
                    continue
                for j in range(8):
                    fc = half * 8 + j
                    dst = hT[:, fc, tc * 128:(tc + 1) * 128]
                    if half == 0:
                        self.act(dst, psv[:, j * 128:(j + 1) * 128], AF.Identity, bias=sh_cols[:, fc:fc + 1],
                                 scale=s_cols[:, fc:fc + 1], r=[pb, self.vecb], w=[hTb[tc][fc]])
                    else:
                        self.ts("dve", dst, psv[:, j * 128:(j + 1) * 128], s_cols[:, fc:fc + 1], sh_cols[:, fc:fc + 1],
                                ALU.mult, ALU.add, r=[pb, self.vecb], w=[hTb[tc][fc]])

    def load_w(self, dst, w_dram, c0, W, buf, k0=0, KC=None):
        src = w_dram.rearrange("(kc p) n -> p kc n", p=128)
        if KC is None:
            self.dma("pool", dst, src[:, :, c0:c0 + W], w=[buf])
        else:
            self.dma("pool", dst, src[:, k0:k0 + KC, c0:c0 + W], w=[buf])

    def hT_reads(self, hTb, kc, tcs):
        return [hTb[tc][kc] for tc in tcs]

    def phase_inproj(self, gst):
        NT, T, NTG = self.NT, self.T, self.NTG
        sb = self.sb
        self.dtp = {}
        for name in ("de_p", "dec_p", "dt_o", "adt_o", "ea_o", "de_o", "dec_o"):
            self.dtp[name] = sb(gst, name, [128, NT, 64], F32)
        self.dtpb = {name: [Buf() for _ in range(NT)] for name in self.dtp}
        with self.scope() as st:
            hT = sb(st, "hT", [128, 16, T], BF16)
            hTb = [[Buf() for _ in range(16)] for _ in range(NT)]
            wt = [sb(st, "wblk%d" % i, [128, 16, 512], BF16) for i in range(2)]
            wtb = [Buf(), Buf()]
            for prefix in (True, False):
              with self.scope() as stn:
                self.norm_transpose(stn, self.xp if prefix else self.xo, hT, hTb, self.s1, self.sh1,
                                    "p" if prefix else "o")
                self.P.barrier()
              with self.scope() as ste:
                self.ip_tiles(ste)
                blocks = []
                if not prefix:
                    blocks.append(("tm0", [(OFF_CQ, 512, 0)]))
                blocks.append(("tm1", [(OFF_CKV, 336, 0), (OFF_DT, 64, 336)]))
                if not prefix:
                    for g in range(8):
                        blocks.append(("z", [(OFF_Z + g * 512, 512, 0)], g))
                for g in range(8):
                    blocks.append(("x", [(OFF_X + g * 512, 512, 0)], g))
                for g2 in range(2):
                    blocks.append(("B", [(OFF_B + g2 * 512, 512, 0)], g2))
                for g2 in range(2):
                    blocks.append(("C", [(OFF_C + g2 * 512, 512, 0)], g2))
                if not prefix:
                    for g in range(8):
                        blocks.append(("gate", [(OFF_GA + g * 512, 512, 0)], g))

                import os as _os
                _lim = _os.environ.get("KLIMIT")
                if _lim is not None:
                    blocks = [b_ for b_ in blocks if b_[0] in _lim.split(",")]
                if not blocks:
                    self.P.barrier()
                    continue

                def issue(i):
                    for (c0, W, d0) in blocks[i][1]:
                        self.load_w(wt[i % 2][:, :, d0:d0 + W], self.w_in, c0, W, wtb[i % 2])
                issue(0)
                for i, blk in enumerate(blocks):
                    if i + 1 < len(blocks):
                        issue(i + 1)
                    w_t, w_b = wt[i % 2], wtb[i % 2]
                    kind = blk[0]
                    if kind == "tm0":
                        self.ep_tm0(hT, hTb, w_t, w_b)
                    elif kind == "tm1":
                        self.ep_tm1(hT, hTb, w_t, w_b, prefix)
                    elif kind == "z":
                        self.ep_z(hT, hTb, w_t, w_b, blk[2])
                    elif kind == "x":
                        self.ep_conv(hT, hTb, w_t, w_b, prefix, "x", blk[2])
                    elif kind == "B":
                        self.ep_conv(hT, hTb, w_t, w_b, prefix, "B", blk[2])
                    elif kind == "C":
                        self.ep_conv(hT, hTb, w_t, w_b, prefix, "C", blk[2])
                    elif kind == "gate":
                        self.ep_gate(hT, hTb, w_t, w_b, blk[2])
                self.P.barrier()

    def ip_tiles(self, st):
        sb = self.sb
        T, NT = self.T, self.NT
        self.pre = sb(st, "pre", [128, T + 3], F32)
        self.preb = Buf()
        self.acc = sb(st, "cacc", [128, T], F32)
        self.accb = Buf()
        self.cvs = [sb(st, "cv%d" % i, [128, T], BF16) for i in range(4)]
        self.cvbs = [Buf() for _ in range(4)]
        self.xstage = sb(st, "xstage", [128, NT, 512], BF16)
        self.xstb = Buf()
        self.st512 = [sb(st, "st512_%d" % i, [128, 512], F32) for i in range(3)]
        self.st512b = [Buf() for _ in range(3)]
        self.st512r = Rot(list(zip(self.st512, self.st512b)))
        self.zst = [sb(st, "zst%d" % i, [128, 512], BF16) for i in range(3)]
        self.zstb = [Buf() for _ in range(3)]
        self.zstr = Rot(list(zip(self.zst, self.zstb)))
        self.sm = [sb(st, "smf%d" % i, [128, 512], F32) for i in range(2)]
        self.smb = [Buf(), Buf()]
        self.smh = [sb(st, "smh%d" % i, [128, 1024], BF16) for i in range(2)]
        self.smhb = [Buf(), Buf()]
        self.memset("dve", self.pre[:, 0:3], 0.0, w=[self.preb])

    def ep_tm0(self, hT, hTb, wt, wb):
        NT = self.NT
        rot = self.psr([0, 1, 2, 3])
        cqv = self.cqT_d.ap().rearrange("(fc p) t -> p fc t", p=128)
        for tc in range(NT):
            ps, pb = rot.next()
            for kc in range(16):
                self.mm(ps[:, 0:512], hT[:, kc, tc * 128:(tc + 1) * 128], wt[:, kc, 0:512], start=(kc == 0),
                        stop=(kc == 15), r=[hTb[tc][kc], wb], w=[pb])
            sm, smb = self.sm[tc % 2], self.smb[tc % 2]
            sh, shb = self.smh[tc % 2], self.smhb[tc % 2]
            self.act(sh[:, 0:512], ps[:, 0:512], AF.Square, accum=sm[:, 0:1], r=[pb], w=[shb, smb])
            self.rstd(sm[:, 1:2], sm[:, 0:1], 512.0, sm[:, 2:3], [smb], [smb])
            self.act(sh[:, 0:512], ps[:, 0:512], AF.Copy, scale=sm[:, 1:2], r=[pb, smb], w=[shb])
            ps2, pb2 = rot.next()
            p2v = ps2[:].bitcast(BF16)
            for fc in range(4):
                self.tr(p2v[:, fc * 128:(fc + 1) * 128], sh[:, fc * 128:(fc + 1) * 128], self.IDB,
                        r=[shb, self.cstb], w=[pb2])
            for fc in range(4):
                self.ts("dve", sh[:, 512 + fc * 128:512 + (fc + 1) * 128], p2v[:, fc * 128:(fc + 1) * 128],
                        self.cqgT[:, fc:fc + 1], None, ALU.mult, r=[pb2, self.vecb], w=[shb])
            self.dma("sp", cqv[:, :, tc * 128:(tc + 1) * 128],
                     sh[:, 512:1024].rearrange("p (fc t) -> p fc t", fc=4), r=[shb], w=[Buf()])

    def ep_tm1(self, hT, hTb, wt, wb, prefix):
        NT, T = self.NT, self.T
        rot = self.psr([0, 1, 2, 3])
        key0 = 0 if prefix else T
        ckvTv = self.ckvT_d.ap().rearrange("(rc p) s -> p rc s", p=128)
        D = self.dtp
        DB = self.dtpb
        for tc in range(NT):
            ps, pb = rot.next()
            for kc in range(16):
                self.mm(ps[:, 0:400], hT[:, kc, tc * 128:(tc + 1) * 128], wt[:, kc, 0:400], start=(kc == 0),
                        stop=(kc == 15), r=[hTb[tc][kc], wb], w=[pb])
            sm, smb = self.sm[tc % 2], self.smb[tc % 2]
            sh, shb = self.smh[tc % 2], self.smhb[tc % 2]
            kpos = key0 + tc * 128
            self.act(sh[:, 0:256], ps[:, 0:256], AF.Square, accum=sm[:, 0:1], r=[pb], w=[shb, smb])
            self.rstd(sm[:, 1:2], sm[:, 0:1], 256.0, sm[:, 2:3], [smb], [smb])
            self.stt(sh[:, 0:256], ps[:, 0:256], sm[:, 1:2], self.ckvg_bc, ALU.mult, ALU.mult,
                     r=[pb, smb, self.bcb], w=[shb])
            self.dma("sp", self.ckvtok_d.ap()[kpos:kpos + 128, :], sh[:, 0:256], r=[shb], w=[Buf()])
            ps2, pb2 = rot.next()
            p2v = ps2[:].bitcast(BF16)
            for rc in range(2):
                self.tr(p2v[:, rc * 128:(rc + 1) * 128], sh[:, rc * 128:(rc + 1) * 128], self.IDB,
                        r=[shb, self.cstb], w=[pb2])
            self.act(sm[:, 64:128], ps[:, 256:320], AF.Identity, accum=sm[:, 3:4], r=[pb], w=[smb])
            self.act(sm[:, 64:128], ps[:, 256:320], AF.Square, accum=sm[:, 4:5], r=[pb], w=[smb])
            self.ts("dve", sm[:, 5:6], sm[:, 3:4], 1.0 / 64, None, ALU.mult, r=[smb], w=[smb])
            self.tt("dve", sm[:, 6:7], sm[:, 5:6], sm[:, 5:6], ALU.mult, r=[smb], w=[smb])
            self.stt(sm[:, 7:8], sm[:, 4:5], 1.0 / 64, sm[:, 6:7], ALU.mult, ALU.subtract, r=[smb], w=[smb])
            self.ts("dve", sm[:, 8:9], sm[:, 7:8], EPS, None, ALU.add, r=[smb], w=[smb])
            self.act(sm[:, 8:9], sm[:, 8:9], AF.Sqrt, r=[smb], w=[smb])
            self.recip(sm[:, 9:10], sm[:, 8:9], r=[smb], w=[smb])
            self.ts("dve", sm[:, 64:128], ps[:, 256:320], sm[:, 5:6], sm[:, 9:10], ALU.subtract, ALU.mult,
                    r=[pb, smb], w=[smb])
            self.tt("dve", sm[:, 64:128], sm[:, 64:128], self.kig_bc, ALU.mult, r=[smb, self.bcb], w=[smb])
            self.tt("dve", sh[:, 256:320], sm[:, 64:128], self.kib_bc, ALU.add, r=[smb, self.bcb], w=[shb])
            self.cp("dve", sh[:, 320:384], sh[:, 256:320], r=[shb], w=[shb])
            self.tr(p2v[:, 256:384], sh[:, 256:384], self.IDB, r=[shb, self.cstb], w=[pb2])
            self.cp("act", sh[:, 512:896], p2v[:, 0:384], r=[pb2], w=[shb])
            self.dma("sp", ckvTv[:, :, kpos:kpos + 128], sh[:, 512:768].rearrange("p (rc s) -> p rc s", rc=2),
                     r=[shb], w=[Buf()])
            self.dma("sp", self.kidxT_d.ap()[:, kpos:kpos + 128], sh[:, 768:896], r=[shb], w=[Buf()])
            if not prefix:
                self.ts("dve", sm[:, 16:32], ps[:, 320:336], 1.0 / 32.0, None, ALU.mult, r=[pb], w=[smb])
                self.dma("sp", self.widx_d.ap()[tc * 128:(tc + 1) * 128, :], sm[:, 16:32], r=[smb],
                         w=[Buf()])
            dt_t = sm[:, 128:192]
            self.tt("dve", dt_t, ps[:, 336:400], self.dtb_bc, ALU.add, r=[pb, self.bcb], w=[smb])
            self.act(dt_t, dt_t, AF.Exp, r=[smb], w=[smb])
            self.act(dt_t, dt_t, AF.Ln, bias=1.0, r=[smb], w=[smb])
            adt = sm[:, 192:256]
            self.tt("dve", adt, dt_t, self.a_bc, ALU.mult, r=[smb, self.bcb], w=[smb])
            ps3, pb3 = rot.next()
            self.mm(ps3[:, 0:64], self.U, adt, r=[self.cstb, smb], w=[pb3])
            self.mm(ps3[:, 64:128], self.ONES, adt, r=[self.cstb, smb], w=[pb3])
            acs = sm[:, 256:320]
            self.cp("act", acs, ps3[:, 0:64], r=[pb3], w=[smb])
            pn = "p" if prefix else "o"
            self.act(D["dec_" + pn][:, tc, :], ps3[:, 64:128], AF.Exp, r=[pb3], w=[DB["dec_" + pn][tc]])
            self.tt("dve", sm[:, 320:384], ps3[:, 64:128], acs, ALU.subtract, r=[pb3, smb], w=[smb])
            self.act(sm[:, 320:384], sm[:, 320:384], AF.Exp, r=[smb], w=[smb])
            self.tt("dve", D["de_" + pn][:, tc, :], sm[:, 320:384], dt_t, ALU.mult, r=[smb], w=[DB["de_" + pn][tc]])
            if not prefix:
                self.cp("dve", D["dt_o"][:, tc, :], dt_t, r=[smb], w=[DB["dt_o"][tc]])
                self.cp("dve", D["adt_o"][:, tc, :], adt, r=[smb], w=[DB["adt_o"][tc]])
                self.act(D["ea_o"][:, tc, :], acs, AF.Exp, r=[smb], w=[DB["ea_o"][tc]])

    def ep_z(self, hT, hTb, wt, wb, g):
        rot = self.psr([0, 1, 2, 3])
        for tc in range(self.NT):
            ps, pb = rot.next()
            for kc in range(16):
                self.mm(ps[:, 0:512], hT[:, kc, tc * 128:(tc + 1) * 128], wt[:, kc, 0:512], start=(kc == 0),
                        stop=(kc == 15), r=[hTb[tc][kc], wb], w=[pb])
            zt, ztb = self.zstr.next()
            self.act(zt[:], ps[:, 0:512], AF.Silu, r=[pb], w=[ztb])
            self.dma("sp", self.z_d.ap()[tc * 128:(tc + 1) * 128, g * 512:(g + 1) * 512], zt[:], r=[ztb],
                     w=[Buf()])

    def ep_gate(self, hT, hTb, wt, wb, g):
        rot = self.psr([0, 1, 2, 3])
        for mc in range(4):
            for tg in range(self.NTG):
                ps, pb = rot.next()
                tcs = range(tg * 4, tg * 4 + 4)
                for kc in range(16):
                    self.mm(ps[:, 0:512], wt[:, kc, mc * 128:(mc + 1) * 128], hT[:, kc, tg * 512:(tg + 1) * 512],
                            start=(kc == 0), stop=(kc == 15), r=[wb] + self.hT_reads(hTb, kc, tcs), w=[pb])
                gt, gtb = self.st512r.next()
                self.act(gt[:], ps[:, 0:512], AF.Sigmoid, r=[pb], w=[gtb])
                row = (g * 4 + mc) * 128
                self.dma("sp", self.gT_d.ap()[row:row + 128, tg * 512:(tg + 1) * 512], gt[:], r=[gtb],
                         w=[Buf()])

    def ep_conv(self, hT, hTb, wt, wb, prefix, kind, g):
        NT, T, NTG = self.NT, self.T, self.NTG
        rot = self.psr([0, 1, 2, 3, 4, 5, 6, 7])
        row0 = 0 if prefix else T
        for cc in range(4):
            cv, cvb = self.cvs[cc], self.cvbs[cc]
            if kind == "x":
                chunk = g * 4 + cc
            elif kind == "B":
                chunk = 32 + g * 4 + cc
            else:
                chunk = 40 + g * 4 + cc
            hidx = chunk
            tgs = range(NTG)
            if kind == "C" and prefix:
                tgs = [NTG - 1]
            if not prefix:
                self.cp("dve", self.pre[:, 0:3], self.halo[:, hidx, :], r=[self.halob], w=[self.preb])
            for tg in tgs:
                ps, pb = rot.next()
                tcs = range(tg * 4, tg * 4 + 4)
                for kc in range(16):
                    self.mm(ps[:, 0:512], wt[:, kc, cc * 128:(cc + 1) * 128], hT[:, kc, tg * 512:(tg + 1) * 512],
                            start=(kc == 0), stop=(kc == 15), r=[wb] + self.hT_reads(hTb, kc, tcs), w=[pb])
                self.cp("act", self.pre[:, 3 + tg * 512:3 + (tg + 1) * 512], ps[:, 0:512], r=[pb], w=[self.preb])
            if prefix:
                self.ts("dve", self.halo[:, hidx, :], self.pre[:, T:T + 3], self.flg[:, 0:1], None, ALU.mult,
                        r=[self.preb, self.flgb], w=[self.halob])
                if kind == "C":
                    continue
            cw = self.convwT
            self.ts("dve", self.acc[:], self.pre[:, 0:T], cw[:, chunk:chunk + 1], None, ALU.mult,
                    r=[self.preb, self.vecb], w=[self.accb])
            for k in range(1, 4):
                self.stt(self.acc[:], self.pre[:, k:k + T], cw[:, k * 48 + chunk:k * 48 + chunk + 1], self.acc[:],
                         ALU.mult, ALU.add, r=[self.preb, self.vecb, self.accb], w=[self.accb])
            self.act(cv[:], self.acc[:], AF.Silu, bias=self.convbT[:, chunk:chunk + 1], r=[self.accb, self.vecb],
                     w=[cvb])
            gc = g * 4 + cc
            if kind == "C":
                self.dma("sp", self.CT_d.ap()[gc * 128:(gc + 1) * 128, :], cv[:], r=[cvb], w=[Buf()])
            elif kind == "B" and not prefix:
                self.dma("sp", self.BT_d.ap()[gc * 128:(gc + 1) * 128, :], cv[:], r=[cvb], w=[Buf()])
        if kind == "C":
            return
        for cc in range(4):
            cv, cvb = self.cvs[cc], self.cvbs[cc]
            for t4 in range(NT // 4):
                ps, pb = rot.next()
                pv = ps[:].bitcast(BF16)
                for q in range(4):
                    tc = t4 * 4 + q
                    self.tr(pv[:, q * 128:(q + 1) * 128], cv[:, tc * 128:(tc + 1) * 128], self.IDB,
                            r=[cvb, self.cstb], w=[pb])
                self.cp("dve" if t4 % 2 == 0 else "act", self.xstage[:, t4 * 4:(t4 + 1) * 4, cc * 128:(cc + 1) * 128],
                        pv[:, 0:512].rearrange("p (q c) -> p q c", q=4), r=[pb], w=[self.xstb])
        if kind == "x":
            dst = self.x_d.ap()[row0:row0 + T, g * 512:(g + 1) * 512].rearrange("(tc p) c -> p tc c", p=128)
            self.dma("sp", dst, self.xstage[:], r=[self.xstb], w=[Buf()])
        else:
            dst = self.Btok_d.ap()[row0:row0 + T, g * 512:(g + 1) * 512].rearrange("(tc p) c -> p tc c", p=128)
            self.dma("sp", dst, self.xstage[:], r=[self.xstb], w=[Buf()])

    def phase_ssd(self):
        NT, T = self.NT, self.T
        sb = self.sb
        D, DB = self.dtp, self.dtpb
        with self.scope() as st:
            self.stateT = sb(st, "stateT", [128, 4096], F32)
            self.stateb = [Buf() for _ in range(8)]
            xg = [sb(st, "xg%d" % i, [128, NT, 512], BF16) for i in range(2)]
            bt = [sb(st, "btok%d" % i, [128, NT, 128], BF16) for i in range(2)]
            zg = [sb(st, "zg%d" % i, [128, NT, 512], BF16) for i in range(2)]
            BgT = [sb(st, "BgT%d" % i, [128, T], BF16) for i in range(2)]
            CgT = [sb(st, "CgT%d" % i, [128, T], BF16) for i in range(2)]
            gsm = [sb(st, "gsm%d" % i, [128, 512], F32) for i in range(2)]
            gb = [Buf(), Buf()]
            ybst = sb(st, "ybst", [128, 4, T], BF16)
            ybb = Buf()
            def dbl(name, shape, dt):
                return [sb(st, name + str(i), shape, dt) for i in range(2)], [Buf(), Buf()]
            rseg, rsegb = dbl("rseg", [128, 8, 128], F32)
            Eb, Ebb = dbl("Eb", [128, 8, 128], BF16)
            cbm, cbmb = dbl("cbm", [128, 128], BF16)
            WT, WTb = dbl("WT", [128, 8, 128], BF16)
            xdt, xdtb = dbl("xdt", [128, 512], BF16)
            xw, xwb = dbl("xw", [128, 512], BF16)
            stb_t = sb(st, "stbf", [128, 512], BF16)
            stbb = Buf()
            y1s, y1bs = dbl("y1", [128, 512], F32)
            y2s, y2bs = dbl("y2", [128, 512], F32)
            y5s, y5bs = dbl("y5", [128, 512], BF16)
            ysms, ysmbs = dbl("ysm", [128, 8], F32)
            junks, junkbs = dbl("yjunk", [128, 512], BF16)
            rot = self.psr([0, 1, 2, 3, 4, 5, 6, 7])
            ybv = self.ybT_d.ap().rearrange("(cc p) t -> p cc t", p=128)

            for prefix in (True, False):
                row0 = 0 if prefix else T
                pn = "p" if prefix else "o"

                def load_group(g):
                    i = g % 2
                    self.dma("sp", xg[i][:], self.x_d.ap()[row0:row0 + T, g * 512:(g + 1) * 512]
                             .rearrange("(tc p) c -> p tc c", p=128), r=[self.scr["x0" if prefix else "x1"]], w=[gb[i]])
                    self.dma("sp", bt[i][:], self.Btok_d.ap()[row0:row0 + T, g * 128:(g + 1) * 128]
                             .rearrange("(tc p) c -> p tc c", p=128), r=[self.scr["Btok0" if prefix else "Btok1"]],
                             w=[gb[i]])
                    if not prefix:
                        self.dma("sp", zg[i][:], self.z_d.ap()[:, g * 512:(g + 1) * 512]
                                 .rearrange("(tc p) c -> p tc c", p=128), r=[self.scr["z"]], w=[gb[i]])
                        self.dma("sp", BgT[i][:], self.BT_d.ap()[g * 128:(g + 1) * 128, :], r=[self.scr["BT"]], w=[gb[i]])
                        self.dma("sp", CgT[i][:], self.CT_d.ap()[g * 128:(g + 1) * 128, :], r=[self.scr["CT"]], w=[gb[i]])
                        self.dma("sp", gsm[i][:], self.ssmg_d[:, g * 512:(g + 1) * 512].partition_broadcast(128),
                                 w=[gb[i]])
                load_group(0)
                for g in range(8):
                    if g + 1 < 8:
                        load_group(g + 1)
                    i = g % 2
                    G = gb[i]
                    stg = self.stateT[:, g * 512:(g + 1) * 512]
                    sbuf = self.stateb[g]
                    hs = slice(g * 8, (g + 1) * 8)
                    if prefix:
                        self.memset("dve", stg, 0.0, w=[sbuf])
                    else:
                        self.ts("dve", stg, stg, self.flg[:, 0:1], None, ALU.mult, r=[sbuf, self.flgb], w=[sbuf])
                        self.cp("act", stb_t[:], stg, r=[sbuf], w=[stbb])

                    def stage1(c):
                        k = c % 2
                        xc = xg[i][:, c, :]
                        self.tt("pool", xw[k][:].rearrange("p (h q) -> p h q", h=8),
                                xc.rearrange("p (h q) -> p h q", h=8),
                                D["de_" + pn][:, c, hs].unsqueeze(2).broadcast_to([128, 8, 64]), ALU.mult,
                                r=[G, DB["de_" + pn][c]], w=[xwb[k]])
                        if prefix:
                            return
                        self.tt("pool", y2s[k][:].rearrange("p (h q) -> p h q", h=8),
                                xc.rearrange("p (h q) -> p h q", h=8),
                                self.dsk_bc[:, hs].unsqueeze(2).broadcast_to([128, 8, 64]), ALU.mult,
                                r=[G, self.bcb], w=[y2bs[k]])
                        adt = D["adt_o"][:, c, hs]
                        self.tt("pool", rseg[k][:], self.U.unsqueeze(1).broadcast_to([128, 8, 128]),
                                adt.unsqueeze(2).broadcast_to([128, 8, 128]), ALU.mult,
                                r=[self.cstb, DB["adt_o"][c]], w=[rsegb[k]])
                        for hh in range(2):
                            psS, psSb = rot.next()
                            self.mm(psS[:, 0:512], self.L1,
                                    rseg[k][:, hh * 4:(hh + 1) * 4, :].rearrange("p h t -> p (h t)"),
                                    r=[self.cstb, rsegb[k]], w=[psSb])
                            self.act(Eb[k][:, hh * 4:(hh + 1) * 4, :].rearrange("p h t -> p (h t)"), psS[:, 0:512],
                                     AF.Exp, r=[psSb], w=[Ebb[k]])
                        psC, psCb = rot.next()
                        self.mm(psC[:, 0:128], BgT[i][:, c * 128:(c + 1) * 128], CgT[i][:, c * 128:(c + 1) * 128],
                                r=[G], w=[psCb])
                        self.tt("dve", cbm[k][:], psC[:, 0:128], self.U, ALU.mult, r=[psCb, self.cstb], w=[cbmb[k]])
                        self.tt("pool", WT[k][:], Eb[k][:], cbm[k][:].unsqueeze(1).broadcast_to([128, 8, 128]), ALU.mult,
                                r=[Ebb[k], cbmb[k]], w=[WTb[k]])
                        self.tt("dve", xdt[k][:].rearrange("p (h q) -> p h q", h=8),
                                xc.rearrange("p (h q) -> p h q", h=8),
                                D["dt_o"][:, c, hs].unsqueeze(2).broadcast_to([128, 8, 64]), ALU.mult,
                                r=[G, DB["dt_o"][c]], w=[xdtb[k]])

                    def stage2(c):
                        k = c % 2
                        xc = xg[i][:, c, :]
                        y1, y1b, y2, y2b, y5, y5b = y1s[k], y1bs[k], y2s[k], y2bs[k], y5s[k], y5bs[k]
                        ysm, ysmb, junk, junkb = ysms[k], ysmbs[k], junks[k], junkbs[k]
                        if not prefix:
                            psY, psYb = rot.next()
                            for h in range(8):
                                self.mm(psY[:, h * 64:(h + 1) * 64], WT[k][:, h, :], xdt[k][:, h * 64:(h + 1) * 64],
                                        r=[WTb[k], xdtb[k]], w=[psYb])
                            psI, psIb = rot.next()
                            self.mm(psI[:, 0:512], CgT[i][:, c * 128:(c + 1) * 128], stb_t[:], r=[G, stbb], w=[psIb])
                        psN, psNb = rot.next()
                        self.mm(psN[:, 0:512], bt[i][:, c, :], xw[k][:], r=[G, xwb[k]], w=[psNb])
                        self.tt("dve", stg.rearrange("p (h q) -> p h q", h=8), stg.rearrange("p (h q) -> p h q", h=8),
                                D["dec_" + pn][:, c, hs].unsqueeze(2).broadcast_to([128, 8, 64]), ALU.mult,
                                r=[sbuf, DB["dec_" + pn][c]], w=[sbuf])
                        self.tt("dve", stg, stg, psN[:, 0:512], ALU.add, r=[sbuf, psNb], w=[sbuf])
                        if prefix:
                            return
                        self.tt("dve", y1[:].rearrange("p (h q) -> p h q", h=8),
                                psI[:, 0:512].rearrange("p (h q) -> p h q", h=8),
                                D["ea_o"][:, c, hs].unsqueeze(2).broadcast_to([128, 8, 64]), ALU.mult,
                                r=[psIb, DB["ea_o"][c]], w=[y1b])
                        if c + 1 < NT:
                            self.cp("act", stb_t[:], stg, r=[sbuf], w=[stbb])
                        self.tt("dve", y1[:], psY[:, 0:512], y1[:], ALU.add, r=[psYb, y1b], w=[y1b])
                        self.tt("dve", y1[:], y1[:], y2[:], ALU.add, r=[y1b, y2b], w=[y1b])
                        self.tt("dve", y1[:], y1[:], zg[i][:, c, :], ALU.mult, r=[y1b, G], w=[y1b])
                        self.act(junk[:], y1[:], AF.Square, accum=ysm[:, 0:1], r=[y1b], w=[junkb, ysmb])
                        self.rstd(ysm[:, 1:2], ysm[:, 0:1], 512.0, ysm[:, 2:3], [ysmb], [ysmb])
                        self.stt(y5[:], y1[:], ysm[:, 1:2], gsm[i][:], ALU.mult, ALU.mult, r=[y1b, ysmb, G], w=[y5b])
                        psT, psTb = rot.next()
                        ptv = psT[:].bitcast(BF16)
                        for cc in range(4):
                            self.tr(ptv[:, cc * 128:(cc + 1) * 128], y5[:, cc * 128:(cc + 1) * 128], self.IDB,
                                    r=[y5b, self.cstb], w=[psTb])
                        self.cp("act", ybst[:, :, c * 128:(c + 1) * 128],
                                ptv[:, 0:512].rearrange("p (cc t) -> p cc t", cc=4), r=[psTb], w=[ybb])

                    stage1(0)
                    for c in range(NT):
                        if c + 1 < NT:
                            stage1(c + 1)
                        stage2(c)
                    if not prefix:
                        self.dma("sp", ybv[:, g * 4:(g + 1) * 4, :], ybst[:], r=[ybb], w=[Buf()])
            self.P.barrier()

    def phase_attn(self):
        NT, T, NTG = self.NT, self.T, self.NTG
        sb = self.sb
        KB = 2 * NT
        with self.scope() as st:
            ckvT = sb(st, "ckvT", [128, 2, 2 * T], BF16)
            ckvtok = sb(st, "ckvtok", [128, KB, 256], BF16)
            kidxT = sb(st, "kidxT", [128, 2 * T], BF16)
            cqT_t = sb(st, "cqT", [128, 4, 512], BF16)
            cqb = Buf()
            widx = sb(st, "widx", [128, NT, 16], F32)
            ldb = Buf()
            self.dma("sp", ckvT[:], self.ckvT_d.ap().rearrange("(rc p) s -> p rc s", p=128), r=[self.scr["ckvT"]], w=[ldb])
            self.dma("sp", ckvtok[:], self.ckvtok_d.ap().rearrange("(kb p) r -> p kb r", p=128), r=[self.scr["ckvtok"]],
                     w=[ldb])
            self.dma("sp", kidxT[:], self.kidxT_d.ap(), r=[self.scr["kidxT"]], w=[ldb])
            self.dma("sp", widx[:], self.widx_d.ap().rearrange("(tc p) h -> p tc h", p=128), r=[self.scr["widx"]], w=[ldb])
            wuq = sb(st, "wuq", [128, 4, 2048], BF16)
            wiq = sb(st, "wiq", [128, 4, 1024], BF16)
            wuv = sb(st, "wuv", [128, 2, 2048], BF16)
            wukT = sb(st, "wukT", [128, 16, 256], BF16)
            biasT = sb(st, "biasT", [128, 16, 2, 128], BF16)
            wb_ = Buf()
            self.load_w(wuq[:], self.w_uq, 0, 2048, wb_)
            self.load_w(wiq[:], self.w_iq, 0, 1024, wb_)
            self.load_w(wuv[:], self.w_uv, 0, 2048, wb_)
            rot = self.psr([0, 1, 2, 3, 4])
            wukTb = Buf()
            biasb = Buf()
            with self.scope() as stmp:
                wuk = sb(stmp, "wuk", [128, 2, 2048], BF16)
                wkb = Buf()
                self.load_w(wuk[:], self.w_uk, 0, 2048, wkb)
                for h in range(16):
                    ps, pb = rot.next()
                    pv = ps[:].bitcast(BF16)
                    for rc in range(2):
                        self.tr(pv[:, rc * 128:(rc + 1) * 128], wuk[:, rc, h * 128:(h + 1) * 128], self.IDB,
                                r=[wkb, self.cstb], w=[pb])
                    self.cp("dve" if h % 2 else "act", wukT[:, h, :], pv[:, 0:256], r=[pb], w=[wukTb])
                relb = sb(stmp, "relb", [32, 16], F32)
                ohd = sb(stmp, "ohd", [32, 384], F32)
                tvs = sb(stmp, "tvs", [16, 384], F32)
                H = sb(stmp, "Hank", [128, 16, 2, 128], F32)
                bb = Buf()
                self.dma("sp", relb[:], self.relb_d, w=[bb])
                self.dma("sp", ohd[:], self.ohd_d, w=[bb])
                ps, pb = rot.next()
                self.mm(ps[0:16, 0:384], relb[:], ohd[:], r=[bb], w=[pb])
                tvb = Buf()
                self.cp("dve", tvs[:], ps[0:16, 0:384], r=[pb], w=[tvb])
                self.dma("sp", self.tv.ap(), tvs[:], r=[tvb], w=[self.scr["tv"]])
                hb = Buf()
                self.dma("sp", H[:], bass.AP(self.tv, 0, [[1, 128], [384, 16], [128, 2], [1, 128]]), r=[self.scr["tv"]],
                         w=[hb])
                Hf = H[:].rearrange("p h k t -> p (h k t)")
                Bf = biasT[:].rearrange("p h k t -> p (h k t)")
                for q in range(8):
                    ps, pb = rot.next()
                    self.mm(ps[:, 0:512], self.J, Hf[:, q * 512:(q + 1) * 512], r=[self.cstb, hb], w=[pb])
                    self.cp("dve" if q % 2 else "act", Bf[:, q * 512:(q + 1) * 512], ps[:, 0:512], r=[pb], w=[biasb])
                self.P.barrier()

            qiT = sb(st, "qiT", [128, 8, 512], BF16)
            qiTb = Buf()
            diagw = sb(st, "diagw", [128, 16, 128], BF16)
            diagb = Buf()
            Rh = [sb(st, "Rh%d" % i, [128, 512], BF16) for i in range(3)]
            Rhb = [Buf() for _ in range(3)]
            Rrot = Rot(list(zip(Rh, Rhb)))
            scores = [sb(st, "score%d" % i, [128, 2 * T], F32) for i in range(2)]
            scbs = [Buf(), Buf()]
            negm = sb(st, "negm", [128, 2 * T], BF16)
            negb = Buf()
            negT = sb(st, "negT", [128, KB, 512], BF16)
            negTb = Buf()
            bis2 = [sb(st, "bis%d" % i, [128, 8], F32) for i in range(2)]
            bisb2 = [Buf(), Buf()]
            junk1 = [sb(st, "junk1_%d" % i, [128, 2], BF16) for i in range(2)]
            junkb1 = [Buf(), Buf()]
            qT = sb(st, "qT", [128, 512], BF16)
            qTb = Buf()
            qlT = sb(st, "qlT", [128, 2, 512], BF16)
            qlTb = Buf()
            pT = [sb(st, "pT%d" % i, [128, 512], BF16) for i in range(4)]
            pTb = [Buf() for _ in range(4)]
            prot = Rot(list(zip(pT, pTb)))
            rec = sb(st, "rec", [128, 512], F32)
            recb = Buf()
            onT = sb(st, "onT", [128, 2, 512], BF16)
            onTb = Buf()
            yast = [sb(st, "yast%d" % i, [128, 512], BF16) for i in range(1)] * 2
            yastb = [Buf()] * 2
            psO0, psO0b = self.ps[5], self.psb[5]
            psO1, psO1b = self.ps[6], self.psb[6]
            psD, psDb = self.ps[7], self.psb[7]
            arot = self.psr([5, 6])
            pfx_blocks = NT
            wsteps = [16.0 / (2 ** k) for k in range(NBIS + 1)]

            for tg in range(NTG):
                nkb = pfx_blocks + (tg + 1) * 4
                self.dma("sp", cqT_t[:], self.cqT_d.ap()[:, tg * 512:(tg + 1) * 512].rearrange("(fc p) t -> p fc t", p=128),
                         r=[self.scr["cqT"]], w=[cqb])
                for hp in range(8):
                    ps, pb = rot.next()
                    for kc in range(4):
                        self.mm(ps[:, 0:512], wiq[:, kc, hp * 128:(hp + 1) * 128], cqT_t[:, kc, :],
                                start=(kc == 0), stop=(kc == 3), r=[wb_, cqb], w=[pb])
                    self.cp("act" if hp % 2 else "dve", qiT[:, hp, :], ps[:, 0:512], r=[pb], w=[qiTb])
                def b_stage(tq):
                    tcq = tg * 4 + tq
                    score, scb = scores[tcq % 2], scbs[tcq % 2]
                    S = (pfx_blocks + tcq + 1) * 128
                    for h in range(16):
                        self.ts("pool", diagw[:, h, :], self.IDB, widx[:, tcq, h:h + 1], None, ALU.mult,
                                r=[self.cstb, ldb], w=[diagb])
                    nb5 = (S + 511) // 512
                    for k5 in range(nb5):
                        wv = min(512, S - k5 * 512)
                        psA, psAb = arot.next()

                        def idx_s(h):
                            hp, base = h // 2, (h % 2) * 64
                            psI, psIb = rot.next()
                            self.mm(psI[:, 0:wv], qiT[base:base + 64, hp, tq * 128:(tq + 1) * 128],
                                    kidxT[base:base + 64, k5 * 512:k5 * 512 + wv], r=[qiTb, ldb], w=[psIb])
                            rh, rhb = Rrot.next()
                            self.act(rh[:, 0:wv], psI[:, 0:wv], AF.Relu, r=[psIb], w=[rhb])
                            return rh, rhb
                        pend = [idx_s(0), idx_s(1)]
                        for h in range(16):
                            if h + 2 < 16:
                                pend.append(idx_s(h + 2))
                            rh, rhb = pend[h]
                            self.mm(psA[:, 0:wv], diagw[:, h, :], rh[:, 0:wv], start=(h == 0), stop=(h == 15),
                                    r=[diagb, rhb], w=[psAb])
                        dst = score[:, k5 * 512:k5 * 512 + wv]
                        if k5 * 512 < T:
                            self.act(dst, psA[:, 0:wv], AF.Identity, bias=self.flg[:, 1:2], r=[psAb, self.flgb], w=[scb])
                        else:
                            self.cp("act", dst, psA[:, 0:wv], r=[psAb], w=[scb])

                def c_pair(tqs):
                    st_ = []
                    for n_, tq in enumerate(tqs):
                        tcq = tg * 4 + tq
                        st_.append((tq, scores[tcq % 2], scbs[tcq % 2], (pfx_blocks + tcq + 1) * 128, bis2[n_], bisb2[n_]))
                    for (tq, score, scb, S, bis, bisb) in st_:
                        self.tt("dve", score[:, S - 128:S], score[:, S - 128:S], self.CM, ALU.add, r=[scb, self.cstb], w=[scb])
                        self.P.op("dve", lambda e, S=S, score=score, bis=bis: e.tensor_reduce(
                            out=bis[:, 0:1], in_=score[:, 0:S], axis=AX.X, op=ALU.max), [scb], [bisb])
                        self.ts("dve", bis[:, 1:2], bis[:, 0:1], -16.0, None, ALU.add, r=[bisb], w=[bisb])
                    for k in range(NBIS):
                        for n_, (tq, score, scb, S, bis, bisb) in enumerate(st_):
                            self.ts("dve", junk1[n_][:, 0:1].broadcast_to([128, S]), score[:, 0:S], bis[:, 1:2], 0.0,
                                    ALU.is_ge, ALU.add, accum=bis[:, 2:3], r=[scb, bisb], w=[junkb1[n_], bisb])
                        for (tq, score, scb, S, bis, bisb) in st_:
                            if k + 1 < NBIS:
                                wn = wsteps[k + 1]
                                self.ts("dve", bis[:, 3:4], bis[:, 2:3], 255.5, 2.0 * wn, ALU.is_ge, ALU.mult, r=[bisb], w=[bisb])
                            else:
                                wl = wsteps[k]
                                self.ts("dve", bis[:, 3:4], bis[:, 2:3], 255.5, wl, ALU.is_ge, ALU.mult, r=[bisb], w=[bisb])
                        for (tq, score, scb, S, bis, bisb) in st_:
                            if k + 1 < NBIS:
                                wn = wsteps[k + 1]
                                self.stt(bis[:, 1:2], bis[:, 3:4], -wn, bis[:, 1:2], ALU.add, ALU.add, r=[bisb], w=[bisb])
                            else:
                                wl = wsteps[k]
                                self.stt(bis[:, 4:5], bis[:, 3:4], -wl, bis[:, 1:2], ALU.add, ALU.add, r=[bisb], w=[bisb])
                    for (tq, score, scb, S, bis, bisb) in st_:
                        self.ts("dve", negm[:, 0:S], score[:, 0:S], bis[:, 4:5], NEG, ALU.is_lt, ALU.mult, r=[scb, bisb],
                                w=[negb])
                        nkq = S // 128
                        for k4 in range((nkq + 3) // 4):
                            n4 = min(4, nkq - k4 * 4)
                            ps, pb = rot.next()
                            pv = ps[:].bitcast(BF16)
                            for q in range(n4):
                                kb = k4 * 4 + q
                                self.tr(pv[:, q * 128:(q + 1) * 128], negm[:, kb * 128:(kb + 1) * 128], self.IDB,
                                        r=[negb, self.cstb], w=[pb])
                            self.cp("act" if k4 % 2 else "dve", negT[:, k4 * 4:k4 * 4 + n4, tq * 128:(tq + 1) * 128],
                                    pv[:, 0:n4 * 128].rearrange("p (q t) -> p q t", q=n4), r=[pb], w=[negTb])
                        if nkq < nkb:
                            self.memset("pool", negT[:, nkq:nkb, tq * 128:(tq + 1) * 128], NEG, w=[negTb])

                b_stage(0)
                b_stage(1)
                c_pair((0, 1))
                b_stage(2)
                b_stage(3)
                c_pair((2, 3))
                for h in range(16):
                    ps, pb = rot.next()
                    for kc in range(4):
                        self.mm(ps[:, 0:512], wuq[:, kc, h * 128:(h + 1) * 128], cqT_t[:, kc, :],
                                start=(kc == 0), stop=(kc == 3), r=[wb_, cqb], w=[pb])
                    self.cp("dve", qT[:], ps[:, 0:512], r=[pb], w=[qTb])
                    for rc in range(2):
                        ps, pb = rot.next()
                        self.mm(ps[:, 0:512], wukT[:, h, rc * 128:(rc + 1) * 128], qT[:], r=[wukTb, qTb], w=[pb])
                        self.act(qlT[:, rc, :], ps[:, 0:512], AF.Copy, scale=128.0 ** -0.5, r=[pb], w=[qlTb])
                    def logits(kb):
                        psL, psLb = rot.next()
                        self.mm(psL[:, 0:512], ckvT[:, 0, kb * 128:(kb + 1) * 128], qlT[:, 0, :], start=True, stop=False,
                                r=[ldb, qlTb], w=[psLb])
                        self.mm(psL[:, 0:512], ckvT[:, 1, kb * 128:(kb + 1) * 128], qlT[:, 1, :], start=False, stop=False,
                                r=[ldb, qlTb], w=[psLb])
                        extra = []
                        for tq in range(4):
                            qb = pfx_blocks + tg * 4 + tq
                            if kb == qb:
                                extra.append((tq, 0))
                            elif kb == qb - 1:
                                extra.append((tq, 1))
                        self.mm(psL[:, 0:512], self.IDB, negT[:, kb, :], start=False, stop=(len(extra) == 0),
                                r=[self.cstb, negTb], w=[psLb])
                        for ei, (tq, kind) in enumerate(extra):
                            self.mm(psL[:, tq * 128:(tq + 1) * 128], self.IDB, biasT[:, h, kind, :], start=False,
                                    stop=(ei == len(extra) - 1), r=[self.cstb, biasb], w=[psLb])
                        pt, ptb = prot.next()
                        self.act(pt[:], psL[:, 0:512], AF.Exp, r=[psLb], w=[ptb])
                        return pt, ptb
                    pendl = [logits(0)]
                    if nkb > 1:
                        pendl.append(logits(1))
                    for kb in range(nkb):
                        if kb + 2 < nkb:
                            pendl.append(logits(kb + 2))
                        pt, ptb = pendl[kb]
                        first, last = (kb == 0), (kb == nkb - 1)
                        self.mm(psO0[:, 0:512], ckvtok[:, kb, 0:128], pt[:], start=first, stop=last, r=[ldb, ptb], w=[psO0b])
                        self.mm(psO1[:, 0:512], ckvtok[:, kb, 128:256], pt[:], start=first, stop=last, r=[ldb, ptb], w=[psO1b])
                        self.mm(psD[:, 0:512], self.ONESB, pt[:], start=first, stop=last, r=[self.cstb, ptb], w=[psDb])
                    self.recip(rec[:], psD[:, 0:512], r=[psDb], w=[recb])
                    self.tt("dve", onT[:, 0, :], psO0[:, 0:512], rec[:], ALU.mult, r=[psO0b, recb], w=[onTb])
                    self.tt("dve", onT[:, 1, :], psO1[:, 0:512], rec[:], ALU.mult, r=[psO1b, recb], w=[onTb])
                    ps, pb = rot.next()
                    for rc in range(2):
                        self.mm(ps[:, 0:512], wuv[:, rc, h * 128:(h + 1) * 128], onT[:, rc, :], start=(rc == 0),
                                stop=(rc == 1), r=[wb_, onTb], w=[pb])
                    ya, yab = yast[h % 2], yastb[h % 2]
                    self.cp("act", ya[:], ps[:, 0:512], r=[pb], w=[yab])
                    self.dma("sp", self.yaT_d.ap()[h * 128:(h + 1) * 128, tg * 512:(tg + 1) * 512], ya[:], r=[yab],
                             w=[Buf()])
            self.P.barrier()

    def phase_merge_ffn(self):
        NT, T, NTG = self.NT, self.T, self.NTG
        sb = self.sb
        mv = self.modv.ap()
        rot = self.psr([0, 1, 2, 3, 4, 5, 6, 7])
        dsrc = self.w_d.rearrange("(kc p) n -> p kc n", p=128)
        with self.scope() as st:
            x1 = sb(st, "x1", [128, 4, 2048], F32)
            x1b = [Buf() for _ in range(4)]
            h2T = sb(st, "h2T", [128, 16, 512], BF16)
            h2Tb = [[Buf() for _ in range(16)] for _ in range(4)]
            xn2 = sb(st, "xn2", [128, 2048], BF16)
            xn2b = Buf()
            mtmp = sb(st, "mtmp", [128, 512], F32)
            mtb = Buf()
            fsm = sb(st, "fsm", [128, 16], F32)
            fsmb = Buf()
            wA = [sb(st, "wA%d" % i, [128, 16, 512], BF16) for i in range(2)]
            wAb = [Buf(), Buf()]
            for tg in range(NTG):
                tsl = slice(tg * 512, (tg + 1) * 512)
                for q in range(4):
                    self.dma("sp", x1[:, q, :], self.xo[tg * 512 + q * 128:tg * 512 + (q + 1) * 128, :], w=[x1b[q]])
                with self.scope() as sm_:
                    g1bc = sb(sm_, "g1bc", [128, 2048], F32)
                    gbb = Buf()
                    self.dma("sp", g1bc[:], mv[32:48, :].rearrange("c p -> (c p)").partition_broadcast(128),
                             r=[self.scr["modv"]], w=[gbb])
                    yaT = sb(sm_, "yaT", [128, 16, 512], BF16)
                    ybT = sb(sm_, "ybT", [128, 32, 512], BF16)
                    yb_ = Buf()
                    mT = sb(sm_, "mT", [128, 16, 512], BF16)
                    mTb = [Buf() for _ in range(16)]
                    wB = [sb(sm_, "wB%d" % i, [128, 32, 128], BF16) for i in range(2)]
                    wBb = [Buf(), Buf()]
                    gt = [sb(sm_, "gt%d" % i, [128, 2, 512], F32) for i in range(2)]
                    gtb = [Buf(), Buf()]
                    self.dma("sp", yaT[:], self.yaT_d.ap()[:, tsl].rearrange("(kc p) t -> p kc t", p=128),
                             r=[self.scr["yaT"]], w=[yb_])
                    self.dma("sp", ybT[:], self.ybT_d.ap()[:, tsl].rearrange("(kc p) t -> p kc t", p=128),
                             r=[self.scr["ybT"]], w=[yb_])

                    def issue_m(i):
                        self.load_w(wA[i % 2][:, :, 0:128], self.w_pa, i * 128, 128, wAb[i % 2])
                        self.load_w(wB[i % 2][:], self.w_pb, i * 128, 128, wBb[i % 2])
                    issue_m(0)
                    for mc in range(16):
                        i = mc
                        if i + 1 < 16:
                            issue_m(i + 1)
                        g_t, g_b = gt[mc % 2], gtb[mc % 2]
                        self.dma("sp", g_t[:, 0, :], self.gT_d.ap()[mc * 128:(mc + 1) * 128, tsl], r=[self.scr["gT"]],
                                 w=[g_b])
                        self.dma("sp", g_t[:, 1, :], self.gT_d.ap()[2048 + mc * 128:2048 + (mc + 1) * 128, tsl],
                                 r=[self.scr["gT"]], w=[g_b])
                        psa, psab = rot.next()
                        for kc in range(16):
                            self.mm(psa[:, 0:512], wA[i % 2][:, kc, 0:128], yaT[:, kc, :],
                                    start=(kc == 0), stop=(kc == 15), r=[wAb[i % 2], yb_], w=[psab])
                        psb_, psbb = rot.next()
                        for kc in range(32):
                            self.mm(psb_[:, 0:512], wB[i % 2][:, kc, :], ybT[:, kc, :],
                                    start=(kc == 0), stop=(kc == 31), r=[wBb[i % 2], yb_], w=[psbb])
                        self.tt("dve", mtmp[:], psa[:, 0:512], g_t[:, 0, :], ALU.mult, r=[psab, g_b], w=[mtb])
                        self.tt("dve", g_t[:, 1, :], psb_[:, 0:512], g_t[:, 1, :], ALU.mult, r=[psbb, g_b], w=[g_b])
                        self.tt("pool", mT[:, mc, :], mtmp[:], g_t[:, 1, :], ALU.add, r=[mtb, g_b], w=[mTb[mc]])

                    def issue_o(i):
                        self.load_w(wA[i % 2][:], self.w_o, i * 512, 512, wAb[i % 2])
                    issue_o(0)
                    for i in range(4):
                        if i + 1 < 4:
                            issue_o(i + 1)
                        for q in range(4):
                            ps, pb = rot.next()
                            for kc in range(16):
                                self.mm(ps[:, 0:512], mT[:, kc, q * 128:(q + 1) * 128], wA[i % 2][:, kc, :],
                                        start=(kc == 0), stop=(kc == 15), r=[mTb[kc], wAb[i % 2]], w=[pb])
                            self.tt("dve", mtmp[:], ps[:, 0:512], g1bc[:, i * 512:(i + 1) * 512], ALU.mult, r=[pb, gbb],
                                    w=[mtb])
                            self.tt("pool", x1[:, q, i * 512:(i + 1) * 512], x1[:, q, i * 512:(i + 1) * 512], mtmp[:],
                                    ALU.add, r=[x1b[q], mtb], w=[x1b[q]])
                    self.P.barrier()
                for q in range(4):
                    self.act(xn2[:], x1[:, q, :], AF.Square, accum=fsm[:, 0:1], r=[x1b[q]], w=[xn2b, fsmb])
                    self.rstd(fsm[:, 1:2], fsm[:, 0:1], 2048.0, fsm[:, 2:3], [fsmb], [fsmb])
                    self.act(xn2[:], x1[:, q, :], AF.Copy, scale=fsm[:, 1:2], r=[x1b[q], fsmb], w=[xn2b])
                    for half in range(2):
                        ps, pb = rot.next()
                        pv = ps[:].bitcast(BF16)
                        for j in range(8):
                            fc = half * 8 + j
                            self.tr(pv[:, j * 128:(j + 1) * 128], xn2[:, fc * 128:(fc + 1) * 128], self.IDB,
                                    r=[xn2b, self.cstb], w=[pb])
                        for j in range(8):
                            fc = half * 8 + j
                            dst = h2T[:, fc, q * 128:(q + 1) * 128]
                            if half == 0:
                                self.act(dst, pv[:, j * 128:(j + 1) * 128], AF.Identity, bias=self.sh2[:, fc:fc + 1],
                                         scale=self.s2[:, fc:fc + 1], r=[pb, self.vecb], w=[h2Tb[q][fc]])
                            else:
                                self.ts("dve", dst, pv[:, j * 128:(j + 1) * 128], self.s2[:, fc:fc + 1],
                                        self.sh2[:, fc:fc + 1], ALU.mult, ALU.add, r=[pb, self.vecb], w=[h2Tb[q][fc]])
                with self.scope() as sf_:
                    g2bc = sb(sf_, "g2bc", [128, 2048], F32)
                    gbb = Buf()
                    self.dma("sp", g2bc[:], mv[80:96, :].rearrange("c p -> (c p)").partition_broadcast(128),
                             r=[self.scr["modv"]], w=[gbb])
                    aT = sb(sf_, "aT", [128, 44, 512], BF16)
                    aTb = [Buf() for _ in range(44)]
                    sg = sb(sf_, "sg", [128, 512], F32)
                    sgb = Buf()
                    wD = [sb(sf_, "wD%d" % i, [128, 44, 256], BF16) for i in range(2)]
                    wDb = [Buf(), Buf()]

                    def issue_f(i):
                        self.load_w(wA[i % 2][:, :, 0:256], self.w_g, i * 256, 256, wAb[i % 2])
                        self.load_w(wA[i % 2][:, :, 256:512], self.w_u, i * 256, 256, wAb[i % 2])

                    def issue_d(i):
                        self.dma("pool", wD[i % 2][:], dsrc[:, :, i * 256:(i + 1) * 256], w=[wDb[i % 2]])
                    issue_f(0)
                    for i in range(22):
                        if i + 1 < 22:
                            issue_f(i + 1)
                        elif True:
                            issue_d(0)
                        for sub in range(2):
                            fcx = i * 2 + sub
                            hr = [h2Tb[q][kc] for q in range(4) for kc in range(16)]
                            psg, psgb = rot.next()
                            for kc in range(16):
                                self.mm(psg[:, 0:512], wA[i % 2][:, kc, sub * 128:(sub + 1) * 128], h2T[:, kc, :],
                                        start=(kc == 0), stop=(kc == 15), r=[wAb[i % 2]] + (hr if kc == 0 else []),
                                        w=[psgb])
                            psu, psub = rot.next()
                            for kc in range(16):
                                self.mm(psu[:, 0:512], wA[i % 2][:, kc, 256 + sub * 128:256 + (sub + 1) * 128],
                                        h2T[:, kc, :], start=(kc == 0), stop=(kc == 15), r=[wAb[i % 2]], w=[psub])
                            self.act(sg[:], psg[:, 0:512], AF.Silu, r=[psgb], w=[sgb])
                            self.tt("dve", aT[:, fcx, :], psu[:, 0:512], sg[:], ALU.mult, r=[psub, sgb], w=[aTb[fcx]])
                    for i in range(8):
                        if i + 1 < 8:
                            issue_d(i + 1)
                        for q in range(4):
                            ps, pb = rot.next()
                            for kc in range(44):
                                self.mm(ps[:, 0:256], aT[:, kc, q * 128:(q + 1) * 128], wD[i % 2][:, kc, :],
                                        start=(kc == 0), stop=(kc == 43), r=[aTb[kc], wDb[i % 2]], w=[pb])
                            self.tt("dve", mtmp[:, 0:256], ps[:, 0:256], g2bc[:, i * 256:(i + 1) * 256], ALU.mult,
                                    r=[pb, gbb], w=[mtb])
                            self.tt("pool", x1[:, q, i * 256:(i + 1) * 256], x1[:, q, i * 256:(i + 1) * 256],
                                    mtmp[:, 0:256], ALU.add, r=[x1b[q], mtb], w=[x1b[q]])
                    self.P.barrier()
                with self.scope() as so_:
                    fgbc = sb(so_, "fgbc", [128, 2048], F32)
                    gbb = Buf()
                    self.dma("sp", fgbc[:], self.fg_d.partition_broadcast(128), w=[gbb])
                    osb = [sb(so_, "osb%d" % i, [128, 2048], F32) for i in range(2)]
                    osbb = [Buf(), Buf()]
                    for q in range(4):
                        ob, obb = osb[q % 2], osbb[q % 2]
                        self.act(xn2[:], x1[:, q, :], AF.Square, accum=fsm[:, 4:5], r=[x1b[q]], w=[xn2b, fsmb])
                        self.rstd(fsm[:, 5:6], fsm[:, 4:5], 2048.0, fsm[:, 6:7], [fsmb], [fsmb])
                        self.stt(ob[:], x1[:, q, :], fsm[:, 5:6], fgbc[:], ALU.mult, ALU.mult, r=[x1b[q], fsmb, gbb],
                                 w=[obb])
                        self.dma("sp", self.y[tg * 512 + q * 128:tg * 512 + (q + 1) * 128, :], ob[:], r=[obb],
                                 w=[Buf()])
                    self.P.barrier()


def _t5_bucket(d):
    n = np.maximum(d, 0)
    nf = np.maximum(n, 1).astype(np.float32)
    large = 16 + (np.log(nf / np.float32(16)) / np.float32(np.log(128 / 16)) * np.float32(16)).astype(np.int32)
    large = np.minimum(large, 31)
    return np.where(n < 16, n, large)


def _consts():
    cst = np.zeros((128, 768), np.float32)
    i = np.arange(128)
    cst[:, 0:128] = np.eye(128)
    cst[:, 128:256] = (i[:, None] <= i[None, :])
    cst[:, 256:384] = (i[:, None] > i[None, :])
    cst[:, 384:512] = np.eye(128)[::-1]
    cst[:, 512:640] = np.where(i[None, :] <= i[:, None], 0.0, -1e30)
    cst[:, 640:768] = 1.0
    ohd = np.zeros((32, 384), np.float32)
    for j in range(384):
        d = j - 127
        if 0 <= d <= 255:
            ohd[int(_t5_bucket(np.array(d))), j] += 1.0
            ohd[31, j] -= 1.0
    return cst, ohd


_CACHE = {}


def make_in_maps(inputs, NT, n_seq):
    T = NT * 128
    f = lambda a: np.ascontiguousarray(np.asarray(a, dtype=np.float32))
    x = f(inputs["x"])
    c = f(inputs["c"])
    cst, ohd = _consts()
    shared = {
        "w_ada": f(inputs["w_ada"][0]), "b_ada": f(inputs["b_ada"][0]).reshape(96, 128),
        "norm1_g": f(inputs["norm1_g"][0]).reshape(16, 128), "w_in": f(inputs["w_in"][0]),
        "cq_g": f(inputs["cq_norm_g"][0]).reshape(4, 128), "ckv_g": f(inputs["ckv_norm_g"][0]).reshape(1, 256),
        "kidx_g": f(inputs["kidx_norm_g"][0]).reshape(1, 64), "kidx_b": f(inputs["kidx_norm_b"][0]).reshape(1, 64),
        "w_uq": f(inputs["w_uq"][0]), "w_iq": f(inputs["w_iq"][0]), "w_uk": f(inputs["w_uk"][0]),
        "w_uv": f(inputs["w_uv"][0]), "rel_bias": f(inputs["rel_bias"]),
        "conv_w": f(inputs["conv_w"][0]).reshape(192, 128), "conv_b": f(inputs["conv_b"][0]).reshape(48, 128),
        "dt_bias": f(inputs["dt_bias"][0]).reshape(1, 64), "a_log": f(inputs["a_log"][0]).reshape(1, 64),
        "d_skip": f(inputs["d_skip"][0]).reshape(1, 64), "ssm_g": f(inputs["ssm_norm_g"][0]).reshape(1, 4096),
        "w_proj_a": f(inputs["w_proj_a"][0]), "w_proj_b": f(inputs["w_proj_b"][0]), "w_out": f(inputs["w_out"][0]),
        "norm2_g": f(inputs["norm2_g"][0]).reshape(16, 128), "w_gate": f(inputs["w_gate"][0]),
        "w_up": f(inputs["w_up"][0]), "w_down": f(inputs["w_down"][0]), "final_g": f(inputs["final_g"]).reshape(1, 2048),
        "cst": cst, "ohd": ohd,
    }
    maps = []
    for core in range(2 * n_seq):
        b, j = core // 2, core % 2
        flg = np.zeros((128, 2), np.float32)
        flg[:, 0] = float(j)
        flg[:, 1] = 0.0 if j == 1 else -1e30
        m = dict(shared)
        m["xo"] = np.ascontiguousarray(x[b, j * T:(j + 1) * T])
        m["xp"] = np.ascontiguousarray(x[b, 0:T])
        m["cb"] = np.ascontiguousarray(c[b].reshape(16, 128))
        m["flg"] = flg
        maps.append(m)
    return maps


def run(inputs, NT, debug=(), stop=99, n_seq=4):
    key = (NT, tuple(debug), stop)
    if key not in _CACHE:
        bld = Builder(NT, debug, stop)
        nc = bld.build()
        _CACHE[key] = (bld, nc)
    bld, nc = _CACHE[key]
    maps = make_in_maps(inputs, NT, n_seq)
    res = run_bass_kernel_spmd(nc, maps, core_ids=list(range(2 * n_seq)))
    T = NT * 128
    out = np.zeros((4, 2 * T, 2048), np.float32)
    for core in range(2 * n_seq):
        b, j = core // 2, core % 2
        out[b, j * T:(j + 1) * T] = res.results[core]["y"]
    return out, res


def kernel(**inputs):
    out, _ = run(inputs, 16)
    return out
```

```python
import numpy as np
from contextlib import ExitStack
import concourse.bass as bass
import concourse.mybir as mybir
from concourse.bass_utils import run_bass_kernel_spmd

F32 = mybir.dt.float32
BF16 = mybir.dt.bfloat16
AF = mybir.ActivationFunctionType
ALU = mybir.AluOpType
AX = mybir.AxisListType

STREAMS = ("pe", "act", "dve", "pool", "sp")
N_DMA_SEMS = 10
EPS = 1e-6

OFF_CQ, OFF_CKV, OFF_KIDX, OFF_WIDX, OFF_Z, OFF_X, OFF_B, OFF_C, OFF_DT, OFF_GA, OFF_GB = (
    0, 512, 768, 832, 848, 4944, 9040, 10064, 11088, 11152, 13200)
NEG = -30000.0
NBIS = 17


class Buf:
    __slots__ = ("w", "r", "excl")

    def __init__(self, excl=False):
        self.w = None
        self.r = {}
        self.excl = excl


class Prog:
    def __init__(self, nc, stack):
        self.nc = nc
        self.ops = {s: [] for s in STREAMS}
        self.cnt = {s: 0 for s in STREAMS}
        self.waited = {s: {} for s in STREAMS}
        self.sems = {}
        for s in STREAMS:
            self.sems[("eng", s)] = stack.enter_context(nc.semaphore("s_" + s))
        self.dma_i = {}
        self.dma_target = {}
        for s in ("sp", "pool", "act"):
            self.dma_i[s] = 0
            for k in range(N_DMA_SEMS):
                key = ("dma", s, k)
                self.sems[key] = stack.enter_context(nc.semaphore("d_%s%d" % (s, k)))
                self.dma_target[key] = 0

    def _collect(self, stream, reads, writes):
        deps = []
        pe = stream == "pe"
        for b in reads:
            if b.w is not None and not (pe and b.w[2] == "pe"):
                deps.append(b.w)
            if b.excl:
                for tok in b.r.values():
                    if tok[2] != stream:
                        deps.append(tok)
        for b in writes:
            if b.w is not None and not (pe and b.w[2] == "pe"):
                deps.append(b.w)
            for tok in b.r.values():
                if not (pe and tok[2] == "pe"):
                    deps.append(tok)
        return deps

    def _waits(self, stream, deps):
        best = {}
        for (key, val, _p) in deps:
            if val > best.get(key, 0):
                best[key] = val
        out = []
        wd = self.waited[stream]
        for key, val in best.items():
            if wd.get(key, 0) >= val:
                continue
            wd[key] = val
            out.append((key, val))
        return out

    def _update(self, tok, reads, writes):
        for b in reads:
            b.r[tok[2]] = tok
        for b in writes:
            b.w = tok
            b.r = {}

    def op(self, stream, fn, reads=(), writes=()):
        deps = self._collect(stream, reads, writes)
        waits = self._waits(stream, deps)
        self.cnt[stream] += 1
        key = ("eng", stream)
        tok = (key, self.cnt[stream], stream)
        self.ops[stream].append((waits, fn, key, 1))
        self._update(tok, reads, writes)
        return tok

    def dma(self, stream, fn, reads=(), writes=()):
        i = self.dma_i[stream]
        self.dma_i[stream] = i + 1
        key = ("dma", stream, i % N_DMA_SEMS)
        deps = self._collect("dma:" + stream, reads, writes)
        prev = self.dma_target[key]
        if prev > 0:
            deps.append((key, prev, "x"))
        waits = self._waits(stream, deps)
        self.dma_target[key] = prev + 16
        tok = (key, prev + 16, "dma:%s%d" % (stream, i % N_DMA_SEMS))
        self.ops[stream].append((waits, fn, key, 16))
        self._update(tok, reads, writes)
        return tok

    def wait_all(self, stream, bufs):
        deps = [b.w for b in bufs if b.w is not None]
        waits = self._waits(stream, deps)
        if waits:
            self.ops[stream].append((waits, None, None, 0))

    def barrier(self):
        deps = [(("eng", s), self.cnt[s], s) for s in STREAMS if self.cnt[s] > 0]
        deps += [(k, v, "x") for k, v in self.dma_target.items() if v > 0]
        for s in STREAMS:
            waits = self._waits(s, [d for d in deps if d[0] != ("eng", s)])
            if waits:
                self.ops[s].append((waits, None, None, 0))

    def emit(self):
        nc = self.nc
        sems = self.sems

        def run(stream, eng):
            for (waits, fn, key, inc) in self.ops[stream]:
                for (wkey, val) in waits:
                    eng.wait_ge(sems[wkey], val)
                if fn is not None:
                    fn(eng).then_inc(sems[key], inc)

        with nc.Block() as block:
            @block.tensor
            def _(e):
                run("pe", e)

            @block.scalar
            def _(e):
                run("act", e)

            @block.vector
            def _(e):
                run("dve", e)

            @block.gpsimd
            def _(e):
                run("pool", e)

            @block.sync
            def _(e):
                run("sp", e)


class Scope:
    def __init__(self, bld):
        self.bld = bld

    def __enter__(self):
        self.mark = self.bld.arena_ptr
        return self

    def __exit__(self, *a):
        self.bld.arena_ptr = self.mark
        return False

    def alloc(self, shape, dt):
        bld = self.bld
        esz = 4 if dt == F32 else 2
        n = 1
        for d in shape[1:]:
            n *= d
        nbytes = (n * esz + 63) // 64 * 64
        off = bld.arena_ptr
        bld.arena_ptr = off + nbytes
        bld.arena_peak = max(bld.arena_peak, bld.arena_ptr)
        assert bld.arena_ptr <= bld.ARENA, "SBUF arena overflow %d" % bld.arena_ptr
        ap = bld.arena[0:shape[0], off:off + n * esz].bitcast(dt)
        if len(shape) == 3:
            ap = ap.rearrange("p (a b) -> p a b", a=shape[1])
        elif len(shape) == 4:
            ap = ap.rearrange("p (a b c) -> p a b c", a=shape[1], b=shape[2])
        return ap


class Rot:
    def __init__(self, items):
        self.items = items
        self.i = 0

    def next(self):
        it = self.items[self.i % len(self.items)]
        self.i += 1
        return it


class Builder:
    def __init__(self, NT, debug=(), stop=99):
        self.stop = stop
        self.NT = NT
        self.T = NT * 128
        self.NTG = NT // 4
        self.debug = set(debug)
        self.dbg_out = {}
        self.nc = bass.Bass("TRN2", target_bir_lowering=False)

    def mm(self, out, lhsT, rhs, start=True, stop=True, r=(), w=()):
        self.P.op("pe", lambda e: e.matmul(out, lhsT=lhsT, rhs=rhs, start=start, stop=stop), r, w)

    def tr(self, out, in_, ident, r=(), w=()):
        self.P.op("pe", lambda e: e.transpose(out, in_, ident), r, w)

    def act(self, out, in_, func, bias=None, scale=None, accum=None, r=(), w=()):
        kw = {}
        if bias is not None:
            kw["bias"] = bias
        if scale is not None:
            kw["scale"] = scale
        if accum is not None:
            kw["accum_out"] = accum
        self.P.op("act", lambda e: e.activation(out=out, in_=in_, func=func, **kw), r, w)

    def ts(self, eng, out, in0, s1, s2, op0, op1=None, accum=None, r=(), w=()):
        kw = {}
        if op1 is not None:
            kw["op1"] = op1
        if accum is not None:
            kw["accum_out"] = accum
        self.P.op(eng, lambda e: e.tensor_scalar(out=out, in0=in0, scalar1=s1, scalar2=s2, op0=op0, **kw), r, w)

    def tt(self, eng, out, in0, in1, op, r=(), w=()):
        self.P.op(eng, lambda e: e.tensor_tensor(out=out, in0=in0, in1=in1, op=op), r, w)

    def stt(self, out, in0, scalar, in1, op0, op1, r=(), w=()):
        self.P.op("dve", lambda e: e.scalar_tensor_tensor(out=out, in0=in0, scalar=scalar, in1=in1,
                                                          op0=op0, op1=op1), r, w)

    def cp(self, eng, out, in_, r=(), w=()):
        if eng == "act":
            self.P.op("act", lambda e: e.activation(out=out, in_=in_, func=AF.Copy), r, w)
        else:
            self.P.op(eng, lambda e: e.tensor_copy(out, in_), r, w)

    def memset(self, eng, ap, val, w=()):
        self.P.op(eng, lambda e: e.memset(ap, val), (), w)

    def dma(self, q, out, in_, r=(), w=()):
        self.P.dma(q, lambda e: e.dma_start(out=out, in_=in_), r, w)

    def recip(self, out, in_, r=(), w=()):
        self.P.op("dve", lambda e: e.reciprocal(out=out, in_=in_), r, w)

    def sb(self, st, name, shape, dt):
        return st.alloc(shape, dt)

    def scope(self):
        return Scope(self)

    def din(self, name, shape, dt=F32):
        return self.nc.dram_tensor(name, list(shape), dt, kind="ExternalInput").ap()

    def dscr(self, name, shape, dt):
        kind = "ExternalOutput" if name in self.debug else "Internal"
        t = self.nc.dram_tensor(name, list(shape), dt, kind=kind)
        if name in self.debug:
            self.dbg_out[name] = (list(shape), dt)
        return t

    def rstd(self, out, ss, n, tmp, bufs_r, bufs_w):
        self.ts("dve", tmp, ss, 1.0 / n, EPS, ALU.mult, ALU.add, r=bufs_r, w=bufs_w)
        self.act(tmp, tmp, AF.Sqrt, r=bufs_w, w=bufs_w)
        self.recip(out, tmp, r=bufs_w, w=bufs_w)

    def build(self):
        nc = self.nc
        NT, T, NTG = self.NT, self.T, self.NTG
        din = self.din
        self.xo = din("xo", [T, 2048])
        self.xp = din("xp", [T, 2048])
        self.cb_d = din("cb", [16, 128])
        self.flg_d = din("flg", [128, 2])
        self.w_ada = din("w_ada", [2048, 12288])
        self.b_ada = din("b_ada", [96, 128])
        self.n1_d = din("norm1_g", [16, 128])
        self.w_in = din("w_in", [2048, 15248])
        self.cqg_d = din("cq_g", [4, 128])
        self.ckvg_d = din("ckv_g", [1, 256])
        self.kig_d = din("kidx_g", [1, 64])
        self.kib_d = din("kidx_b", [1, 64])
        self.w_uq = din("w_uq", [512, 2048])
        self.w_iq = din("w_iq", [512, 1024])
        self.w_uk = din("w_uk", [256, 2048])
        self.w_uv = din("w_uv", [256, 2048])
        self.relb_d = din("rel_bias", [32, 16])
        self.convw_d = din("conv_w", [192, 128])
        self.convb_d = din("conv_b", [48, 128])
        self.dtb_d = din("dt_bias", [1, 64])
        self.alog_d = din("a_log", [1, 64])
        self.dsk_d = din("d_skip", [1, 64])
        self.ssmg_d = din("ssm_g", [1, 4096])
        self.w_pa = din("w_proj_a", [2048, 2048])
        self.w_pb = din("w_proj_b", [4096, 2048])
        self.w_o = din("w_out", [2048, 2048])
        self.n2_d = din("norm2_g", [16, 128])
        self.w_g = din("w_gate", [2048, 5632])
        self.w_u = din("w_up", [2048, 5632])
        self.w_d = din("w_down", [5632, 2048])
        self.fg_d = din("final_g", [1, 2048])
        self.cst_d = din("cst", [128, 768])
        self.ohd_d = din("ohd", [32, 384])
        self.y = nc.dram_tensor("y", [T, 2048], F32, kind="ExternalOutput").ap()

        self.modv = self.dscr("modv", [96, 128], F32)
        self.tv = self.dscr("tv", [16, 384], F32)
        self.ckvT_d = self.dscr("ckvT_d", [256, 2 * T], BF16)
        self.ckvtok_d = self.dscr("ckvtok_d", [2 * T, 256], BF16)
        self.kidxT_d = self.dscr("kidxT_d", [128, 2 * T], BF16)
        self.cqT_d = self.dscr("cqT_d", [512, T], BF16)
        self.widx_d = self.dscr("widx_d", [T, 16], F32)
        self.x_d = self.dscr("x_d", [2 * T, 4096], BF16)
        self.z_d = self.dscr("z_d", [T, 4096], BF16)
        self.BT_d = self.dscr("BT_d", [1024, T], BF16)
        self.CT_d = self.dscr("CT_d", [1024, T], BF16)
        self.Btok_d = self.dscr("Btok_d", [2 * T, 1024], BF16)
        self.gT_d = self.dscr("gT_d", [4096, T], F32)
        self.yaT_d = self.dscr("yaT_d", [2048, T], BF16)
        self.ybT_d = self.dscr("ybT_d", [4096, T], BF16)
        self.scr = {n: Buf() for n in ("modv", "tv", "ckvT", "ckvtok", "kidxT", "cqT", "widx", "x0", "x1", "z",
                                       "BT", "CT", "Btok0", "Btok1", "gT", "yaT", "ybT")}
        self.outb = Buf()

        with ExitStack() as gst0:
            self.P = Prog(nc, gst0)
            self.ps = [gst0.enter_context(nc.psum_tensor("ps%d" % i, [128, 512], F32)) for i in range(8)]
            self.psb = [Buf(excl=True) for _ in range(8)]
            self.ARENA = 207 * 1024
            self.arena = gst0.enter_context(nc.sbuf_tensor("arena", [128, self.ARENA], mybir.dt.uint8))
            self.arena_ptr = 0
            self.arena_peak = 0
            self.gst = Scope(self)
            self.gst.__enter__()
            self.phase0()
            with self.scope() as s1:
                if self.stop >= 1:
                    self.phase_inproj(s1)
                if self.stop >= 2:
                    self.phase_ssd()
            if self.stop >= 3:
                self.phase_attn()
            if self.stop >= 4:
                self.phase_merge_ffn()
            self.P.wait_all("sp", [self.outb])
            self.P.barrier()
            self.P.emit()
        return nc

    def psr(self, idxs):
        return Rot([(self.ps[i], self.psb[i]) for i in idxs])

    def load_cols(self, st, rows_ap, R, out_ap, out_buf, name, psrot):
        tmp = self.sb(st, "lc_" + name, [R, 128], F32)
        tb = Buf()
        self.dma("sp", tmp[:], rows_ap, w=[tb])
        ps, pb = psrot.next()
        self.tr(ps[:, 0:R], tmp[:], self.IDF[0:R, 0:R], r=[tb, self.cstb], w=[pb])
        self.cp("dve", out_ap, ps[:, 0:R], r=[pb], w=[out_buf])

    def phase0(self):
        nc, gst = self.nc, self.gst
        sb = self.sb
        self.cstf = sb(gst, "cstf", [128, 768], F32)
        self.cstb = Buf()
        self.cstbf = sb(gst, "cstbf", [128, 768], BF16)
        self.cstbb = self.cstb
        self.dma("sp", self.cstf[:], self.cst_d, w=[self.cstb])
        self.cp("dve", self.cstbf[:], self.cstf[:], r=[self.cstb], w=[self.cstb])
        c = self.cstf
        self.IDF, self.U, self.L1, self.J, self.CM, self.ONES = (c[:, 0:128], c[:, 128:256], c[:, 256:384],
                                                                 c[:, 384:512], c[:, 512:640], c[:, 640:768])
        cbf = self.cstbf
        self.IDB, self.UB, self.ONESB = cbf[:, 0:128], cbf[:, 128:256], cbf[:, 640:768]
        self.vec = sb(gst, "vec", [128, 512], F32)
        self.vecb = Buf()
        v = self.vec
        self.modT = v[:, 0:96]
        self.n1T, self.n2T = v[:, 96:112], v[:, 112:128]
        self.s1, self.s2 = v[:, 128:144], v[:, 144:160]
        self.cqgT = v[:, 160:164]
        self.convbT = v[:, 164:212]
        self.convwT = v[:, 212:404]
        self.cT = v[:, 404:420]
        self.badaT = v[:, 420:516] if False else None
        self.flg = sb(gst, "flgt", [128, 2], F32)
        self.flgb = Buf()
        self.dma("sp", self.flg[:], self.flg_d, w=[self.flgb])
        self.bc = sb(gst, "bct", [128, 256 + 64 * 5], F32)
        self.bcb = Buf()
        b = self.bc
        self.ckvg_bc, self.kig_bc, self.kib_bc = b[:, 0:256], b[:, 256:320], b[:, 320:384]
        self.dtb_bc, self.a_bc, self.dsk_bc = b[:, 384:448], b[:, 448:512], b[:, 512:576]
        for ap, src in ((self.ckvg_bc, self.ckvg_d), (self.kig_bc, self.kig_d), (self.kib_bc, self.kib_d),
                        (self.dtb_bc, self.dtb_d), (self.a_bc, self.alog_d), (self.dsk_bc, self.dsk_d)):
            self.dma("sp", ap, src.partition_broadcast(128), w=[self.bcb])
        self.act(self.a_bc, self.a_bc, AF.Exp, r=[self.bcb], w=[self.bcb])
        self.ts("dve", self.a_bc, self.a_bc, -1.0, None, ALU.mult, r=[self.bcb], w=[self.bcb])
        self.c_actT = sb(gst, "c_actT", [128, 16], BF16)
        self.halo = sb(gst, "halo", [128, 48, 3], F32)
        self.halob = Buf()

        with self.scope() as st:
            rot = self.psr([0, 1, 2, 3])
            badaT = sb(st, "badaT", [128, 96], F32)
            bb = Buf()
            self.load_cols(st, self.b_ada, 96, badaT[:], bb, "bada", rot)
            self.load_cols(st, self.n1_d, 16, self.n1T, self.vecb, "n1", rot)
            self.load_cols(st, self.n2_d, 16, self.n2T, self.vecb, "n2", rot)
            self.load_cols(st, self.cqg_d, 4, self.cqgT, self.vecb, "cqg", rot)
            self.load_cols(st, self.convb_d, 48, self.convbT, self.vecb, "cvb", rot)
            self.load_cols(st, self.convw_d[0:96, :], 96, self.convwT[:, 0:96], self.vecb, "cvw0", rot)
            self.load_cols(st, self.convw_d[96:192, :], 96, self.convwT[:, 96:192], self.vecb, "cvw1", rot)
            self.load_cols(st, self.cb_d, 16, self.cT, self.vecb, "cb", rot)
            self.act(self.c_actT[:], self.cT, AF.Silu, r=[self.vecb], w=[self.vecb])
            wa = [sb(st, "wa%d" % i, [128, 16, 1536], BF16) for i in range(2)]
            wab = [Buf(), Buf()]
            wsrc = self.w_ada.rearrange("(kc p) n -> p kc n", p=128)
            psm, psmb = self.ps[4], self.psb[4]
            self.dma("pool", wa[0][:], wsrc[:, :, 0:1536], w=[wab[0]])
            for fgp in range(8):
                if fgp + 1 < 8:
                    self.dma("pool", wa[(fgp + 1) % 2][:], wsrc[:, :, (fgp + 1) * 1536:(fgp + 2) * 1536],
                             w=[wab[(fgp + 1) % 2]])
                wt, wb = wa[fgp % 2], wab[fgp % 2]
                for fc in range(12):
                    col = fgp * 12 + fc
                    for kc in range(16):
                        self.mm(psm[:, col:col + 1], wt[:, kc, fc * 128:(fc + 1) * 128], self.c_actT[:, kc:kc + 1],
                                start=(kc == 0), stop=(kc == 15), r=[wb, self.vecb], w=[psmb])
            self.tt("dve", self.modT, psm[:, 0:96], badaT[:], ALU.add, r=[psmb, bb], w=[self.vecb])
            self.stt(self.s1, self.modT[:, 16:32], 1.0, self.n1T, ALU.add, ALU.mult, r=[self.vecb], w=[self.vecb])
            self.stt(self.s2, self.modT[:, 64:80], 1.0, self.n2T, ALU.add, ALU.mult, r=[self.vecb], w=[self.vecb])
            self.sh1, self.sh2 = self.modT[:, 0:16], self.modT[:, 48:64]
            ps, pb = rot.next()
            self.tr(ps[0:96, 0:128], self.modT, self.IDF, r=[self.vecb, self.cstb], w=[pb])
            modr = sb(st, "modr", [96, 128], F32)
            mrb = Buf()
            self.cp("dve", modr[:], ps[0:96, 0:128], r=[pb], w=[mrb])
            self.dma("sp", self.modv.ap(), modr[:], r=[mrb], w=[self.scr["modv"]])
            self.P.barrier()

    def norm_transpose(self, st, x_dram, hT, hTb, s_cols, sh_cols, tag, src_tile=None):
        NT = self.NT
        xb_t = [self.sb(st, "xt%s%d" % (tag, i), [128, 2048], F32) for i in range(2)]
        xbb = [Buf(), Buf()]
        xn_t = [self.sb(st, "xn%s%d" % (tag, i), [128, 2048], BF16) for i in range(2)]
        xnb = [Buf(), Buf()]
        ss = self.sb(st, "ss" + tag, [128, 3 * NT], F32)
        ssb = [Buf() for _ in range(NT)]
        rot = self.psr([0, 1, 2, 3])
        import os as _os
        lvl = int(_os.environ.get("KNT", "9"))
        for tc in range(NT):
            xt, xtb = xb_t[tc % 2], xbb[tc % 2]
            xn, xnbb = xn_t[tc % 2], xnb[tc % 2]
            self.dma("sp", xt[:], x_dram[tc * 128:(tc + 1) * 128, :], w=[xtb])
            if lvl < 1:
                continue
            self.act(xn[:], xt[:], AF.Square, accum=ss[:, 3 * tc:3 * tc + 1], r=[xtb], w=[xnbb, ssb[tc]])
            self.rstd(ss[:, 3 * tc + 1:3 * tc + 2], ss[:, 3 * tc:3 * tc + 1], 2048.0, ss[:, 3 * tc + 2:3 * tc + 3],
                      [ssb[tc]], [ssb[tc]])
            if lvl < 2:
                continue
            self.act(xn[:], xt[:], AF.Copy, scale=ss[:, 3 * tc + 1:3 * tc + 2], r=[xtb, ssb[tc]], w=[xnbb])
            if lvl < 3:
                continue
            for half in range(2):
                ps, pb = rot.next()
                psv = ps[:].bitcast(BF16)
                for j in range(8):
                    fc = half * 8 + j
                    self.tr(psv[:, j * 128:(j + 1) * 128], xn[:, fc * 128:(fc + 1) * 128], self.IDB,
                            r=[xnbb, self.cstb], w=[pb])
                if lvl < 4:
                    continue
                for j in range(8):
                    fc = half * 8 + j
                    dst = hT[:, fc, tc * 128:(tc + 1) * 128]
                    if half == 0:
                        self.act(dst, psv[:, j * 128:(j + 1) * 128], AF.Identity, bias=sh_cols[:, fc:fc + 1],
                                 scale=s_cols[:, fc:fc + 1], r=[pb, self.vecb], w=[hTb[tc][fc]])
                    else:
                        self.ts("dve", dst, psv[:, j * 128:(j + 1) * 128], s_cols[:, fc:fc + 1], sh_cols[:, fc:fc + 1],
                                ALU.mult, ALU.add, r=[pb, self.vecb], w=[hTb[tc][fc]])

    def load_w(self, dst, w_dram, c0, W, buf, k0=0, KC=None):
        src = w_dram.rearrange("(kc p) n -> p kc n", p=128)
        if KC is None:
            self.dma("pool", dst, src[:, :, c0:c0 + W], w=[buf])
        else:
            self.dma("pool", dst, src[:, k0:k0 + KC, c0:c0 + W], w=[buf])

    def hT_reads(self, hTb, kc, tcs):
        return [hTb[tc][kc] for tc in tcs]

    def phase_inproj(self, gst):
        NT, T, NTG = self.NT, self.T, self.NTG
        sb = self.sb
        self.dtp = {}
        for name in ("de_p", "dec_p", "dt_o", "adt_o", "ea_o", "de_o", "dec_o"):
            self.dtp[name] = sb(gst, name, [128, NT, 64], F32)
        self.dtpb = {name: [Buf() for _ in range(NT)] for name in self.dtp}
        with self.scope() as st:
            hT = sb(st, "hT", [128, 16, T], BF16)
            hTb = [[Buf() for _ in range(16)] for _ in range(NT)]
            wt = [sb(st, "wblk%d" % i, [128, 16, 512], BF16) for i in range(2)]
            wtb = [Buf(), Buf()]
            for prefix in (True, False):
              with self.scope() as stn:
                self.norm_transpose(stn, self.xp if prefix else self.xo, hT, hTb, self.s1, self.sh1,
                                    "p" if prefix else "o")
                self.P.barrier()
              with self.scope() as ste:
                self.ip_tiles(ste)
                blocks = []
                if not prefix:
                    blocks.append(("tm0", [(OFF_CQ, 512, 0)]))
                blocks.append(("tm1", [(OFF_CKV, 336, 0), (OFF_DT, 64, 336)]))
                if not prefix:
                    for g in range(8):
                        blocks.append(("z", [(OFF_Z + g * 512, 512, 0)], g))
                for g in range(8):
                    blocks.append(("x", [(OFF_X + g * 512, 512, 0)], g))
                for g2 in range(2):
                    blocks.append(("B", [(OFF_B + g2 * 512, 512, 0)], g2))
                for g2 in range(2):
                    blocks.append(("C", [(OFF_C + g2 * 512, 512, 0)], g2))
                if not prefix:
                    for g in range(8):
                        blocks.append(("gate", [(OFF_GA + g * 512, 512, 0)], g))

                import os as _os
                _lim = _os.environ.get("KLIMIT")
                if _lim is not None:
                    blocks = [b_ for b_ in blocks if b_[0] in _lim.split(",")]
                if not blocks:
                    self.P.barrier()
                    continue

                def issue(i):
                    for (c0, W, d0) in blocks[i][1]:
                        self.load_w(wt[i % 2][:, :, d0:d0 + W], self.w_in, c0, W, wtb[i % 2])
                issue(0)
                for i, blk in enumerate(blocks):
                    if i + 1 < len(blocks):
                        issue(i + 1)
                    w_t, w_b = wt[i % 2], wtb[i % 2]
                    kind = blk[0]
                    if kind == "tm0":
                        self.ep_tm0(hT, hTb, w_t, w_b)
                    elif kind == "tm1":
                        self.ep_tm1(hT, hTb, w_t, w_b, prefix)
                    elif kind == "z":
                        self.ep_z(hT, hTb, w_t, w_b, blk[2])
                    elif kind == "x":
                        self.ep_conv(hT, hTb, w_t, w_b, prefix, "x", blk[2])
                    elif kind == "B":
                        self.ep_conv(hT, hTb, w_t, w_b, prefix, "B", blk[2])
                    elif kind == "C":
                        self.ep_conv(hT, hTb, w_t, w_b, prefix, "C", blk[2])
                    elif kind == "gate":
                        self.ep_gate(hT, hTb, w_t, w_b, blk[2])
                self.P.barrier()

    def ip_tiles(self, st):
        sb = self.sb
        T, NT = self.T, self.NT
        self.pre = sb(st, "pre", [128, T + 3], F32)
        self.preb = Buf()
        self.acc = sb(st, "cacc", [128, T], F32)
        self.accb = Buf()
        self.cvs = [sb(st, "cv%d" % i, [128, T], BF16) for i in range(4)]
        self.cvbs = [Buf() for _ in range(4)]
        self.xstage = sb(st, "xstage", [128, NT, 512], BF16)
        self.xstb = Buf()
        self.st512 = [sb(st, "st512_%d" % i, [128, 512], F32) for i in range(3)]
        self.st512b = [Buf() for _ in range(3)]
        self.st512r = Rot(list(zip(self.st512, self.st512b)))
        self.zst = [sb(st, "zst%d" % i, [128, 512], BF16) for i in range(3)]
        self.zstb = [Buf() for _ in range(3)]
        self.zstr = Rot(list(zip(self.zst, self.zstb)))
        self.sm = [sb(st, "smf%d" % i, [128, 512], F32) for i in range(2)]
        self.smb = [Buf(), Buf()]
        self.smh = [sb(st, "smh%d" % i, [128, 1024], BF16) for i in range(2)]
        self.smhb = [Buf(), Buf()]
        self.memset("dve", self.pre[:, 0:3], 0.0, w=[self.preb])

    def ep_tm0(self, hT, hTb, wt, wb):
        NT = self.NT
        rot = self.psr([0, 1, 2, 3])
        cqv = self.cqT_d.ap().rearrange("(fc p) t -> p fc t", p=128)
        for tc in range(NT):
            ps, pb = rot.next()
            for kc in range(16):
                self.mm(ps[:, 0:512], hT[:, kc, tc * 128:(tc + 1) * 128], wt[:, kc, 0:512], start=(kc == 0),
                        stop=(kc == 15), r=[hTb[tc][kc], wb], w=[pb])
            sm, smb = self.sm[tc % 2], self.smb[tc % 2]
            sh, shb = self.smh[tc % 2], self.smhb[tc % 2]
            self.act(sh[:, 0:512], ps[:, 0:512], AF.Square, accum=sm[:, 0:1], r=[pb], w=[shb, smb])
            self.rstd(sm[:, 1:2], sm[:, 0:1], 512.0, sm[:, 2:3], [smb], [smb])
            self.act(sh[:, 0:512], ps[:, 0:512], AF.Copy, scale=sm[:, 1:2], r=[pb, smb], w=[shb])
            ps2, pb2 = rot.next()
            p2v = ps2[:].bitcast(BF16)
            for fc in range(4):
                self.tr(p2v[:, fc * 128:(fc + 1) * 128], sh[:, fc * 128:(fc + 1) * 128], self.IDB,
                        r=[shb, self.cstb], w=[pb2])
            for fc in range(4):
                self.ts("dve", sh[:, 512 + fc * 128:512 + (fc + 1) * 128], p2v[:, fc * 128:(fc + 1) * 128],
                        self.cqgT[:, fc:fc + 1], None, ALU.mult, r=[pb2, self.vecb], w=[shb])
            self.dma("sp", cqv[:, :, tc * 128:(tc + 1) * 128],
                     sh[:, 512:1024].rearrange("p (fc t) -> p fc t", fc=4), r=[shb], w=[Buf()])

    def ep_tm1(self, hT, hTb, wt, wb, prefix):
        NT, T = self.NT, self.T
        rot = self.psr([0, 1, 2, 3])
        key0 = 0 if prefix else T
        ckvTv = self.ckvT_d.ap().rearrange("(rc p) s -> p rc s", p=128)
        D = self.dtp
        DB = self.dtpb
        for tc in range(NT):
            ps, pb = rot.next()
            for kc in range(16):
                self.mm(ps[:, 0:400], hT[:, kc, tc * 128:(tc + 1) * 128], wt[:, kc, 0:400], start=(kc == 0),
                        stop=(kc == 15), r=[hTb[tc][kc], wb], w=[pb])
            sm, smb = self.sm[tc % 2], self.smb[tc % 2]
            sh, shb = self.smh[tc % 2], self.smhb[tc % 2]
            kpos = key0 + tc * 128
            self.act(sh[:, 0:256], ps[:, 0:256], AF.Square, accum=sm[:, 0:1], r=[pb], w=[shb, smb])
            self.rstd(sm[:, 1:2], sm[:, 0:1], 256.0, sm[:, 2:3], [smb], [smb])
            self.stt(sh[:, 0:256], ps[:, 0:256], sm[:, 1:2], self.ckvg_bc, ALU.mult, ALU.mult,
                     r=[pb, smb, self.bcb], w=[shb])
            self.dma("sp", self.ckvtok_d.ap()[kpos:kpos + 128, :], sh[:, 0:256], r=[shb], w=[Buf()])
            ps2, pb2 = rot.next()
            p2v = ps2[:].bitcast(BF16)
            for rc in range(2):
                self.tr(p2v[:, rc * 128:(rc + 1) * 128], sh[:, rc * 128:(rc + 1) * 128], self.IDB,
                        r=[shb, self.cstb], w=[pb2])
            self.act(sm[:, 64:128], ps[:, 256:320], AF.Identity, accum=sm[:, 3:4], r=[pb], w=[smb])
            self.act(sm[:, 64:128], ps[:, 256:320], AF.Square, accum=sm[:, 4:5], r=[pb], w=[smb])
            self.ts("dve", sm[:, 5:6], sm[:, 3:4], 1.0 / 64, None, ALU.mult, r=[smb], w=[smb])
            self.tt("dve", sm[:, 6:7], sm[:, 5:6], sm[:, 5:6], ALU.mult, r=[smb], w=[smb])
            self.stt(sm[:, 7:8], sm[:, 4:5], 1.0 / 64, sm[:, 6:7], ALU.mult, ALU.subtract, r=[smb], w=[smb])
            self.ts("dve", sm[:, 8:9], sm[:, 7:8], EPS, None, ALU.add, r=[smb], w=[smb])
            self.act(sm[:, 8:9], sm[:, 8:9], AF.Sqrt, r=[smb], w=[smb])
            self.recip(sm[:, 9:10], sm[:, 8:9], r=[smb], w=[smb])
            self.ts("dve", sm[:, 64:128], ps[:, 256:320], sm[:, 5:6], sm[:, 9:10], ALU.subtract, ALU.mult,
                    r=[pb, smb], w=[smb])
            self.tt("dve", sm[:, 64:128], sm[:, 64:128], self.kig_bc, ALU.mult, r=[smb, self.bcb], w=[smb])
            self.tt("dve", sh[:, 256:320], sm[:, 64:128], self.kib_bc, ALU.add, r=[smb, self.bcb], w=[shb])
            self.cp("dve", sh[:, 320:384], sh[:, 256:320], r=[shb], w=[shb])
            self.tr(p2v[:, 256:384], sh[:, 256:384], self.IDB, r=[shb, self.cstb], w=[pb2])
            self.cp("act", sh[:, 512:896], p2v[:, 0:384], r=[pb2], w=[shb])
            self.dma("sp", ckvTv[:, :, kpos:kpos + 128], sh[:, 512:768].rearrange("p (rc s) -> p rc s", rc=2),
                     r=[shb], w=[Buf()])
            self.dma("sp", self.kidxT_d.ap()[:, kpos:kpos + 128], sh[:, 768:896], r=[shb], w=[Buf()])
            if not prefix:
                self.ts("dve", sm[:, 16:32], ps[:, 320:336], 1.0 / 32.0, None, ALU.mult, r=[pb], w=[smb])
                self.dma("sp", self.widx_d.ap()[tc * 128:(tc + 1) * 128, :], sm[:, 16:32], r=[smb],
                         w=[Buf()])
            dt_t = sm[:, 128:192]
            self.tt("dve", dt_t, ps[:, 336:400], self.dtb_bc, ALU.add, r=[pb, self.bcb], w=[smb])
            self.act(dt_t, dt_t, AF.Exp, r=[smb], w=[smb])
            self.act(dt_t, dt_t, AF.Ln, bias=1.0, r=[smb], w=[smb])
            adt = sm[:, 192:256]
            self.tt("dve", adt, dt_t, self.a_bc, ALU.mult, r=[smb, self.bcb], w=[smb])
            ps3, pb3 = rot.next()
            self.mm(ps3[:, 0:64], self.U, adt, r=[self.cstb, smb], w=[pb3])
            self.mm(ps3[:, 64:128], self.ONES, adt, r=[self.cstb, smb], w=[pb3])
            acs = sm[:, 256:320]
            self.cp("act", acs, ps3[:, 0:64], r=[pb3], w=[smb])
            pn = "p" if prefix else "o"
            self.act(D["dec_" + pn][:, tc, :], ps3[:, 64:128], AF.Exp, r=[pb3], w=[DB["dec_" + pn][tc]])
            self.tt("dve", sm[:, 320:384], ps3[:, 64:128], acs, ALU.subtract, r=[pb3, smb], w=[smb])
            self.act(sm[:, 320:384], sm[:, 320:384], AF.Exp, r=[smb], w=[smb])
            self.tt("dve", D["de_" + pn][:, tc, :], sm[:, 320:384], dt_t, ALU.mult, r=[smb], w=[DB["de_" + pn][tc]])
            if not prefix:
                self.cp("dve", D["dt_o"][:, tc, :], dt_t, r=[smb], w=[DB["dt_o"][tc]])
                self.cp("dve", D["adt_o"][:, tc, :], adt, r=[smb], w=[DB["adt_o"][tc]])
                self.act(D["ea_o"][:, tc, :], acs, AF.Exp, r=[smb], w=[DB["ea_o"][tc]])

    def ep_z(self, hT, hTb, wt, wb, g):
        rot = self.psr([0, 1, 2, 3])
        for tc in range(self.NT):
            ps, pb = rot.next()
            for kc in range(16):
                self.mm(ps[:, 0:512], hT[:, kc, tc * 128:(tc + 1) * 128], wt[:, kc, 0:512], start=(kc == 0),
                        stop=(kc == 15), r=[hTb[tc][kc], wb], w=[pb])
            zt, ztb = self.zstr.next()
            self.act(zt[:], ps[:, 0:512], AF.Silu, r=[pb], w=[ztb])
            self.dma("sp", self.z_d.ap()[tc * 128:(tc + 1) * 128, g * 512:(g + 1) * 512], zt[:], r=[ztb],
                     w=[Buf()])

    def ep_gate(self, hT, hTb, wt, wb, g):
        rot = self.psr([0, 1, 2, 3])
        for mc in range(4):
            for tg in range(self.NTG):
                ps, pb = rot.next()
                tcs = range(tg * 4, tg * 4 + 4)
                for kc in range(16):
                    self.mm(ps[:, 0:512], wt[:, kc, mc * 128:(mc + 1) * 128], hT[:, kc, tg * 512:(tg + 1) * 512],
                            start=(kc == 0), stop=(kc == 15), r=[wb] + self.hT_reads(hTb, kc, tcs), w=[pb])
                gt, gtb = self.st512r.next()
                self.act(gt[:], ps[:, 0:512], AF.Sigmoid, r=[pb], w=[gtb])
                row = (g * 4 + mc) * 128
                self.dma("sp", self.gT_d.ap()[row:row + 128, tg * 512:(tg + 1) * 512], gt[:], r=[gtb],
                         w=[Buf()])

    def ep_conv(self, hT, hTb, wt, wb, prefix, kind, g):
        NT, T, NTG = self.NT, self.T, self.NTG
        rot = self.psr([0, 1, 2, 3, 4, 5, 6, 7])
        row0 = 0 if prefix else T
        for cc in range(4):
            cv, cvb = self.cvs[cc], self.cvbs[cc]
            if kind == "x":
                chunk = g * 4 + cc
            elif kind == "B":
                chunk = 32 + g * 4 + cc
            else:
                chunk = 40 + g * 4 + cc
            hidx = chunk
            tgs = range(NTG)
            if kind == "C" and prefix:
                tgs = [NTG - 1]
            if not prefix:
                self.cp("dve", self.pre[:, 0:3], self.halo[:, hidx, :], r=[self.halob], w=[self.preb])
            for tg in tgs:
                ps, pb = rot.next()
                tcs = range(tg * 4, tg * 4 + 4)
                for kc in range(16):
                    self.mm(ps[:, 0:512], wt[:, kc, cc * 128:(cc + 1) * 128], hT[:, kc, tg * 512:(tg + 1) * 512],
                            start=(kc == 0), stop=(kc == 15), r=[wb] + self.hT_reads(hTb, kc, tcs), w=[pb])
                self.cp("act", self.pre[:, 3 + tg * 512:3 + (tg + 1) * 512], ps[:, 0:512], r=[pb], w=[self.preb])
            if prefix:
                self.ts("dve", self.halo[:, hidx, :], self.pre[:, T:T + 3], self.flg[:, 0:1], None, ALU.mult,
                        r=[self.preb, self.flgb], w=[self.halob])
                if kind == "C":
                    continue
            cw = self.convwT
            self.ts("dve", self.acc[:], self.pre[:, 0:T], cw[:, chunk:chunk + 1], None, ALU.mult,
                    r=[self.preb, self.vecb], w=[self.accb])
            for k in range(1, 4):
                self.stt(self.acc[:], self.pre[:, k:k + T], cw[:, k * 48 + chunk:k * 48 + chunk + 1], self.acc[:],
                         ALU.mult, ALU.add, r=[self.preb, self.vecb, self.accb], w=[self.accb])
            self.act(cv[:], self.acc[:], AF.Silu, bias=self.convbT[:, chunk:chunk + 1], r=[self.accb, self.vecb],
                     w=[cvb])
            gc = g * 4 + cc
            if kind == "C":
                self.dma("sp", self.CT_d.ap()[gc * 128:(gc + 1) * 128, :], cv[:], r=[cvb], w=[Buf()])
            elif kind == "B" and not prefix:
                self.dma("sp", self.BT_d.ap()[gc * 128:(gc + 1) * 128, :], cv[:], r=[cvb], w=[Buf()])
        if kind == "C":
            return
        for cc in range(4):
            cv, cvb = self.cvs[cc], self.cvbs[cc]
            for t4 in range(NT // 4):
                ps, pb = rot.next()
                pv = ps[:].bitcast(BF16)
                for q in range(4):
                    tc = t4 * 4 + q
                    self.tr(pv[:, q * 128:(q + 1) * 128], cv[:, tc * 128:(tc + 1) * 128], self.IDB,
                            r=[cvb, self.cstb], w=[pb])
                self.cp("dve" if t4 % 2 == 0 else "act", self.xstage[:, t4 * 4:(t4 + 1) * 4, cc * 128:(cc + 1) * 128],
                        pv[:, 0:512].rearrange("p (q c) -> p q c", q=4), r=[pb], w=[self.xstb])
        if kind == "x":
            dst = self.x_d.ap()[row0:row0 + T, g * 512:(g + 1) * 512].rearrange("(tc p) c -> p tc c", p=128)
            self.dma("sp", dst, self.xstage[:], r=[self.xstb], w=[Buf()])
        else:
            dst = self.Btok_d.ap()[row0:row0 + T, g * 512:(g + 1) * 512].rearrange("(tc p) c -> p tc c", p=128)
            self.dma("sp", dst, self.xstage[:], r=[self.xstb], w=[Buf()])

    def phase_ssd(self):
        NT, T = self.NT, self.T
        sb = self.sb
        D, DB = self.dtp, self.dtpb
        with self.scope() as st:
            self.stateT = sb(st, "stateT", [128, 4096], F32)
            self.stateb = [Buf() for _ in range(8)]
            xg = [sb(st, "xg%d" % i, [128, NT, 512], BF16) for i in range(2)]
            bt = [sb(st, "btok%d" % i, [128, NT, 128], BF16) for i in range(2)]
            zg = [sb(st, "zg%d" % i, [128, NT, 512], BF16) for i in range(2)]
            BgT = [sb(st, "BgT%d" % i, [128, T], BF16) for i in range(2)]
            CgT = [sb(st, "CgT%d" % i, [128, T], BF16) for i in range(2)]
            gsm = [sb(st, "gsm%d" % i, [128, 512], F32) for i in range(2)]
            gb = [Buf(), Buf()]
            ybst = sb(st, "ybst", [128, 4, T], BF16)
            ybb = Buf()
            def dbl(name, shape, dt):
                return [sb(st, name + str(i), shape, dt) for i in range(2)], [Buf(), Buf()]
            rseg, rsegb = dbl("rseg", [128, 8, 128], F32)
            Eb, Ebb = dbl("Eb", [128, 8, 128], BF16)
            cbm, cbmb = dbl("cbm", [128, 128], BF16)
            WT, WTb = dbl("WT", [128, 8, 128], BF16)
            xdt, xdtb = dbl("xdt", [128, 512], BF16)
            xw, xwb = dbl("xw", [128, 512], BF16)
            stb_t = sb(st, "stbf", [128, 512], BF16)
            stbb = Buf()
            y1s, y1bs = dbl("y1", [128, 512], F32)
            y2s, y2bs = dbl("y2", [128, 512], F32)
            y5s, y5bs = dbl("y5", [128, 512], BF16)
            ysms, ysmbs = dbl("ysm", [128, 8], F32)
            junks, junkbs = dbl("yjunk", [128, 512], BF16)
            rot = self.psr([0, 1, 2, 3, 4, 5, 6, 7])
            ybv = self.ybT_d.ap().rearrange("(cc p) t -> p cc t", p=128)

            for prefix in (True, False):
                row0 = 0 if prefix else T
                pn = "p" if prefix else "o"

                def load_group(g):
                    i = g % 2
                    self.dma("sp", xg[i][:], self.x_d.ap()[row0:row0 + T, g * 512:(g + 1) * 512]
                             .rearrange("(tc p) c -> p tc c", p=128), r=[self.scr["x0" if prefix else "x1"]], w=[gb[i]])
                    self.dma("sp", bt[i][:], self.Btok_d.ap()[row0:row0 + T, g * 128:(g + 1) * 128]
                             .rearrange("(tc p) c -> p tc c", p=128), r=[self.scr["Btok0" if prefix else "Btok1"]],
                             w=[gb[i]])
                    if not prefix:
                        self.dma("sp", zg[i][:], self.z_d.ap()[:, g * 512:(g + 1) * 512]
                                 .rearrange("(tc p) c -> p tc c", p=128), r=[self.scr["z"]], w=[gb[i]])
                        self.dma("sp", BgT[i][:], self.BT_d.ap()[g * 128:(g + 1) * 128, :], r=[self.scr["BT"]], w=[gb[i]])
                        self.dma("sp", CgT[i][:], self.CT_d.ap()[g * 128:(g + 1) * 128, :], r=[self.scr["CT"]], w=[gb[i]])
                        self.dma("sp", gsm[i][:], self.ssmg_d[:, g * 512:(g + 1) * 512].partition_broadcast(128),
                                 w=[gb[i]])
                load_group(0)
                for g in range(8):
                    if g + 1 < 8:
                        load_group(g + 1)
                    i = g % 2
                    G = gb[i]
                    stg = self.stateT[:, g * 512:(g + 1) * 512]
                    sbuf = self.stateb[g]
                    hs = slice(g * 8, (g + 1) * 8)
                    if prefix:
                        self.memset("dve", stg, 0.0, w=[sbuf])
                    else:
                        self.ts("dve", stg, stg, self.flg[:, 0:1], None, ALU.mult, r=[sbuf, self.flgb], w=[sbuf])
                        self.cp("act", stb_t[:], stg, r=[sbuf], w=[stbb])

                    def stage1(c):
                        k = c % 2
                        xc = xg[i][:, c, :]
                        self.tt("pool", xw[k][:].rearrange("p (h q) -> p h q", h=8),
                                xc.rearrange("p (h q) -> p h q", h=8),
                                D["de_" + pn][:, c, hs].unsqueeze(2).broadcast_to([128, 8, 64]), ALU.mult,
                                r=[G, DB["de_" + pn][c]], w=[xwb[k]])
                        if prefix:
                            return
                        adt = D["adt_o"][:, c, hs]
                        self.tt("pool", rseg[k][:], self.U.unsqueeze(1).broadcast_to([128, 8, 128]),
                                adt.unsqueeze(2).broadcast_to([128, 8, 128]), ALU.mult,
                                r=[self.cstb, DB["adt_o"][c]], w=[rsegb[k]])
                        for hh in range(2):
                            psS, psSb = rot.next()
                            self.mm(psS[:, 0:512], self.L1,
                                    rseg[k][:, hh * 4:(hh + 1) * 4, :].rearrange("p h t -> p (h t)"),
                                    r=[self.cstb, rsegb[k]], w=[psSb])
                            self.act(Eb[k][:, hh * 4:(hh + 1) * 4, :].rearrange("p h t -> p (h t)"), psS[:, 0:512],
                                     AF.Exp, r=[psSb], w=[Ebb[k]])
                        psC, psCb = rot.next()
                        self.mm(psC[:, 0:128], BgT[i][:, c * 128:(c + 1) * 128], CgT[i][:, c * 128:(c + 1) * 128],
                                r=[G], w=[psCb])
                        self.tt("dve", cbm[k][:], psC[:, 0:128], self.U, ALU.mult, r=[psCb, self.cstb], w=[cbmb[k]])
                        self.tt("pool", WT[k][:], Eb[k][:], cbm[k][:].unsqueeze(1).broadcast_to([128, 8, 128]), ALU.mult,
                                r=[Ebb[k], cbmb[k]], w=[WTb[k]])
                        self.tt("dve", xdt[k][:].rearrange("p (h q) -> p h q", h=8),
                                xc.rearrange("p (h q) -> p h q", h=8),
                                D["dt_o"][:, c, hs].unsqueeze(2).broadcast_to([128, 8, 64]), ALU.mult,
                                r=[G, DB["dt_o"][c]], w=[xdtb[k]])

                    def stage2(c):
                        k = c % 2
                        xc = xg[i][:, c, :]
                        y1, y1b, y2, y2b, y5, y5b = y1s[k], y1bs[k], y2s[k], y2bs[k], y5s[k], y5bs[k]
                        ysm, ysmb, junk, junkb = ysms[k], ysmbs[k], junks[k], junkbs[k]
                        if not prefix:
                            psY, psYb = rot.next()
                            for h in range(8):
                                self.mm(psY[:, h * 64:(h + 1) * 64], WT[k][:, h, :], xdt[k][:, h * 64:(h + 1) * 64],
                                        r=[WTb[k], xdtb[k]], w=[psYb])
                            psI, psIb = rot.next()
                            self.mm(psI[:, 0:512], CgT[i][:, c * 128:(c + 1) * 128], stb_t[:], r=[G, stbb], w=[psIb])
                        psN, psNb = rot.next()
                        self.mm(psN[:, 0:512], bt[i][:, c, :], xw[k][:], r=[G, xwb[k]], w=[psNb])
                        self.tt("dve", stg.rearrange("p (h q) -> p h q", h=8), stg.rearrange("p (h q) -> p h q", h=8),
                                D["dec_" + pn][:, c, hs].unsqueeze(2).broadcast_to([128, 8, 64]), ALU.mult,
                                r=[sbuf, DB["dec_" + pn][c]], w=[sbuf])
                        self.tt("dve", stg, stg, psN[:, 0:512], ALU.add, r=[sbuf, psNb], w=[sbuf])
                        if prefix:
                            return
                        self.tt("dve", y1[:].rearrange("p (h q) -> p h q", h=8),
                                psI[:, 0:512].rearrange("p (h q) -> p h q", h=8),
                                D["ea_o"][:, c, hs].unsqueeze(2).broadcast_to([128, 8, 64]), ALU.mult,
                                r=[psIb, DB["ea_o"][c]], w=[y1b])
                        if c + 1 < NT:
                            self.cp("act", stb_t[:], stg, r=[sbuf], w=[stbb])
                        self.tt("dve", y1[:], psY[:, 0:512], y1[:], ALU.add, r=[psYb, y1b], w=[y1b])
                        self.tt("pool", y2[:].rearrange("p (h q) -> p h q", h=8),
                                xc.rearrange("p (h q) -> p h q", h=8),
                                self.dsk_bc[:, hs].unsqueeze(2).broadcast_to([128, 8, 64]), ALU.mult,
                                r=[G, self.bcb], w=[y2b])
                        self.tt("dve", y1[:], y1[:], y2[:], ALU.add, r=[y1b, y2b], w=[y1b])
                        self.tt("dve", y1[:], y1[:], zg[i][:, c, :], ALU.mult, r=[y1b, G], w=[y1b])
                        self.act(junk[:], y1[:], AF.Square, accum=ysm[:, 0:1], r=[y1b], w=[junkb, ysmb])
                        self.rstd(ysm[:, 1:2], ysm[:, 0:1], 512.0, ysm[:, 2:3], [ysmb], [ysmb])
                        self.stt(y5[:], y1[:], ysm[:, 1:2], gsm[i][:], ALU.mult, ALU.mult, r=[y1b, ysmb, G], w=[y5b])
                        psT, psTb = rot.next()
                        ptv = psT[:].bitcast(BF16)
                        for cc in range(4):
                            self.tr(ptv[:, cc * 128:(cc + 1) * 128], y5[:, cc * 128:(cc + 1) * 128], self.IDB,
                                    r=[y5b, self.cstb], w=[psTb])
                        self.cp("act", ybst[:, :, c * 128:(c + 1) * 128],
                                ptv[:, 0:512].rearrange("p (cc t) -> p cc t", cc=4), r=[psTb], w=[ybb])

                    stage1(0)
                    for c in range(NT):
                        if c + 1 < NT:
                            stage1(c + 1)
                        stage2(c)
                    if not prefix:
                        self.dma("sp", ybv[:, g * 4:(g + 1) * 4, :], ybst[:], r=[ybb], w=[Buf()])
            self.P.barrier()

    def phase_attn(self):
        NT, T, NTG = self.NT, self.T, self.NTG
        sb = self.sb
        KB = 2 * NT
        with self.scope() as st:
            ckvT = sb(st, "ckvT", [128, 2, 2 * T], BF16)
            ckvtok = sb(st, "ckvtok", [128, KB, 256], BF16)
            kidxT = sb(st, "kidxT", [128, 2 * T], BF16)
            cqT_t = sb(st, "cqT", [128, 4, 512], BF16)
            cqb = Buf()
            widx = sb(st, "widx", [128, NT, 16], F32)
            ldb = Buf()
            self.dma("sp", ckvT[:], self.ckvT_d.ap().rearrange("(rc p) s -> p rc s", p=128), r=[self.scr["ckvT"]], w=[ldb])
            self.dma("sp", ckvtok[:], self.ckvtok_d.ap().rearrange("(kb p) r -> p kb r", p=128), r=[self.scr["ckvtok"]],
                     w=[ldb])
            self.dma("sp", kidxT[:], self.kidxT_d.ap(), r=[self.scr["kidxT"]], w=[ldb])
            self.dma("sp", widx[:], self.widx_d.ap().rearrange("(tc p) h -> p tc h", p=128), r=[self.scr["widx"]], w=[ldb])
            wuq = sb(st, "wuq", [128, 4, 2048], BF16)
            wiq = sb(st, "wiq", [128, 4, 1024], BF16)
            wuv = sb(st, "wuv", [128, 2, 2048], BF16)
            wukT = sb(st, "wukT", [128, 16, 256], BF16)
            biasT = sb(st, "biasT", [128, 16, 2, 128], BF16)
            wb_ = Buf()
            self.load_w(wuq[:], self.w_uq, 0, 2048, wb_)
            self.load_w(wiq[:], self.w_iq, 0, 1024, wb_)
            self.load_w(wuv[:], self.w_uv, 0, 2048, wb_)
            rot = self.psr([0, 1, 2, 3, 4])
            wukTb = Buf()
            biasb = Buf()
            with self.scope() as stmp:
                wuk = sb(stmp, "wuk", [128, 2, 2048], BF16)
                wkb = Buf()
                self.load_w(wuk[:], self.w_uk, 0, 2048, wkb)
                for h in range(16):
                    ps, pb = rot.next()
                    pv = ps[:].bitcast(BF16)
                    for rc in range(2):
                        self.tr(pv[:, rc * 128:(rc + 1) * 128], wuk[:, rc, h * 128:(h + 1) * 128], self.IDB,
                                r=[wkb, self.cstb], w=[pb])
                    self.cp("dve" if h % 2 else "act", wukT[:, h, :], pv[:, 0:256], r=[pb], w=[wukTb])
                relb = sb(stmp, "relb", [32, 16], F32)
                ohd = sb(stmp, "ohd", [32, 384], F32)
                tvs = sb(stmp, "tvs", [16, 384], F32)
                H = sb(stmp, "Hank", [128, 16, 2, 128], F32)
                bb = Buf()
                self.dma("sp", relb[:], self.relb_d, w=[bb])
                self.dma("sp", ohd[:], self.ohd_d, w=[bb])
                ps, pb = rot.next()
                self.mm(ps[0:16, 0:384], relb[:], ohd[:], r=[bb], w=[pb])
                tvb = Buf()
                self.cp("dve", tvs[:], ps[0:16, 0:384], r=[pb], w=[tvb])
                self.dma("sp", self.tv.ap(), tvs[:], r=[tvb], w=[self.scr["tv"]])
                hb = Buf()
                self.dma("sp", H[:], bass.AP(self.tv, 0, [[1, 128], [384, 16], [128, 2], [1, 128]]), r=[self.scr["tv"]],
                         w=[hb])
                Hf = H[:].rearrange("p h k t -> p (h k t)")
                Bf = biasT[:].rearrange("p h k t -> p (h k t)")
                for q in range(8):
                    ps, pb = rot.next()
                    self.mm(ps[:, 0:512], self.J, Hf[:, q * 512:(q + 1) * 512], r=[self.cstb, hb], w=[pb])
                    self.cp("dve" if q % 2 else "act", Bf[:, q * 512:(q + 1) * 512], ps[:, 0:512], r=[pb], w=[biasb])
                self.P.barrier()

            qiT = sb(st, "qiT", [128, 8, 512], BF16)
            qiTb = Buf()
            diagw = sb(st, "diagw", [128, 16, 128], BF16)
            diagb = [Buf() for _ in range(16)]
            Rh = [sb(st, "Rh%d" % i, [128, 512], BF16) for i in range(3)]
            Rhb = [Buf() for _ in range(3)]
            Rrot = Rot(list(zip(Rh, Rhb)))
            scores = [sb(st, "score%d" % i, [128, 2 * T], F32) for i in range(2)]
            scbs = [Buf(), Buf()]
            negm = sb(st, "negm", [128, 2 * T], BF16)
            negb = Buf()
            negT = sb(st, "negT", [128, KB, 512], BF16)
            negTb = Buf()
            bis2 = [sb(st, "bis%d" % i, [128, 8], F32) for i in range(2)]
            bisb2 = [Buf(), Buf()]
            junk1 = [sb(st, "junk1_%d" % i, [128, 2], BF16) for i in range(2)]
            junkb1 = [Buf(), Buf()]
            qTs = [sb(st, "qT%d" % i, [128, 512], BF16) for i in range(2)]
            qTbs = [Buf(), Buf()]
            qlTs = [sb(st, "qlT%d" % i, [128, 2, 512], BF16) for i in range(2)]
            qlTbs = [Buf(), Buf()]
            pT = [sb(st, "pT%d" % i, [128, 512], BF16) for i in range(4)]
            pTb = [Buf() for _ in range(4)]
            prot = Rot(list(zip(pT, pTb)))
            rec = sb(st, "rec", [128, 512], F32)
            recb = Buf()
            onT = sb(st, "onT", [128, 2, 512], BF16)
            onTb = Buf()
            yast = [sb(st, "yast%d" % i, [128, 512], BF16) for i in range(1)] * 2
            yastb = [Buf()] * 2
            psO0, psO0b = self.ps[5], self.psb[5]
            psO1, psO1b = self.ps[6], self.psb[6]
            psD, psDb = self.ps[7], self.psb[7]
            arot = self.psr([5, 6])
            pfx_blocks = NT
            wsteps = [16.0 / (2 ** k) for k in range(NBIS + 1)]

            for tg in range(NTG):
                nkb = pfx_blocks + (tg + 1) * 4
                self.dma("sp", cqT_t[:], self.cqT_d.ap()[:, tg * 512:(tg + 1) * 512].rearrange("(fc p) t -> p fc t", p=128),
                         r=[self.scr["cqT"]], w=[cqb])
                for hp in range(8):
                    ps, pb = rot.next()
                    for kc in range(4):
                        self.mm(ps[:, 0:512], wiq[:, kc, hp * 128:(hp + 1) * 128], cqT_t[:, kc, :],
                                start=(kc == 0), stop=(kc == 3), r=[wb_, cqb], w=[pb])
                    self.cp("act" if hp % 2 else "dve", qiT[:, hp, :], ps[:, 0:512], r=[pb], w=[qiTb])
                def b_stage(tq):
                    tcq = tg * 4 + tq
                    score, scb = scores[tcq % 2], scbs[tcq % 2]
                    S = (pfx_blocks + tcq + 1) * 128
                    for h in range(16):
                        self.ts("pool", diagw[:, h, :], self.IDB, widx[:, tcq, h:h + 1], None, ALU.mult,
                                r=[self.cstb, ldb], w=[diagb[h]])
                    nb5 = (S + 511) // 512
                    for k5 in range(nb5):
                        wv = min(512, S - k5 * 512)
                        psA, psAb = arot.next()

                        def idx_s(h):
                            hp, base = h // 2, (h % 2) * 64
                            psI, psIb = rot.next()
                            self.mm(psI[:, 0:wv], qiT[base:base + 64, hp, tq * 128:(tq + 1) * 128],
                                    kidxT[base:base + 64, k5 * 512:k5 * 512 + wv], r=[qiTb, ldb], w=[psIb])
                            rh, rhb = Rrot.next()
                            self.act(rh[:, 0:wv], psI[:, 0:wv], AF.Relu, r=[psIb], w=[rhb])
                            return rh, rhb
                        pend = [idx_s(0), idx_s(1)]
                        for h in range(16):
                            if h + 2 < 16:
                                pend.append(idx_s(h + 2))
                            rh, rhb = pend[h]
                            self.mm(psA[:, 0:wv], diagw[:, h, :], rh[:, 0:wv], start=(h == 0), stop=(h == 15),
                                    r=[diagb[h], rhb], w=[psAb])
                        dst = score[:, k5 * 512:k5 * 512 + wv]
                        if k5 * 512 < T:
                            self.act(dst, psA[:, 0:wv], AF.Identity, bias=self.flg[:, 1:2], r=[psAb, self.flgb], w=[scb])
                        else:
                            self.cp("act", dst, psA[:, 0:wv], r=[psAb], w=[scb])

                def c_pair(tqs):
                    st_ = []
                    for n_, tq in enumerate(tqs):
                        tcq = tg * 4 + tq
                        st_.append((tq, scores[tcq % 2], scbs[tcq % 2], (pfx_blocks + tcq + 1) * 128, bis2[n_], bisb2[n_]))
                    for (tq, score, scb, S, bis, bisb) in st_:
                        self.tt("dve", score[:, S - 128:S], score[:, S - 128:S], self.CM, ALU.add, r=[scb, self.cstb], w=[scb])
                        self.P.op("dve", lambda e, S=S, score=score, bis=bis: e.tensor_reduce(
                            out=bis[:, 0:1], in_=score[:, 0:S], axis=AX.X, op=ALU.max), [scb], [bisb])
                        self.ts("dve", bis[:, 1:2], bis[:, 0:1], -16.0, None, ALU.add, r=[bisb], w=[bisb])
                    for k in range(NBIS):
                        for n_, (tq, score, scb, S, bis, bisb) in enumerate(st_):
                            self.ts("dve", junk1[n_][:, 0:1].broadcast_to([128, S]), score[:, 0:S], bis[:, 1:2], 0.0,
                                    ALU.is_ge, ALU.add, accum=bis[:, 2:3], r=[scb, bisb], w=[junkb1[n_], bisb])
                        for (tq, score, scb, S, bis, bisb) in st_:
                            if k + 1 < NBIS:
                                wn = wsteps[k + 1]
                                self.ts("dve", bis[:, 3:4], bis[:, 2:3], 255.5, 2.0 * wn, ALU.is_ge, ALU.mult, r=[bisb], w=[bisb])
                            else:
                                wl = wsteps[k]
                                self.ts("dve", bis[:, 3:4], bis[:, 2:3], 255.5, wl, ALU.is_ge, ALU.mult, r=[bisb], w=[bisb])
                        for (tq, score, scb, S, bis, bisb) in st_:
                            if k + 1 < NBIS:
                                wn = wsteps[k + 1]
                                self.stt(bis[:, 1:2], bis[:, 3:4], -wn, bis[:, 1:2], ALU.add, ALU.add, r=[bisb], w=[bisb])
                            else:
                                wl = wsteps[k]
                                self.stt(bis[:, 4:5], bis[:, 3:4], -wl, bis[:, 1:2], ALU.add, ALU.add, r=[bisb], w=[bisb])
                    for (tq, score, scb, S, bis, bisb) in st_:
                        self.ts("dve", negm[:, 0:S], score[:, 0:S], bis[:, 4:5], NEG, ALU.is_lt, ALU.mult, r=[scb, bisb],
                                w=[negb])
                        nkq = S // 128
                        for k4 in range((nkq + 3) // 4):
                            n4 = min(4, nkq - k4 * 4)
                            ps, pb = rot.next()
                            pv = ps[:].bitcast(BF16)
                            for q in range(n4):
                                kb = k4 * 4 + q
                                self.tr(pv[:, q * 128:(q + 1) * 128], negm[:, kb * 128:(kb + 1) * 128], self.IDB,
                                        r=[negb, self.cstb], w=[pb])
                            self.cp("act" if k4 % 2 else "dve", negT[:, k4 * 4:k4 * 4 + n4, tq * 128:(tq + 1) * 128],
                                    pv[:, 0:n4 * 128].rearrange("p (q t) -> p q t", q=n4), r=[pb], w=[negTb])
                        if nkq < nkb:
                            self.memset("pool", negT[:, nkq:nkb, tq * 128:(tq + 1) * 128], NEG, w=[negTb])

                b_stage(0)
                b_stage(1)
                c_pair((0, 1))
                b_stage(2)
                b_stage(3)
                c_pair((2, 3))
                def q_stage(h):
                    qT, qTb, qlT, qlTb = qTs[h % 2], qTbs[h % 2], qlTs[h % 2], qlTbs[h % 2]
                    ps, pb = rot.next()
                    for kc in range(4):
                        self.mm(ps[:, 0:512], wuq[:, kc, h * 128:(h + 1) * 128], cqT_t[:, kc, :],
                                start=(kc == 0), stop=(kc == 3), r=[wb_, cqb], w=[pb])
                    self.cp("dve", qT[:], ps[:, 0:512], r=[pb], w=[qTb])
                    for rc in range(2):
                        ps, pb = rot.next()
                        self.mm(ps[:, 0:512], wukT[:, h, rc * 128:(rc + 1) * 128], qT[:], r=[wukTb, qTb], w=[pb])
                        self.act(qlT[:, rc, :], ps[:, 0:512], AF.Copy, scale=128.0 ** -0.5, r=[pb], w=[qlTb])
                q_stage(0)
                for h in range(16):
                    if h + 1 < 16:
                        q_stage(h + 1)
                    qlT, qlTb = qlTs[h % 2], qlTbs[h % 2]
                    def logits(kb):
                        psL, psLb = rot.next()
                        self.mm(psL[:, 0:512], ckvT[:, 0, kb * 128:(kb + 1) * 128], qlT[:, 0, :], start=True, stop=False,
                                r=[ldb, qlTb], w=[psLb])
                        self.mm(psL[:, 0:512], ckvT[:, 1, kb * 128:(kb + 1) * 128], qlT[:, 1, :], start=False, stop=False,
                                r=[ldb, qlTb], w=[psLb])
                        extra = []
                        for tq in range(4):
                            qb = pfx_blocks + tg * 4 + tq
                            if kb == qb:
                                extra.append((tq, 0))
                            elif kb == qb - 1:
                                extra.append((tq, 1))
                        self.mm(psL[:, 0:512], self.IDB, negT[:, kb, :], start=False, stop=(len(extra) == 0),
                                r=[self.cstb, negTb], w=[psLb])
                        for ei, (tq, kind) in enumerate(extra):
                            self.mm(psL[:, tq * 128:(tq + 1) * 128], self.IDB, biasT[:, h, kind, :], start=False,
                                    stop=(ei == len(extra) - 1), r=[self.cstb, biasb], w=[psLb])
                        pt, ptb = prot.next()
                        self.act(pt[:], psL[:, 0:512], AF.Exp, r=[psLb], w=[ptb])
                        return pt, ptb
                    pendl = [logits(0)]
                    if nkb > 1:
                        pendl.append(logits(1))
                    for kb in range(nkb):
                        if kb + 2 < nkb:
                            pendl.append(logits(kb + 2))
                        pt, ptb = pendl[kb]
                        first, last = (kb == 0), (kb == nkb - 1)
                        self.mm(psO0[:, 0:512], ckvtok[:, kb, 0:128], pt[:], start=first, stop=last, r=[ldb, ptb], w=[psO0b])
                        self.mm(psO1[:, 0:512], ckvtok[:, kb, 128:256], pt[:], start=first, stop=last, r=[ldb, ptb], w=[psO1b])
                        self.mm(psD[:, 0:512], self.ONESB, pt[:], start=first, stop=last, r=[self.cstb, ptb], w=[psDb])
                    self.recip(rec[:], psD[:, 0:512], r=[psDb], w=[recb])
                    self.tt("dve", onT[:, 0, :], psO0[:, 0:512], rec[:], ALU.mult, r=[psO0b, recb], w=[onTb])
                    self.tt("dve", onT[:, 1, :], psO1[:, 0:512], rec[:], ALU.mult, r=[psO1b, recb], w=[onTb])
                    ps, pb = rot.next()
                    for rc in range(2):
                        self.mm(ps[:, 0:512], wuv[:, rc, h * 128:(h + 1) * 128], onT[:, rc, :], start=(rc == 0),
                                stop=(rc == 1), r=[wb_, onTb], w=[pb])
                    ya, yab = yast[h % 2], yastb[h % 2]
                    self.cp("act", ya[:], ps[:, 0:512], r=[pb], w=[yab])
                    self.dma("sp", self.yaT_d.ap()[h * 128:(h + 1) * 128, tg * 512:(tg + 1) * 512], ya[:], r=[yab],
                             w=[Buf()])
            self.P.barrier()

    def phase_merge_ffn(self):
        NT, T, NTG = self.NT, self.T, self.NTG
        sb = self.sb
        mv = self.modv.ap()
        rot = self.psr([0, 1, 2, 3, 4, 5, 6, 7])
        dsrc = self.w_d.rearrange("(kc p) n -> p kc n", p=128)
        with self.scope() as st:
            x1 = sb(st, "x1", [128, 4, 2048], F32)
            x1b = [Buf() for _ in range(4)]
            h2T = sb(st, "h2T", [128, 16, 512], BF16)
            h2Tb = [[Buf() for _ in range(16)] for _ in range(4)]
            xn2 = sb(st, "xn2", [128, 2048], BF16)
            xn2b = Buf()
            mtmp = sb(st, "mtmp", [128, 512], F32)
            mtb = Buf()
            fsm = sb(st, "fsm", [128, 16], F32)
            fsmb = Buf()
            wA = [sb(st, "wA%d" % i, [128, 16, 512], BF16) for i in range(2)]
            wAb = [Buf(), Buf()]
            for tg in range(NTG):
                tsl = slice(tg * 512, (tg + 1) * 512)
                for q in range(4):
                    self.dma("sp", x1[:, q, :], self.xo[tg * 512 + q * 128:tg * 512 + (q + 1) * 128, :], w=[x1b[q]])
                with self.scope() as sm_:
                    g1bc = sb(sm_, "g1bc", [128, 2048], F32)
                    gbb = Buf()
                    self.dma("sp", g1bc[:], mv[32:48, :].rearrange("c p -> (c p)").partition_broadcast(128),
                             r=[self.scr["modv"]], w=[gbb])
                    yaT = sb(sm_, "yaT", [128, 16, 512], BF16)
                    ybT = sb(sm_, "ybT", [128, 32, 512], BF16)
                    yb_ = Buf()
                    mT = sb(sm_, "mT", [128, 16, 512], BF16)
                    mTb = [Buf() for _ in range(16)]
                    wB = [sb(sm_, "wB%d" % i, [128, 32, 128], BF16) for i in range(2)]
                    wBb = [Buf(), Buf()]
                    gt = [sb(sm_, "gt%d" % i, [128, 2, 512], F32) for i in range(2)]
                    gtb = [Buf(), Buf()]
                    self.dma("sp", yaT[:], self.yaT_d.ap()[:, tsl].rearrange("(kc p) t -> p kc t", p=128),
                             r=[self.scr["yaT"]], w=[yb_])
                    self.dma("sp", ybT[:], self.ybT_d.ap()[:, tsl].rearrange("(kc p) t -> p kc t", p=128),
                             r=[self.scr["ybT"]], w=[yb_])

                    def issue_m(i):
                        self.load_w(wA[i % 2][:, :, 0:128], self.w_pa, i * 128, 128, wAb[i % 2])
                        self.load_w(wB[i % 2][:], self.w_pb, i * 128, 128, wBb[i % 2])
                    issue_m(0)
                    for mc in range(16):
                        i = mc
                        if i + 1 < 16:
                            issue_m(i + 1)
                        g_t, g_b = gt[mc % 2], gtb[mc % 2]
                        self.dma("sp", g_t[:, 0, :], self.gT_d.ap()[mc * 128:(mc + 1) * 128, tsl], r=[self.scr["gT"]],
                                 w=[g_b])
                        self.dma("sp", g_t[:, 1, :], self.gT_d.ap()[2048 + mc * 128:2048 + (mc + 1) * 128, tsl],
                                 r=[self.scr["gT"]], w=[g_b])
                        psa, psab = rot.next()
                        for kc in range(16):
                            self.mm(psa[:, 0:512], wA[i % 2][:, kc, 0:128], yaT[:, kc, :],
                                    start=(kc == 0), stop=(kc == 15), r=[wAb[i % 2], yb_], w=[psab])
                        psb_, psbb = rot.next()
                        for kc in range(32):
                            self.mm(psb_[:, 0:512], wB[i % 2][:, kc, :], ybT[:, kc, :],
                                    start=(kc == 0), stop=(kc == 31), r=[wBb[i % 2], yb_], w=[psbb])
                        self.tt("dve", mtmp[:], psa[:, 0:512], g_t[:, 0, :], ALU.mult, r=[psab, g_b], w=[mtb])
                        self.tt("dve", g_t[:, 1, :], psb_[:, 0:512], g_t[:, 1, :], ALU.mult, r=[psbb, g_b], w=[g_b])
                        self.tt("pool", mT[:, mc, :], mtmp[:], g_t[:, 1, :], ALU.add, r=[mtb, g_b], w=[mTb[mc]])

                    def issue_o(i):
                        self.load_w(wA[i % 2][:], self.w_o, i * 512, 512, wAb[i % 2])
                    issue_o(0)
                    for i in range(4):
                        if i + 1 < 4:
                            issue_o(i + 1)
                        for q in range(4):
                            ps, pb = rot.next()
                            for kc in range(16):
                                self.mm(ps[:, 0:512], mT[:, kc, q * 128:(q + 1) * 128], wA[i % 2][:, kc, :],
                                        start=(kc == 0), stop=(kc == 15), r=[mTb[kc], wAb[i % 2]], w=[pb])
                            self.tt("dve", mtmp[:], ps[:, 0:512], g1bc[:, i * 512:(i + 1) * 512], ALU.mult, r=[pb, gbb],
                                    w=[mtb])
                            self.tt("pool", x1[:, q, i * 512:(i + 1) * 512], x1[:, q, i * 512:(i + 1) * 512], mtmp[:],
                                    ALU.add, r=[x1b[q], mtb], w=[x1b[q]])
                    self.P.barrier()
                for q in range(4):
                    self.act(xn2[:], x1[:, q, :], AF.Square, accum=fsm[:, 0:1], r=[x1b[q]], w=[xn2b, fsmb])
                    self.rstd(fsm[:, 1:2], fsm[:, 0:1], 2048.0, fsm[:, 2:3], [fsmb], [fsmb])
                    self.act(xn2[:], x1[:, q, :], AF.Copy, scale=fsm[:, 1:2], r=[x1b[q], fsmb], w=[xn2b])
                    for half in range(2):
                        ps, pb = rot.next()
                        pv = ps[:].bitcast(BF16)
                        for j in range(8):
                            fc = half * 8 + j
                            self.tr(pv[:, j * 128:(j + 1) * 128], xn2[:, fc * 128:(fc + 1) * 128], self.IDB,
                                    r=[xn2b, self.cstb], w=[pb])
                        for j in range(8):
                            fc = half * 8 + j
                            dst = h2T[:, fc, q * 128:(q + 1) * 128]
                            if half == 0:
                                self.act(dst, pv[:, j * 128:(j + 1) * 128], AF.Identity, bias=self.sh2[:, fc:fc + 1],
                                         scale=self.s2[:, fc:fc + 1], r=[pb, self.vecb], w=[h2Tb[q][fc]])
                            else:
                                self.ts("dve", dst, pv[:, j * 128:(j + 1) * 128], self.s2[:, fc:fc + 1],
                                        self.sh2[:, fc:fc + 1], ALU.mult, ALU.add, r=[pb, self.vecb], w=[h2Tb[q][fc]])
                with self.scope() as sf_:
                    g2bc = sb(sf_, "g2bc", [128, 2048], F32)
                    gbb = Buf()
                    self.dma("sp", g2bc[:], mv[80:96, :].rearrange("c p -> (c p)").partition_broadcast(128),
                             r=[self.scr["modv"]], w=[gbb])
                    aT = sb(sf_, "aT", [128, 44, 512], BF16)
                    aTb = [Buf() for _ in range(44)]
                    sg = sb(sf_, "sg", [128, 512], F32)
                    sgb = Buf()
                    wD = [sb(sf_, "wD%d" % i, [128, 44, 256], BF16) for i in range(2)]
                    wDb = [Buf(), Buf()]

                    def issue_f(i):
                        self.load_w(wA[i % 2][:, :, 0:256], self.w_g, i * 256, 256, wAb[i % 2])
                        self.load_w(wA[i % 2][:, :, 256:512], self.w_u, i * 256, 256, wAb[i % 2])

                    def issue_d(i):
                        self.dma("pool", wD[i % 2][:], dsrc[:, :, i * 256:(i + 1) * 256], w=[wDb[i % 2]])
                    issue_f(0)
                    for i in range(22):
                        if i + 1 < 22:
                            issue_f(i + 1)
                        elif True:
                            issue_d(0)
                        for sub in range(2):
                            fcx = i * 2 + sub
                            hr = [h2Tb[q][kc] for q in range(4) for kc in range(16)]
                            psg, psgb = rot.next()
                            for kc in range(16):
                                self.mm(psg[:, 0:512], wA[i % 2][:, kc, sub * 128:(sub + 1) * 128], h2T[:, kc, :],
                                        start=(kc == 0), stop=(kc == 15), r=[wAb[i % 2]] + (hr if kc == 0 else []),
                                        w=[psgb])
                            psu, psub = rot.next()
                            for kc in range(16):
                                self.mm(psu[:, 0:512], wA[i % 2][:, kc, 256 + sub * 128:256 + (sub + 1) * 128],
                                        h2T[:, kc, :], start=(kc == 0), stop=(kc == 15), r=[wAb[i % 2]], w=[psub])
                            self.act(sg[:], psg[:, 0:512], AF.Silu, r=[psgb], w=[sgb])
                            self.tt("dve", aT[:, fcx, :], psu[:, 0:512], sg[:], ALU.mult, r=[psub, sgb], w=[aTb[fcx]])
                    for i in range(8):
                        if i + 1 < 8:
                            issue_d(i + 1)
                        for q in range(4):
                            ps, pb = rot.next()
                            for kc in range(44):
                                self.mm(ps[:, 0:256], aT[:, kc, q * 128:(q + 1) * 128], wD[i % 2][:, kc, :],
                                        start=(kc == 0), stop=(kc == 43), r=[aTb[kc], wDb[i % 2]], w=[pb])
                            self.tt("dve", mtmp[:, 0:256], ps[:, 0:256], g2bc[:, i * 256:(i + 1) * 256], ALU.mult,
                                    r=[pb, gbb], w=[mtb])
                            self.tt("pool", x1[:, q, i * 256:(i + 1) * 256], x1[:, q, i * 256:(i + 1) * 256],
                                    mtmp[:, 0:256], ALU.add, r=[x1b[q], mtb], w=[x1b[q]])
                    self.P.barrier()
                with self.scope() as so_:
                    fgbc = sb(so_, "fgbc", [128, 2048], F32)
                    gbb = Buf()
                    self.dma("sp", fgbc[:], self.fg_d.partition_broadcast(128), w=[gbb])
                    osb = [sb(so_, "osb%d" % i, [128, 2048], F32) for i in range(2)]
                    osbb = [Buf(), Buf()]
                    for q in range(4):
                        ob, obb = osb[q % 2], osbb[q % 2]
                        self.act(xn2[:], x1[:, q, :], AF.Square, accum=fsm[:, 4:5], r=[x1b[q]], w=[xn2b, fsmb])
                        self.rstd(fsm[:, 5:6], fsm[:, 4:5], 2048.0, fsm[:, 6:7], [fsmb], [fsmb])
                        self.stt(ob[:], x1[:, q, :], fsm[:, 5:6], fgbc[:], ALU.mult, ALU.mult, r=[x1b[q], fsmb, gbb],
                                 w=[obb])
                        self.dma("sp", self.y[tg * 512 + q * 128:tg * 512 + (q + 1) * 128, :], ob[:], r=[obb],
                                 w=[Buf()])
                    self.P.barrier()


def _t5_bucket(d):
    n = np.maximum(d, 0)
    nf = np.maximum(n, 1).astype(np.float32)
    large = 16 + (np.log(nf / np.float32(16)) / np.float32(np.log(128 / 16)) * np.float32(16)).astype(np.int32)
    large = np.minimum(large, 31)
    return np.where(n < 16, n, large)


def _consts():
    cst = np.zeros((128, 768), np.float32)
    i = np.arange(128)
    cst[:, 0:128] = np.eye(128)
    cst[:, 128:256] = (i[:, None] <= i[None, :])
    cst[:, 256:384] = (i[:, None] > i[None, :])
    cst[:, 384:512] = np.eye(128)[::-1]
    cst[:, 512:640] = np.where(i[None, :] <= i[:, None], 0.0, -1e30)
    cst[:, 640:768] = 1.0
    ohd = np.zeros((32, 384), np.float32)
    for j in range(384):
        d = j - 127
        if 0 <= d <= 255:
            ohd[int(_t5_bucket(np.array(d))), j] += 1.0
            ohd[31, j] -= 1.0
    return cst, ohd


_CACHE = {}


def make_in_maps(inputs, NT, n_seq):
    T = NT * 128
    f = lambda a: np.ascontiguousarray(np.asarray(a, dtype=np.float32))
    x = f(inputs["x"])
    c = f(inputs["c"])
    cst, ohd = _consts()
    shared = {
        "w_ada": f(inputs["w_ada"][0]), "b_ada": f(inputs["b_ada"][0]).reshape(96, 128),
        "norm1_g": f(inputs["norm1_g"][0]).reshape(16, 128), "w_in": f(inputs["w_in"][0]),
        "cq_g": f(inputs["cq_norm_g"][0]).reshape(4, 128), "ckv_g": f(inputs["ckv_norm_g"][0]).reshape(1, 256),
        "kidx_g": f(inputs["kidx_norm_g"][0]).reshape(1, 64), "kidx_b": f(inputs["kidx_norm_b"][0]).reshape(1, 64),
        "w_uq": f(inputs["w_uq"][0]), "w_iq": f(inputs["w_iq"][0]), "w_uk": f(inputs["w_uk"][0]),
        "w_uv": f(inputs["w_uv"][0]), "rel_bias": f(inputs["rel_bias"]),
        "conv_w": f(inputs["conv_w"][0]).reshape(192, 128), "conv_b": f(inputs["conv_b"][0]).reshape(48, 128),
        "dt_bias": f(inputs["dt_bias"][0]).reshape(1, 64), "a_log": f(inputs["a_log"][0]).reshape(1, 64),
        "d_skip": f(inputs["d_skip"][0]).reshape(1, 64), "ssm_g": f(inputs["ssm_norm_g"][0]).reshape(1, 4096),
        "w_proj_a": f(inputs["w_proj_a"][0]), "w_proj_b": f(inputs["w_proj_b"][0]), "w_out": f(inputs["w_out"][0]),
        "norm2_g": f(inputs["norm2_g"][0]).reshape(16, 128), "w_gate": f(inputs["w_gate"][0]),
        "w_up": f(inputs["w_up"][0]), "w_down": f(inputs["w_down"][0]), "final_g": f(inputs["final_g"]).reshape(1, 2048),
        "cst": cst, "ohd": ohd,
    }
    maps = []
    for core in range(2 * n_seq):
        b, j = core // 2, core % 2
        flg = np.zeros((128, 2), np.float32)
        flg[:, 0] = float(j)
        flg[:, 1] = 0.0 if j == 1 else -1e30
        m = dict(shared)
        m["xo"] = np.ascontiguousarray(x[b, j * T:(j + 1) * T])
        m["xp"] = np.ascontiguousarray(x[b, 0:T])
        m["cb"] = np.ascontiguousarray(c[b].reshape(16, 128))
        m["flg"] = flg
        maps.append(m)
    return maps


def run(inputs, NT, debug=(), stop=99, n_seq=4):
    key = (NT, tuple(debug), stop)
    if key not in _CACHE:
        bld = Builder(NT, debug, stop)
        nc = bld.build()
        _CACHE[key] = (bld, nc)
    bld, nc = _CACHE[key]
    maps = make_in_maps(inputs, NT, n_seq)
    res = run_bass_kernel_spmd(nc, maps, core_ids=list(range(2 * n_seq)))
    T = NT * 128
    out = np.zeros((4, 2 * T, 2048), np.float32)
    for core in range(2 * n_seq):
        b, j = core // 2, core % 2
        out[b, j * T:(j + 1) * T] = res.results[core]["y"]
    return out, res


def kernel(**inputs):
    out, _ = run(inputs, 16)
    return out
```
